# Optimizing a Trainium2 kernel written in Bass

```python
import math
import jax, jax.numpy as jnp
from jax import lax
import numpy as np

D_MODEL = 1024
BATCH = 16
SEQ = 4096
DEPTH = 2

HEAD_DIM = 64
SCALE = HEAD_DIM ** -0.5
ROPE_THETA = 10000.0
NORM_EPS = 1e-5
BAND = 128

A_HEADS = 8
A_KV_HEADS = 2
A_GROUP = A_HEADS // A_KV_HEADS
CMP_BLOCK = 32
CMP_STRIDE = 16
CMP_HIDDEN = 256
SEL_BLOCK = 64
SEL_TOPK = 16
NSA_WINDOW = 512
NSA_QBLOCK = 64
N_BRANCH = 3

B_HEADS = 8
B_KV_HEADS = 2
B_GROUP = B_HEADS // B_KV_HEADS
B_WINDOW = 128

C_HEADS = 16
C_PATTERNS = ((128, 1), (512, 4), (2048, 16))

A_Q = A_HEADS * HEAD_DIM
A_KV = A_KV_HEADS * HEAD_DIM
A_GATES = A_HEADS * N_BRANCH
B_Q = B_HEADS * HEAD_DIM
B_KV = B_KV_HEADS * HEAD_DIM
EVEN_IN = A_Q + 6 * A_KV + A_GATES + B_Q + 2 * B_KV
EVEN_MIX = A_Q + B_Q
ODD_IN = 3 * C_HEADS * HEAD_DIM
ODD_MIX = C_HEADS * HEAD_DIM
FFN_HIDDEN = -(-8 * D_MODEL // (3 * 256)) * 256
N_EVEN = (DEPTH + 1) // 2
N_ODD = DEPTH // 2

kernel_name = "hybrid_nsa_swasink_dilated_swiglu"


def rms_norm(x, g):
    xf = x.astype(jnp.float32)
    y = xf * lax.rsqrt(jnp.mean(xf * xf, axis=-1, keepdims=True) + NORM_EPS)
    return (y * g.astype(jnp.float32)).astype(x.dtype)


def rope_tables(seq, dtype):
    inv = 1.0 / (ROPE_THETA ** (jnp.arange(0, HEAD_DIM, 2, dtype=jnp.float32) / HEAD_DIM))
    ang = jnp.arange(seq, dtype=jnp.float32)[:, None] * inv[None, :]
    ang = jnp.concatenate([ang, ang], axis=-1)
    return jnp.cos(ang).astype(dtype), jnp.sin(ang).astype(dtype)


def apply_rope(x, cos, sin):
    x1, x2 = jnp.split(x, 2, axis=-1)
    rot = jnp.concatenate([-x2, x1], axis=-1)
    return x * cos[:, None, :] + rot * sin[:, None, :]


def banded_attention(q, k, v, reach, sink=None):
    L, dh = q.shape[-2], q.shape[-1]
    nb = -(-L // BAND)
    pad = nb * BAND - L
    if pad:
        q = jnp.pad(q, [(0, 0)] * (q.ndim - 2) + [(0, pad), (0, 0)])
        k = jnp.pad(k, [(0, 0)] * (k.ndim - 2) + [(0, pad), (0, 0)])
        v = jnp.pad(v, [(0, 0)] * (v.ndim - 2) + [(0, pad), (0, 0)])
    qb = q.reshape(q.shape[:-2] + (nb, BAND, dh))
    kb = k.reshape(k.shape[:-2] + (nb, BAND, dh))
    vb = v.reshape(v.shape[:-2] + (nb, BAND, dh))

    def with_prev(t):
        prev = jnp.pad(t, [(0, 0)] * (t.ndim - 3) + [(1, 0), (0, 0), (0, 0)])[..., :-1, :, :]
        return jnp.concatenate([prev, t], axis=-2)

    kc, vc = with_prev(kb), with_prev(vb)
    s = jnp.einsum('...gnqd,...nkd->...gnqk', qb, kc, preferred_element_type=jnp.float32) * SCALE
    qi = jnp.arange(BAND)[:, None] + BAND
    kj = jnp.arange(2 * BAND)[None, :]
    diff = qi - kj
    mask = (diff >= 0) & (diff <= reach)
    not_before_start = (jnp.arange(nb)[:, None, None] > 0) | (kj[None] >= BAND)
    mask = mask[None] & not_before_start
    s = jnp.where(mask, s, -jnp.inf)
    m = jnp.max(s, axis=-1)
    if sink is not None:
        m = jnp.maximum(m, sink)
    p = jnp.exp(s - m[..., None])
    den = jnp.sum(p, axis=-1)
    if sink is not None:
        den = den + jnp.exp(sink - m)
    o = jnp.einsum('...gnqk,...nkd->...gnqd', p, vc.astype(jnp.float32)) / den[..., None]
    o = o.reshape(q.shape[:-2] + (nb * BAND, dh))[..., :L, :]
    lse = (m + jnp.log(den)).reshape(q.shape[:-2] + (nb * BAND,))[..., :L]
    return o, lse


def compress_blocks(t, pe, w1, w2):
    bsz, hkv, seq, dh = t.shape
    chunks = t.reshape(bsz, hkv, seq // CMP_STRIDE, CMP_STRIDE, dh)
    blocks = jnp.concatenate([chunks[:, :, :-1], chunks[:, :, 1:]], axis=-2)
    nc = blocks.shape[2]
    blocks = (blocks + pe).reshape(bsz, hkv, nc, CMP_BLOCK * dh)
    return jax.nn.silu(blocks @ w1) @ w2


def nsa_attention(q_plain, q_rot, k_cmp, v_cmp, k_sel, v_sel, k_win, v_win, gates,
                  pe_k, pe_v, wk1, wk2, wv1, wv2):
    bsz, seq = q_plain.shape[:2]
    dh = HEAD_DIM
    nc = seq // CMP_STRIDE - 1
    nb = seq // SEL_BLOCK
    n_sel = min(SEL_TOPK, nb)

    def group_q(t):
        return t.reshape(bsz, seq, A_KV_HEADS, A_GROUP, dh).transpose(0, 2, 3, 1, 4)

    def heads_first(t):
        return t.transpose(0, 2, 1, 3)

    qp, qr = group_q(q_plain), group_q(q_rot)
    g = gates.reshape(bsz, seq, A_KV_HEADS, A_GROUP, N_BRANCH).transpose(0, 2, 3, 1, 4)
    kc = compress_blocks(heads_first(k_cmp), pe_k, wk1, wk2)
    vc = compress_blocks(heads_first(v_cmp), pe_v, wv1, wv2).astype(jnp.float32)
    ks_b = heads_first(k_sel).reshape(bsz, A_KV_HEADS, nb, SEL_BLOCK, dh)
    vs_b = heads_first(v_sel).reshape(bsz, A_KV_HEADS, nb, SEL_BLOCK, dh)
    pad_w = ((0, 0), (0, 0), (NSA_WINDOW, 0), (0, 0))
    kw_p = jnp.pad(heads_first(k_win), pad_w)
    vw_p = jnp.pad(heads_first(v_win), pad_w)

    cmp_end = jnp.arange(nc) * CMP_STRIDE + CMP_BLOCK - 1
    c_start = np.arange(nc) * CMP_STRIDE
    s_start = np.arange(nb) * SEL_BLOCK
    overlap = ((c_start[:, None] <= s_start[None, :] + SEL_BLOCK - 1) &
               (c_start[:, None] + CMP_BLOCK - 1 >= s_start[None, :]))
    cmp_to_sel = jnp.asarray(overlap.astype(np.float32))
    bi = jnp.arange(bsz)[:, None, None, None]
    hi = jnp.arange(A_KV_HEADS)[None, :, None, None]
    blk = jnp.arange(nb)[None, :]

    def query_block(i):
        q0 = i * NSA_QBLOCK
        tq = q0 + jnp.arange(NSA_QBLOCK)
        qp_b = lax.dynamic_slice_in_dim(qp, q0, NSA_QBLOCK, axis=3)
        qr_b = lax.dynamic_slice_in_dim(qr, q0, NSA_QBLOCK, axis=3)
        g_b = lax.dynamic_slice_in_dim(g, q0, NSA_QBLOCK, axis=3).astype(jnp.float32)

        s = jnp.einsum('bhgqd,bhcd->bhgqc', qp_b, kc, preferred_element_type=jnp.float32) * SCALE
        s = jnp.where(cmp_end[None, :] <= tq[:, None], s, -jnp.inf)
        m = jnp.max(s, axis=-1, keepdims=True)
        m = jnp.where(jnp.isfinite(m), m, 0.0)
        p = jnp.exp(s - m)
        den = jnp.sum(p, axis=-1, keepdims=True)
        p = p / jnp.where(den > 0, den, 1.0)
        o_cmp = jnp.einsum('bhgqc,bhcd->bhgqd', p, vc)

        imp = jnp.einsum('bhgqc,cn->bhqn', p, cmp_to_sel)
        cur = (tq // SEL_BLOCK)[:, None]
        forced = (blk == 0) | (blk == cur) | (blk == cur - 1)
        imp = jnp.where(forced, jnp.inf, jnp.where(blk > cur, -jnp.inf, imp))
        _, idx = lax.top_k(imp, n_sel)
        kg = ks_b[bi, hi, idx].reshape(bsz, A_KV_HEADS, NSA_QBLOCK, n_sel * SEL_BLOCK, dh)
        vg = vs_b[bi, hi, idx].reshape(bsz, A_KV_HEADS, NSA_QBLOCK, n_sel * SEL_BLOCK, dh)
        pos = (idx[..., None] * SEL_BLOCK + jnp.arange(SEL_BLOCK)).reshape(
            bsz, A_KV_HEADS, NSA_QBLOCK, n_sel * SEL_BLOCK)
        s = jnp.einsum('bhgqd,bhqkd->bhgqk', qr_b, kg, preferred_element_type=jnp.float32) * SCALE
        s = jnp.where((pos <= tq[:, None])[:, :, None], s, -jnp.inf)
        o_sel = jnp.einsum('bhgqk,bhqkd->bhgqd', jax.nn.softmax(s, axis=-1), vg.astype(jnp.float32))

        kw = lax.dynamic_slice_in_dim(kw_p, q0, NSA_QBLOCK + NSA_WINDOW, axis=2)
        vw = lax.dynamic_slice_in_dim(vw_p, q0, NSA_QBLOCK + NSA_WINDOW, axis=2)
        kpos = q0 - NSA_WINDOW + jnp.arange(NSA_QBLOCK + NSA_WINDOW)
        diff = tq[:, None] - kpos[None, :]
        wmask = (diff >= 0) & (diff < NSA_WINDOW) & (kpos[None, :] >= 0)
        s = jnp.einsum('bhgqd,bhkd->bhgqk', qr_b, kw, preferred_element_type=jnp.float32) * SCALE
        s = jnp.where(wmask, s, -jnp.inf)
        o_win = jnp.einsum('bhgqk,bhkd->bhgqd', jax.nn.softmax(s, axis=-1), vw.astype(jnp.float32))

        return g_b[..., 0:1] * o_cmp + g_b[..., 1:2] * o_sel + g_b[..., 2:3] * o_win

    out = lax.map(query_block, jnp.arange(seq // NSA_QBLOCK))
    out = out.transpose(1, 0, 4, 2, 3, 5)
    return out.reshape(bsz, seq, A_HEADS * dh)


def swa_sink_attention(q, k, v, sinks):
    bsz, seq = q.shape[:2]
    qg = q.reshape(bsz, seq, B_KV_HEADS, B_GROUP, HEAD_DIM).transpose(0, 2, 3, 1, 4)
    sink = sinks.astype(jnp.float32).reshape(B_KV_HEADS, B_GROUP, 1, 1)
    o, _ = banded_attention(qg, k.transpose(0, 2, 1, 3), v.transpose(0, 2, 1, 3), B_WINDOW - 1, sink)
    return o.transpose(0, 3, 1, 2, 4).reshape(bsz, seq, B_HEADS * HEAD_DIM)


def dilated_attention(q, k, v):
    bsz, seq, nh, dh = q.shape
    qt, kt, vt = (t.transpose(0, 2, 1, 3) for t in (q, k, v))
    outs, lses = [], []
    for window, dil in C_PATTERNS:
        sub = seq // dil

        def strided(t):
            return t.reshape(bsz, nh, sub, dil, dh).transpose(0, 1, 3, 2, 4)

        o, lse = banded_attention(strided(qt)[..., None, :, :], strided(kt), strided(vt), window // dil)
        outs.append(o[..., 0, :, :].transpose(0, 1, 3, 2, 4).reshape(bsz, nh, seq, dh))
        lses.append(lse[..., 0, :].transpose(0, 1, 3, 2).reshape(bsz, nh, seq))
    w = jax.nn.softmax(jnp.stack(lses, axis=0), axis=0)
    o = jnp.einsum('pbht,pbhtd->bhtd', w, jnp.stack(outs, axis=0))
    return o.transpose(0, 2, 1, 3).reshape(bsz, seq, nh * dh)


def setup_inputs(seed: int = 0) -> dict:
    key = jax.random.key(seed)
    ks = jax.random.split(key, 17)
    nrm = jax.random.normal
    f32 = jnp.float32
    return {
        "x": nrm(ks[0], (BATCH, SEQ, D_MODEL), f32),
        "attn_norm": 1.0 + 0.02 * nrm(ks[1], (DEPTH, D_MODEL), f32),
        "ffn_norm": 1.0 + 0.02 * nrm(ks[2], (DEPTH, D_MODEL), f32),
        "final_norm": 1.0 + 0.02 * nrm(ks[3], (D_MODEL,), f32),
        "w_in_e": nrm(ks[4], (N_EVEN, D_MODEL, EVEN_IN), f32) * D_MODEL ** -0.5,
        "w_out_e": nrm(ks[5], (N_EVEN, EVEN_MIX, D_MODEL), f32) * EVEN_MIX ** -0.5,
        "cmp_pe_k": 0.02 * nrm(ks[6], (N_EVEN, CMP_BLOCK, HEAD_DIM), f32),
        "cmp_pe_v": 0.02 * nrm(ks[7], (N_EVEN, CMP_BLOCK, HEAD_DIM), f32),
        "cmp_k_w1": nrm(ks[8], (N_EVEN, CMP_BLOCK * HEAD_DIM, CMP_HIDDEN), f32) * (CMP_BLOCK * HEAD_DIM) ** -0.5,
        "cmp_k_w2": nrm(ks[9], (N_EVEN, CMP_HIDDEN, HEAD_DIM), f32) * CMP_HIDDEN ** -0.5,
        "cmp_v_w1": nrm(ks[10], (N_EVEN, CMP_BLOCK * HEAD_DIM, CMP_HIDDEN), f32) * (CMP_BLOCK * HEAD_DIM) ** -0.5,
        "cmp_v_w2": nrm(ks[11], (N_EVEN, CMP_HIDDEN, HEAD_DIM), f32) * CMP_HIDDEN ** -0.5,
        "sinks": 0.5 * nrm(ks[12], (N_EVEN, B_HEADS), f32),
        "w_qkv_o": nrm(ks[13], (N_ODD, D_MODEL, ODD_IN), f32) * D_MODEL ** -0.5,
        "w_out_o": nrm(ks[14], (N_ODD, ODD_MIX, D_MODEL), f32) * ODD_MIX ** -0.5,
        "w_gate_up": nrm(ks[15], (DEPTH, D_MODEL, 2 * FFN_HIDDEN), f32) * D_MODEL ** -0.5,
        "w_down": nrm(ks[16], (DEPTH, FFN_HIDDEN, D_MODEL), f32) * FFN_HIDDEN ** -0.5,
    }


def reference(x, attn_norm, ffn_norm, final_norm, w_in_e, w_out_e, cmp_pe_k, cmp_pe_v,
              cmp_k_w1, cmp_k_w2, cmp_v_w1, cmp_v_w2, sinks, w_qkv_o, w_out_o,
              w_gate_up, w_down):
    bsz, seq, _ = x.shape
    cos, sin = rope_tables(seq, x.dtype)
    for layer in range(DEPTH):
        h = rms_norm(x, attn_norm[layer])
        if layer % 2 == 0:
            i = layer // 2
            z = h @ w_in_e[i]
            o1 = A_Q
            o2 = o1 + 6 * A_KV
            o3 = o2 + A_GATES
            o4 = o3 + B_Q
            aq = z[..., :o1].reshape(bsz, seq, A_HEADS, HEAD_DIM)
            akc, avc, aks, avs, akw, avw = [t.reshape(bsz, seq, A_KV_HEADS, HEAD_DIM)
                                            for t in jnp.split(z[..., o1:o2], 6, axis=-1)]
            gates = jax.nn.sigmoid(z[..., o2:o3].astype(jnp.float32)).reshape(bsz, seq, A_HEADS, N_BRANCH)
            bq = apply_rope(z[..., o3:o4].reshape(bsz, seq, B_HEADS, HEAD_DIM), cos, sin)
            bk, bv = [t.reshape(bsz, seq, B_KV_HEADS, HEAD_DIM) for t in jnp.split(z[..., o4:], 2, axis=-1)]
            bk = apply_rope(bk, cos, sin)
            a_out = nsa_attention(aq, apply_rope(aq, cos, sin), akc, avc,
                                  apply_rope(aks, cos, sin), avs, apply_rope(akw, cos, sin), avw, gates,
                                  cmp_pe_k[i], cmp_pe_v[i], cmp_k_w1[i], cmp_k_w2[i], cmp_v_w1[i], cmp_v_w2[i])
            b_out = swa_sink_attention(bq, bk, bv, sinks[i])
            mix = jnp.concatenate([a_out, b_out], axis=-1).astype(x.dtype)
            x = x + mix @ w_out_e[i]
        else:
            j = layer // 2
            z = h @ w_qkv_o[j]
            q, k, v = [t.reshape(bsz, seq, C_HEADS, HEAD_DIM) for t in jnp.split(z, 3, axis=-1)]
            c_out = dilated_attention(apply_rope(q, cos, sin), apply_rope(k, cos, sin), v)
            x = x + c_out.astype(x.dtype) @ w_out_o[j]
        h = rms_norm(x, ffn_norm[layer])
        gate, up = jnp.split(h @ w_gate_up[layer], 2, axis=-1)
        x = x + (jax.nn.silu(gate) * up) @ w_down[layer]
    return rms_norm(x, final_norm)
```

```python
import numpy as np
import ml_dtypes
import concourse.bass as bass
import concourse.mybir as mybir
from concourse.bass_utils import run_bass_kernel_spmd
from contextlib import ExitStack

F32 = mybir.dt.float32
BF16 = mybir.dt.bfloat16
ALU = mybir.AluOpType
AF = mybir.ActivationFunctionType

T = 4096
D = 1024
NSEQ = 2
TC = 512
NCH = T // TC
FH = 2816
BIG = 30000.0
SCALE = 0.125
EPS = 1e-5

QP, QR, KC, VC, KS, KW, BQ, BK = 0, 512, 1024, 1152, 1280, 1408, 1536, 2048
ZT0_ROWS = 2176


class Buf:
    __slots__ = ("name", "w", "r", "box", "psum")

    def __init__(self, name="", psum=False):
        self.name = name
        self.w = None
        self.r = {}
        self.box = None
        self.psum = psum


class _Rec:
    def __init__(self):
        self.call = None

    def __getattr__(self, name):
        def f(*a, **kw):
            self.call = (name, a, kw)
            return None
        return f


class Sched:
    ENG = ("pe", "act", "dve", "pool", "sp")

    def __init__(self, nc, es, nbox=44):
        self.nc = nc
        self.es = es
        self.q = {e: [] for e in self.ENG}
        self.esem = {e: es.enter_context(nc.semaphore("es_" + e)) for e in self.ENG}
        self.ecount = {e: 0 for e in self.ENG}
        self.waited = {e: {} for e in self.ENG}
        self.boxes = [[es.enter_context(nc.semaphore("dq%d" % i)), 0] for i in range(2 * nbox)]
        self.boxpool = {"sp": self.boxes[:nbox], "pool": self.boxes[nbox:]}
        self.nextbox = {"sp": 0, "pool": 0}
        self.nops = 0

    def box(self, buf, eng):
        if buf.box is None:
            buf.box = {}
        if eng not in buf.box:
            pool = self.boxpool[eng]
            buf.box[eng] = pool[self.nextbox[eng] % len(pool)]
            self.nextbox[eng] += 1
        return buf.box[eng]

    def _add(self, deps, tok, eng, is_dma, raw):
        ek, sem, val, bx = tok
        if bx is not None:
            val = bx[1]
        elif ek == eng and not is_dma:
            if eng == "pe":
                return
        k = id(sem)
        if k not in deps or deps[k][1] < val:
            deps[k] = (sem, val)

    maxops = None

    def op(self, eng, fn, reads=(), writes=(), inc=True, dma=None, force=False):
        if self.maxops is not None and self.nops >= self.maxops and not force:
            return None
        rec = _Rec()
        fn(rec)
        call = rec.call
        is_dma = dma is not None
        deps = {}
        for b in reads:
            if b.w is not None:
                self._add(deps, b.w, eng, is_dma, True)
            if b.psum:
                for t in b.r.values():
                    if t[0] != eng:
                        self._add(deps, t, eng, is_dma, False)
        for b in writes:
            if b.w is not None:
                self._add(deps, b.w, eng, is_dma, False)
            for t in b.r.values():
                self._add(deps, t, eng, is_dma, False)
        waits = []
        wd = self.waited[eng]
        for k, (sem, val) in deps.items():
            if wd.get(k, 0) >= val:
                continue
            wd[k] = val
            waits.append((sem, val))
        if is_dma:
            dma[1] += 16
            tok = ("dma", dma[0], dma[1], dma)
            incspec = (dma[0], 16)
        elif inc:
            self.ecount[eng] += 1
            tok = (eng, self.esem[eng], self.ecount[eng], None)
            incspec = (self.esem[eng], 1)
        else:
            tok = (eng, self.esem[eng], self.ecount[eng] + 1, None)
            incspec = None
        kt = id(tok[1])
        for b in reads:
            o = b.r.get(kt)
            if o is None or o[2] < tok[2]:
                b.r[kt] = tok
        for b in writes:
            b.w = tok
            b.r = {}
        self.nops += 1
        self.q[eng].append((waits, call, incspec))
        return tok

    def barrier(self):
        targets = [(self.esem[f], self.ecount[f]) for f in self.ENG if self.ecount[f] > 0]
        targets += [(bx[0], bx[1]) for bx in self.boxes if bx[1] > 0]
        for e in self.ENG:
            waits = []
            wd = self.waited[e]
            for sem, val in targets:
                k = id(sem)
                if sem is self.esem[e] and e == "pe":
                    continue
                if wd.get(k, 0) >= val:
                    continue
                wd[k] = val
                waits.append((sem, val))
            self.q[e].append((waits, ("nop", (), {}), None))

    def emit(self, block):
        def run(e, key):
            for waits, (name, a, kw), incspec in self.q[key]:
                for sem, val in waits:
                    e.wait_ge(sem, val)
                ins = getattr(e, name)(*a, **kw)
                if incspec is not None:
                    ins.then_inc(incspec[0], incspec[1])

        @block.tensor
        def _(e):
            run(e, "pe")

        @block.scalar
        def _(e):
            run(e, "act")

        @block.vector
        def _(e):
            run(e, "dve")

        @block.gpsimd
        def _(e):
            run(e, "pool")

        @block.sync
        def _(e):
            run(e, "sp")


class Ring:
    def __init__(self, kb, name, shape, dt, n, psum=False):
        self.items = []
        for i in range(n):
            self.items.append(kb.ps(name + str(i), shape, dt) if psum else kb.sb(name + str(i), shape, dt))
        self.i = 0

    def next(self):
        it = self.items[self.i % len(self.items)]
        self.i += 1
        return it


class KB:
    def __init__(self, nc, es):
        self.nc = nc
        self.es = es
        self.S = Sched(nc, es)
        self.dbufs = {}

    _uid = 0

    def sb(self, name, shape, dt):
        KB._uid += 1
        name = "%s_%d" % (name, KB._uid)
        return self.es.enter_context(self.nc.sbuf_tensor(name, shape, dt)), Buf(name)

    def ps(self, name, shape, dt):
        KB._uid += 1
        name = "%s_%d" % (name, KB._uid)
        return self.es.enter_context(self.nc.psum_tensor(name, shape, dt)), Buf(name, psum=True)

    def db(self, *key):
        b = self.dbufs.get(key)
        if b is None:
            b = self.dbufs[key] = Buf(str(key))
        return b

    def load(self, dst, dbuf, src, sbufs=(), eng="sp", nonc=False):
        kw = dict(allow_slow_non_contiguous=True) if nonc else {}
        self.S.op(eng, lambda e: e.dma_start(out=dst, in_=src, **kw), reads=list(sbufs), writes=[dbuf],
                  dma=self.S.box(dbuf, eng))

    def store(self, dst, dbufs, src, sbuf, eng="pool"):
        self.S.op(eng, lambda e: e.dma_start(out=dst, in_=src), reads=[sbuf], writes=list(dbufs),
                  dma=self.S.box(sbuf, eng))

    def op(self, eng, fn, reads=(), writes=(), inc=True):
        return self.S.op(eng, fn, reads=reads, writes=writes, inc=inc)


def host_consts():
    bf = ml_dtypes.bfloat16
    c = {}
    inv = 1.0 / (10000.0 ** (np.arange(0, 64, 2, dtype=np.float32) / 64.0))
    ang = np.arange(T, dtype=np.float32)[:, None] * inv[None, :]
    ang = np.concatenate([ang, ang], axis=-1)
    cos = np.cos(ang).astype(np.float32).T
    sin = np.sin(ang).astype(np.float32).T
    c["c_cos"] = np.ascontiguousarray(np.concatenate([cos, cos], 0))
    c["c_sin"] = np.ascontiguousarray(np.concatenate([sin, sin], 0))
    c["c_identf"] = np.eye(128, dtype=np.float32)
    c["c_identb"] = np.eye(128, dtype=np.float32).astype(bf)
    c["c_onesb"] = np.ones((128, 128), dtype=np.float32).astype(bf)
    k = np.arange(128)[:, None]
    q = np.arange(128)[None, :]
    tri = np.where(k > q, -BIG, 0.0)
    anti_s = np.where(k <= q, -BIG, 0.0)
    anti_i = np.where(k < q, -BIG, 0.0)
    full = np.full((128, 128), -BIG)
    c["c_masks"] = np.ascontiguousarray(np.stack([tri, anti_s, anti_i, full], 1).astype(np.float32)).astype(bf)
    comps = []
    for prev in (anti_i, anti_s):
        for v in range(4):
            p0 = full if (v & 1) else prev
            p1 = full if (v & 2) else prev
            comps.append(np.concatenate([p0, tri, p1, tri], axis=1))
    c["c_cmask"] = np.ascontiguousarray(np.stack(comps, 1).astype(np.float32)).astype(bf)
    u = np.arange(T)[None, :]
    c["c_st"] = np.where(16 * k + 31 > u, -BIG, 0.0).astype(np.float32).astype(bf)
    n = np.arange(64)[:, None]
    j = np.arange(T)[None, :]
    c["c_e"] = np.where(j // 64 == n, BIG, 0.0).astype(np.float32).astype(bf)
    cs = np.arange(256) * 16
    ss = np.arange(64) * 64
    ov = ((cs[:, None] <= ss[None, :] + 63) & (cs[:, None] + 31 >= ss[None, :])).astype(np.float32)
    ov[255, :] = 0.0
    ova = np.zeros((256, 65), np.float32)
    ova[:, :64] = ov
    ova[:255, 64] = 1.0
    c["c_ova"] = np.ascontiguousarray(ova.reshape(2, 128, 65).transpose(1, 0, 2)).astype(bf)
    t = np.arange(T)
    cur = t // 64
    blk = np.arange(64)[None, :]
    fb = np.zeros((T, 64), np.float32)
    fb[blk > cur[:, None]] = -1e4
    forced = (blk == 0) | (blk == cur[:, None]) | (blk == cur[:, None] - 1)
    fb[forced] = 1e4
    c["c_fb"] = np.ascontiguousarray(fb.reshape(32, 128, 64).transpose(1, 0, 2))
    return c


CONST_SPECS = [("c_cos", [128, T], F32), ("c_sin", [128, T], F32), ("c_identf", [128, 128], F32),
               ("c_identb", [128, 128], BF16), ("c_onesb", [128, 128], BF16), ("c_masks", [128, 4, 128], BF16), ("c_cmask", [128, 8, 512], BF16),
               ("c_st", [128, T], BF16), ("c_e", [64, T], BF16), ("c_ova", [128, 2, 65], BF16),
               ("c_fb", [128, 32, 64], F32)]

IN_SPECS = [("x", [NSEQ, T, D]), ("attn_norm", [2, D]), ("ffn_norm", [2, D]), ("final_norm", [D]),
            ("w_in_e", [1, D, 2072]), ("w_out_e", [1, D, D]), ("cmp_pe_k", [1, 32, 64]), ("cmp_pe_v", [1, 32, 64]),
            ("cmp_k_w1", [1, 2048, 256]), ("cmp_k_w2", [1, 256, 64]), ("cmp_v_w1", [1, 2048, 256]),
            ("cmp_v_w2", [1, 256, 64]), ("sinks", [1, 8]), ("w_qkv_o", [1, D, 3072]), ("w_out_o", [1, D, D]),
            ("w_gate_up", [2, D, 2 * FH]), ("w_down", [2, FH, D])]


class Prog:
    def __init__(self, stop_after=None, dump=(), lim=None, maxops=None, alim=None, only=None):
        self.only = only
        Sched.maxops = maxops
        self.alim = alim
        self.lim = lim
        self.stop_after = stop_after
        self.dump = set(dump)
        nc = self.nc = bass.Bass("TRN2", target_bir_lowering=False)
        self.I = {}
        for name, shape in IN_SPECS:
            self.I[name] = nc.dram_tensor(name, shape, F32, kind="ExternalInput").ap()
        for name, shape, dt in CONST_SPECS:
            self.I[name] = nc.dram_tensor(name, shape, dt, kind="ExternalInput").ap()
        self.out = nc.dram_tensor("out", [NSEQ, T, D], F32, kind="ExternalOutput").ap()

        def scratch(name, shape, dt):
            kind = "ExternalOutput" if name in self.dump else "Internal"
            return nc.dram_tensor(name, shape, dt, kind=kind).ap()

        self.XT = scratch("XT", [NSEQ, D, T], F32)
        self.ZT0 = scratch("ZT0", [NSEQ, ZT0_ROWS, T], BF16)
        self.VT0 = scratch("VT0", [NSEQ, T, 384], BF16)
        self.GT = scratch("GT", [NSEQ, 24, T], F32)
        self.MIXT = scratch("MIXT", [NSEQ, D, T], BF16)
        self.ZT1 = scratch("ZT1", [NSEQ, 2048, T], BF16)
        self.VT1 = scratch("VT1", [NSEQ, T, D], BF16)
        self.DBG = scratch("DBG", [128, 4096], F32)

    def build(self):
        nc = self.nc
        phases = [self.ph_inproj0, self.ph_attn0, self.ph_outproj(0), self.ph_ffn(0),
                  self.ph_inproj1, self.ph_attn1, self.ph_outproj(1), self.ph_ffn(1), self.ph_final]
        names = ["inproj0", "attn0", "outproj0", "ffn0", "inproj1", "attn1", "outproj1", "ffn1", "final"]
        with ExitStack() as es0:
            kb = self.kb = KB(nc, es0)
            self.consts_setup(kb)
            self.final_bufs = []
            for ph, nm in zip(phases, names):
                if self.only is not None and nm not in self.only:
                    continue
                with ExitStack() as es:
                    kb.es = es
                    ph(kb)
                    kb.S.barrier()
                kb.es = es0
                if self.stop_after == nm:
                    break
            kb.S.op("sp", lambda e: e.nop(), reads=list(kb.dbufs.values()), force=True)
            with nc.Block() as block:
                kb.S.emit(block)
        return nc

    def consts_setup(self, kb):
        I = self.I
        self.identf, self.b_identf = kb.sb("identf", [128, 128], F32)
        self.identb, self.b_identb = kb.sb("identb", [128, 128], BF16)
        self.onesb, self.b_onesb = kb.sb("onesb", [128, 128], BF16)
        self.masks, self.b_masks = kb.sb("masks", [128, 4, 128], BF16)
        self.cmask, self.b_cmask = kb.sb("cmask", [128, 8, 512], BF16)
        kb.load(self.cmask[:], self.b_cmask, I["c_cmask"])
        self.epsb, self.b_eps = kb.sb("epsb", [128, 1], F32)
        kb.load(self.identf[:], self.b_identf, I["c_identf"])
        kb.load(self.identb[:], self.b_identb, I["c_identb"])
        kb.load(self.onesb[:], self.b_onesb, I["c_onesb"])
        kb.load(self.masks[:], self.b_masks, I["c_masks"])
        kb.op("pool", lambda e: e.memset(self.epsb[:], EPS), writes=[self.b_eps])
        self.psS = Ring(kb, "psS", [128, 512], F32, 4, psum=True)
        self.psO = Ring(kb, "psO", [128, 512], F32, 2, psum=True)
        self.psM = Ring(kb, "psM", [128, 512], F32, 2, psum=True)

        class _Cat:
            def __init__(self, rings):
                self.items = [it for r in rings for it in r.items]
                self.i = 0

            def next(self):
                it = self.items[self.i % len(self.items)]
                self.i += 1
                return it
        self.psA = _Cat([self.psM, self.psS])

    def load_weight(self, kb, wt, wbuf, src2d, col_pairs, gvec=None, stg_r=None, CH=2048):
        nk = wt.shape[1]
        if stg_r is None:
            stg_r = Ring(kb, "wstg_" + wbuf.name, [128, CH], F32, 3)
        gt = gb = None
        if gvec is not None:
            gt, gb = kb.sb("g_" + wbuf.name, [128, nk], F32)
            kb.load(gt[:], gb, gvec.rearrange("(k p) -> p k", p=128), nonc=True)
        i = 0
        for k in range(nk):
            for (dc, sc, n) in col_pairs:
                for c0 in range(0, n, CH):
                    cn = min(CH, n - c0)
                    st, bst = stg_r.next()
                    kb.load(st[:, 0:cn], bst, src2d[k * 128:(k + 1) * 128, sc + c0:sc + c0 + cn])
                    dst = wt[:, k, dc + c0:dc + c0 + cn]
                    if i % 2 == 0:
                        if gt is not None:
                            kb.op("dve", lambda e: e.tensor_scalar(dst, st[:, 0:cn], gt[:, k:k + 1], None, ALU.mult), reads=[bst, gb], writes=[wbuf])
                        else:
                            kb.op("dve", lambda e: e.tensor_copy(dst, st[:, 0:cn]), reads=[bst], writes=[wbuf])
                    else:
                        if gt is not None:
                            kb.op("act", lambda e: e.activation(out=dst, in_=st[:, 0:cn], func=AF.Copy, scale=gt[:, k:k + 1]), reads=[bst, gb], writes=[wbuf])
                        else:
                            kb.op("act", lambda e: e.copy(dst, st[:, 0:cn]), reads=[bst], writes=[wbuf])
                    i += 1

    def build_rot(self, kb, wt, wbuf, ranges):
        nk = wt.shape[1]
        for (dc, sc, nb) in ranges:
            for k in range(nk):
                dv = wt[:, k, dc:dc + 64 * nb].rearrange("p (b h j) -> p b h j", h=2, j=32)
                sv = wt[:, k, sc:sc + 64 * nb].rearrange("p (b h j) -> p b h j", h=2, j=32)
                kb.op("pool", lambda e, dv=dv, sv=sv: e.tensor_scalar(dv[:, :, 0, :], sv[:, :, 1, :], -1.0, None, ALU.mult),
                      reads=[wbuf], writes=[wbuf])
                kb.op("pool", lambda e, dv=dv, sv=sv: e.tensor_copy(dv[:, :, 1, :], sv[:, :, 0, :]),
                      reads=[wbuf], writes=[wbuf])

    def norm_chunk(self, kb, xT, bx, hT, bh, sq, bsq, rstd, brstd, ntok):
        kb.op("act", lambda e: e.activation(out=sq[:, :, 0:ntok], in_=xT[:, :, 0:ntok], func=AF.Square),
              reads=[bx], writes=[bsq])
        pm, bpm = self.psM.next()
        for k in range(8):
            kb.op("pe", lambda e, k=k: e.matmul(pm[:, 0:ntok], self.onesb[:], sq[:, k, 0:ntok], start=(k == 0), stop=(k == 7)),
                  reads=[bsq, self.b_onesb], writes=[bpm], inc=(k == 7))
        kb.op("act", lambda e: e.activation(out=rstd[:, 0:ntok], in_=pm[:, 0:ntok], func=AF.Ln, bias=self.epsb[:, 0:1], scale=1.0 / D),
              reads=[bpm, self.b_eps], writes=[brstd])
        kb.op("act", lambda e: e.activation(out=rstd[:, 0:ntok], in_=rstd[:, 0:ntok], func=AF.Exp, scale=-0.5),
              reads=[brstd], writes=[brstd])
        if hT is not None:
            kb.op("dve", lambda e: e.tensor_tensor(hT[:, :, 0:ntok], xT[:, :, 0:ntok],
                                                   rstd[:, 0:ntok].unsqueeze(1).to_broadcast([128, 8, ntok]), ALU.mult),
                  reads=[bx, brstd], writes=[bh])

    def ph_inproj0(self, kb):
        I = self.I
        NC0 = 3488
        wt, bw = kb.sb("w0", [128, 8, NC0], BF16)
        w2d = I["w_in_e"][0]
        self.load_weight(kb, wt, bw, w2d,
                         [(0, 0, 896), (896, 1024, 128), (1024, 1304, 640), (1664, 1280, 24),
                          (1696, 896, 128), (1824, 1152, 128), (1952, 1944, 128)], gvec=I["attn_norm"][0])
        self.build_rot(kb, wt, bw, [(2080, 0, 8), (2592, 768, 4), (2848, 1024, 8), (3360, 1536, 2)])
        xin_r = Ring(kb, "xin", [128, 4, D], F32, 2)
        xT_r = Ring(kb, "xT", [128, 8, TC], F32, 2)
        hT_r = Ring(kb, "hT", [128, 8, TC], BF16, 2)
        sq_r = Ring(kb, "sq", [128, 8, TC], BF16, 1)
        rs_r = Ring(kb, "rs", [128, TC], F32, 2)
        cs_r = Ring(kb, "cs", [128, 2, TC], F32, 2)
        t1_r = Ring(kb, "t1", [128, TC], F32, 2)
        t2_r = Ring(kb, "t2", [128, TC], F32, 2)
        st_r = Ring(kb, "stg", [128, TC], BF16, 4)
        sv_r = Ring(kb, "stv", [128, 384], BF16, 2)
        sg_r = Ring(kb, "stgt", [24, TC], F32, 2)
        tiles = []
        for p in range(4):
            tiles.append((128 * p, 2080 + 128 * p, 128, QR + 128 * p, QP + 128 * p))
        tiles.append((512, None, 128, KC, None))
        tiles.append((640, None, 128, VC, None))
        tiles.append((768, 2592, 128, KS, None))
        tiles.append((896, 2720, 128, KW, None))
        for p in range(4):
            tiles.append((1024 + 128 * p, 2848 + 128 * p, 128, BQ + 128 * p, None))
        tiles.append((1536, 3360, 128, BK, None))

        def mm_group(ps, bps, hT, bh, col, rows):
            for k in range(8):
                kb.op("pe", lambda e, k=k: e.matmul(ps[0:rows, :], wt[:, k, col:col + rows], hT[:, k, :],
                                                    start=(k == 0), stop=(k == 7)),
                      reads=[bw, bh], writes=[bps], inc=(k == 7))

        chunks = [(s, c) for s in range(NSEQ) for c in range(NCH) if not (self.lim is not None and s * NCH + c >= self.lim)]

        def load_norm(i):
            s, c = chunks[i]
            t0 = c * TC
            xin, bxin = xin_r.next()
            kb.load(xin[:], bxin, I["x"][s, t0:t0 + TC, :].rearrange("(j p) f -> p j f", p=128))
            cs, bcs = cs_r.next()
            kb.load(cs[:, 0, :], bcs, I["c_cos"][:, t0:t0 + TC])
            kb.load(cs[:, 1, :], bcs, I["c_sin"][:, t0:t0 + TC])
            xT, bxT = xT_r.next()
            for k in range(8):
                pm, bpm = self.psA.next()
                for j in range(4):
                    kb.op("pe", lambda e, k=k, j=j, pm=pm: e.transpose(pm[:, j * 128:(j + 1) * 128], xin[:, j, k * 128:(k + 1) * 128], self.identf[:]),
                          reads=[bxin, self.b_identf], writes=[bpm], inc=(j == 3))
                kb.op("act", lambda e, k=k, pm=pm: e.copy(xT[:, k, :], pm[:]), reads=[bpm], writes=[bxT])
            for k in range(8):
                kb.store(self.XT[s, k * 128:(k + 1) * 128, t0:t0 + TC], [kb.db("XT", s, c)], xT[:, k, :], bxT)
            hT, bh = hT_r.next()
            sq, bsq = sq_r.next()
            rs, brs = rs_r.next()
            self.norm_chunk(kb, xT, bxT, hT, bh, sq, bsq, rs, brs, TC)
            return (s, c, t0, hT, bh, cs, bcs)

        nxt = load_norm(0) if chunks else None
        for i in range(len(chunks)):
            s, c, t0, hT, bh, cs, bcs = nxt
            for ti, (col, rcol, rows, zrow, prow) in enumerate(tiles):
                if ti == 8 and i + 1 < len(chunks):
                    nxt = load_norm(i + 1)
                pa, bpa = self.psA.next()
                mm_group(pa, bpa, hT, bh, col, rows)
                if rcol is None:
                    stg, bst = st_r.next()
                    kb.op("act", lambda e, stg=stg, pa=pa: e.copy(stg[:], pa[:]), reads=[bpa], writes=[bst])
                    kb.store(self.ZT0[s, zrow:zrow + 128, t0:t0 + TC], [kb.db("ZT0", s, zrow, c)], stg[:], bst)
                    continue
                pb, bpb = self.psA.next()
                mm_group(pb, bpb, hT, bh, rcol, rows)
                if prow is not None:
                    stg, bst = st_r.next()
                    kb.op("act", lambda e, stg=stg, pa=pa: e.copy(stg[:], pa[:]), reads=[bpa], writes=[bst])
                    kb.store(self.ZT0[s, prow:prow + 128, t0:t0 + TC], [kb.db("ZT0", s, prow, c)], stg[:], bst)
                t1, bt1 = t1_r.next()
                t2, bt2 = t2_r.next()
                kb.op("dve", lambda e, t1=t1, pa=pa, cs=cs: e.tensor_tensor(t1[:], pa[:], cs[:, 0, :], ALU.mult),
                      reads=[bpa, bcs], writes=[bt1])
                kb.op("dve", lambda e, t2=t2, pb=pb, cs=cs: e.tensor_tensor(t2[:], pb[:], cs[:, 1, :], ALU.mult),
                      reads=[bpb, bcs], writes=[bt2])
                stg, bst = st_r.next()
                kb.op("pool", lambda e, stg=stg, t1=t1, t2=t2: e.tensor_tensor(stg[:], t1[:], t2[:], ALU.add),
                      reads=[bt1, bt2], writes=[bst])
                kb.store(self.ZT0[s, zrow:zrow + 128, t0:t0 + TC], [kb.db("ZT0", s, zrow, c)], stg[:], bst)
            pa, bpa = self.psA.next()
            mm_group(pa, bpa, hT, bh, 1664, 24)
            sg, bsg = sg_r.next()
            kb.op("act", lambda e, sg=sg, pa=pa: e.activation(out=sg[:], in_=pa[0:24, :], func=AF.Sigmoid),
                  reads=[bpa], writes=[bsg])
            kb.store(self.GT[s, :, t0:t0 + TC], [kb.db("GT", s, c)], sg[:], bsg)
            for j in range(4):
                pa, bpa = self.psA.next()
                for k in range(8):
                    kb.op("pe", lambda e, k=k, j=j, pa=pa: e.matmul(pa[:, 0:384], hT[:, k, j * 128:(j + 1) * 128], wt[:, k, 1696:2080],
                                                                    start=(k == 0), stop=(k == 7)),
                          reads=[bw, bh], writes=[bpa], inc=(k == 7))
                sv, bsv = sv_r.next()
                kb.op("act", lambda e, sv=sv, pa=pa: e.copy(sv[:], pa[:, 0:384]), reads=[bpa], writes=[bsv])
                kb.store(self.VT0[s, t0 + j * 128:t0 + (j + 1) * 128, :], [kb.db("VT0", s, c)], sv[:], bsv)

    def attn_setup(self, kb):
        self.pt_r = Ring(kb, "pt", [128, 512], BF16, 4)

    def attend(self, kb, qtiles, kdim, q_bufs, post, pair_kind=None, look=3):
        banks = []
        cur = []
        for u, unit in enumerate(qtiles):
            for j, (qap, blocks) in enumerate(unit):
                nb = len(blocks)
                for bi, (lh, mk, va) in enumerate(blocks):
                    cur.append((u, j, bi, nb, qap, lh, mk, va, j == len(unit) - 1 and bi == nb - 1))
                    if len(cur) == 4:
                        banks.append(cur)
                        cur = []
        if cur:
            banks.append(cur)
        state = {"ob": None, "u": -1, "first": True}
        pending_posts = []

        def qk(bank):
            ps, bps = self.psS.next()
            n = len(bank)
            first = True
            if pair_kind is not None and n == 4:
                v = (1 if bank[0][6] == 3 else 0) + (2 if bank[2][6] == 3 else 0)
                kb.op("pe", lambda e: e.matmul(ps[:], self.identb[:], self.cmask[:, 4 * pair_kind + v, :], start=True, stop=False, skip_group_check=True),
                      reads=[self.b_identb, self.b_cmask], writes=[bps], inc=False)
                first = False
            else:
                for si, (u, j, bi, nb, qap, lh, mk, va, last) in enumerate(bank):
                    if mk is not None:
                        sl = ps[:, si * 128:(si + 1) * 128]
                        kb.op("pe", lambda e: e.matmul(sl, self.identb[:], self.masks[:, mk, :], start=first, stop=False, skip_group_check=True),
                              reads=[self.b_identb, self.b_masks], writes=[bps], inc=False)
                        first = False
            for si, (u, j, bi, nb, qap, lh, mk, va, last) in enumerate(bank):
                sl = ps[:, si * 128:(si + 1) * 128]
                kb.op("pe", lambda e: e.matmul(sl, lh, qap, start=first, stop=(si == n - 1), skip_group_check=True),
                      reads=q_bufs, writes=[bps], inc=(si == n - 1))
                first = False
            return ps, bps

        def ex_pv(bank, ps, bps):
            n = len(bank)
            pt, bpt = self.pt_r.next()
            kb.op("act", lambda e: e.activation(out=pt[:, 0:n * 128], in_=ps[:, 0:n * 128], func=AF.Exp, scale=SCALE),
                  reads=[bps], writes=[bpt])
            while pending_posts:
                pending_posts.pop(0)()
            for si, (u, j, bi, nb, qap, lh, mk, va, last) in enumerate(bank):
                if u != state["u"]:
                    state["ob"] = self.psO.next()
                    state["u"] = u
                    state["first"] = True
                ob, bob = state["ob"]
                psl = pt[:, si * 128:(si + 1) * 128]
                kb.op("pe", lambda e: e.matmul(ob[:, j * 128:(j + 1) * 128], va, psl, start=state["first"], stop=last, skip_group_check=True),
                      reads=[bpt] + q_bufs, writes=[bob], inc=(si == n - 1 or last))
                state["first"] = False
                if last:
                    pending_posts.append(lambda u=u, ob=ob, bob=bob: post(u, ob, bob))

        pend = []
        for bank in banks:
            r = qk(bank)
            pend.append((bank, r[0], r[1]))
            if len(pend) > look:
                ex_pv(*pend.pop(0))
        while pend:
            ex_pv(*pend.pop(0))
        while pending_posts:
            pending_posts.pop(0)()

    def ph_attn0(self, kb):
        I = self.I
        nc = self.nc
        self.attn_setup(kb)
        stt, bstt = kb.sb("stt", [128, T], BF16)
        kb.load(stt[:], bstt, I["c_st"])
        ova, bova = kb.sb("ova", [128, 2, 65], BF16)
        kb.load(ova[:], bova, I["c_ova"])
        fb, bfb = kb.sb("fb", [128, 32, 64], F32)
        kb.load(fb[:], bfb, I["c_fb"])
        es_t, bes = kb.sb("esink", [128, 8], F32)
        kb.load(es_t[:], bes, I["sinks"][0].partition_broadcast(128))
        kb.op("act", lambda e: e.activation(out=es_t[:], in_=es_t[:], func=AF.Exp), reads=[bes], writes=[bes])
        acc = [kb.sb("acc%d" % g, [64, T], F32) for g in range(4)]
        G = [kb.sb("G%d" % i, [128, T], BF16) for i in range(4)]
        w1 = {"k": (acc[0][0][:].bitcast(BF16).rearrange("p (j m) -> p j m", m=256), acc[0][1]),
              "v": (acc[1][0][:].bitcast(BF16).rearrange("p (j m) -> p j m", m=256), acc[1][1])}
        w2 = {}
        pet = {}
        for nm in ("k", "v"):
            w2[nm] = kb.sb("cw2" + nm, [128, 2, 64], BF16)
            kb.load(w2[nm][0][:], w2[nm][1], I["cmp_%s_w2" % nm][0].rearrange("(c p) d -> p c d", p=128), eng="pool")
            pet[nm] = kb.sb("pet" + nm, [64, 32], BF16)
            kb.load(pet[nm][0][:], pet[nm][1], I["cmp_pe_" + nm][0].rearrange("j d -> d j"), eng="pool", nonc=True)
        cbias = {nm: kb.sb("cb" + nm, [128, 2], F32) for nm in ("k", "v")}

        def load_w1_and_bias(first):
            for nm in ("k", "v"):
                kb.load(w1[nm][0], w1[nm][1], I["cmp_%s_w1" % nm][0].rearrange("(j d) m -> d j m", d=64), eng="pool")
            if not first:
                return
            for nm in ("k", "v"):
                for mc in range(2):
                    pm, bpm = self.psM.next()
                    for j in range(32):
                        kb.op("pe", lambda e: e.matmul(pm[:, 0:1], w1[nm][0][:, j, mc * 128:(mc + 1) * 128], pet[nm][0][:, j:j + 1],
                                                       start=(j == 0), stop=(j == 31)),
                              reads=[w1[nm][1], pet[nm][1]], writes=[bpm], inc=(j == 31))
                    kb.op("act", lambda e: e.copy(cbias[nm][0][:, mc:mc + 1], pm[:, 0:1]), reads=[bpm], writes=[cbias[nm][1]])
        kin, bkin = kb.sb("kin", [64, T], BF16)
        hid, bhid = kb.sb("hid", [128, 2, 256], BF16)
        kcT = [kb.sb("kcT%d" % h, [128, 256], BF16) for h in range(2)]
        vca = [kb.sb("vca%d" % h, [128, 2, 128], BF16) for h in range(2)]
        qaug, bqaug = kb.sb("qaug", [128, 4, T], BF16)
        kaug, bkaug = G[0][0], G[0][1]
        kw, bkw = G[1][0][0:64, :], G[1][1]
        kwf = G[1][0]
        for gi_ in (1, 2, 3):
            kb.op("pool", lambda e: e.memset(G[gi_][0][64:128, :], 0.0), writes=[G[gi_][1]])
        vs, bvs = G[2][0][:].rearrange("p (t d) -> p t d", d=128), G[2][1]
        vw, bvw = G[3][0][:].rearrange("p (t d) -> p t d", d=128), G[3][1]
        gate_r = Ring(kb, "gate", [64, 4, 128], F32, 2)
        g1_r = Ring(kb, "gate1", [64, 512], F32, 2)
        d_r = Ring(kb, "dd", [64, 512], F32, 2)
        tmp_r = Ring(kb, "tmp", [64, 512], F32, 2)
        osb_r = Ring(kb, "osb", [64, 512], F32, 2)
        imp_r = Ring(kb, "impa", [128, 64], F32, 2)
        imp2_r = Ring(kb, "impb", [128, 64], F32, 2)
        m8_r = Ring(kb, "m8", [128, 16], F32, 2)
        rd_r = Ring(kb, "rd", [128, 4], F32, 2)
        selq_r = Ring(kb, "selq", [128, 64], BF16, 2)
        mix_r = Ring(kb, "mixs", [64, T], BF16, 1)
        for h in range(2):
            kb.op("pool", lambda e: e.memset(vca[h][0][:], 0.0), writes=[vca[h][1]])
            kb.op("pool", lambda e: e.memset(kcT[h][0][:], 0.0), writes=[kcT[h][1]])
        kb.op("pool", lambda e: e.memset(hid[:], 0.0), writes=[bhid])
        kb.load(kaug[64:128, :], bkaug, I["c_e"])

        AL = self.alim
        for s in range(NSEQ):
            if AL and s >= AL["nseq"]:
                continue
            zt = self.ZT0[s]
            zdeps = [kb.db("ZT0", s, r, c) for c in range(NCH) for r in (KC, VC)]
            load_w1_and_bias(s == 0)
            for nm, zrow in (("k", KC), ("v", VC)):
                for h in range(2):
                    kb.load(kin[:], bkin, zt[zrow + 64 * h:zrow + 64 * h + 64, :], sbufs=zdeps)
                    for mc in range(2):
                        pm, bpm = self.psM.next()
                        for j in range(32):
                            kb.op("pe", lambda e: e.matmul(pm[:, 0:255], w1[nm][0][:, j, mc * 128:(mc + 1) * 128],
                                                           kin[:, j:j + 16 * 254 + 1:16], start=(j == 0), stop=(j == 31)),
                                  reads=[w1[nm][1], bkin], writes=[bpm], inc=(j == 31))
                        kb.op("act", lambda e: e.activation(out=hid[:, mc, 0:255], in_=pm[:, 0:255], func=AF.Silu,
                                                            bias=cbias[nm][0][:, mc:mc + 1]),
                              reads=[bpm, cbias[nm][1]], writes=[bhid])
                    if nm == "k":
                        pm, bpm = self.psM.next()
                        for mc in range(2):
                            kb.op("pe", lambda e: e.matmul(pm[0:64, 0:255], w2["k"][0][:, mc, :], hid[:, mc, 0:255],
                                                           start=(mc == 0), stop=(mc == 1)),
                                  reads=[w2["k"][1], bhid], writes=[bpm], inc=(mc == 1))
                        kb.op("act", lambda e: e.copy(kcT[h][0][0:64, 0:255], pm[0:64, 0:255]), reads=[bpm], writes=[kcT[h][1]])
                    else:
                        for ct in range(2):
                            rows = 128 if ct == 0 else 127
                            pm, bpm = self.psM.next()
                            for mc in range(2):
                                kb.op("pe", lambda e: e.matmul(pm[0:rows, 0:64], hid[:, mc, ct * 128:ct * 128 + rows], w2["v"][0][:, mc, :],
                                                               start=(mc == 0), stop=(mc == 1)),
                                      reads=[w2["v"][1], bhid], writes=[bpm], inc=(mc == 1))
                            kb.op("act", lambda e: e.copy(vca[h][0][0:rows, ct, 0:64], pm[0:rows, 0:64]),
                                  reads=[bpm], writes=[vca[h][1]])
                            kb.op("pool", lambda e: e.memset(vca[h][0][0:rows, ct, 64:128], 1.0), writes=[vca[h][1]])
            zall = [kb.db("ZT0", s, r, c) for c in range(NCH) for r in (QP, QP + 128, QP + 256, QP + 384, QR, QR + 128, QR + 256, QR + 384, KS, KW)]
            vdeps = [kb.db("VT0", s, c) for c in range(NCH)]
            gdeps = [kb.db("GT", s, c) for c in range(NCH)]
            for h in range(2):
                if AL and h not in AL["hs"]:
                    continue
                for g in range(4):
                    hd = 4 * h + g
                    kb.load(qaug[0:64, g, :], bqaug, zt[QR + 64 * hd:QR + 64 * hd + 64, :], sbufs=zall)
                qps = []
                for g in range(4):
                    qp, bqp = G[g][0], G[g][1]
                    kb.load(qp[0:64, :], bqp, zt[QP + 64 * (4 * h + g):QP + 64 * (4 * h + g) + 64, :], sbufs=zall)
                    qps.append((qp, bqp))
                backs = []
                for qs in range(32):
                    ncts = 2 if qs >= 16 else 1
                    ob, bob = self.psO.next()
                    pi, bpi = self.psM.next()
                    pts = []
                    for ct in range(ncts):
                        ps, bps = self.psS.next()
                        off = 128 * qs - 2048 * ct
                        kb.op("pe", lambda e: e.matmul(ps[:], self.identb[:], stt[:, off:off + 128].unsqueeze(1).to_broadcast([128, 4, 128]),
                                                       start=True, stop=False, skip_group_check=True),
                              reads=[self.b_identb, bstt], writes=[bps], inc=False)
                        for g in range(4):
                            kb.op("pe", lambda e: e.matmul(ps[:, g * 128:(g + 1) * 128], kcT[h][0][:, ct * 128:(ct + 1) * 128],
                                                           qps[g][0][:, qs * 128:(qs + 1) * 128], start=False, stop=(g == 3), skip_group_check=True),
                                  reads=[kcT[h][1], qps[g][1]], writes=[bps], inc=(g == 3))
                        pt, bpt = self.pt_r.next()
                        kb.op("act", lambda e: e.activation(out=pt[:], in_=ps[:], func=AF.Exp, scale=SCALE),
                              reads=[bps], writes=[bpt])
                        pts.append((pt, bpt))
                    for g in range(4):
                        for ct in range(ncts):
                            pt, bpt = pts[ct]
                            kb.op("pe", lambda e: e.matmul(ob[:, g * 128:(g + 1) * 128], vca[h][0][:, ct, :], pt[:, g * 128:(g + 1) * 128],
                                                           start=(ct == 0), stop=(ct == ncts - 1)),
                                  reads=[bpt, vca[h][1]], writes=[bob], inc=False)
                    for g in range(4):
                        for ct in range(ncts):
                            pt, bpt = pts[ct]
                            kb.op("pe", lambda e: e.matmul(pi[:, g * 65:(g + 1) * 65], pt[:, g * 128:(g + 1) * 128], ova[:, ct, :],
                                                           start=(ct == 0), stop=(ct == ncts - 1)),
                                  reads=[bpt, bova], writes=[bpi, bob], inc=(g == 3 and ct == ncts - 1))
                    def back(qs=qs, ob=ob, bob=bob, pi=pi, bpi=bpi):
                        dd, bdd = d_r.next()
                        obv = ob[:].rearrange("p (g c) -> p g c", g=4)
                        ddv = dd[:].rearrange("p (g c) -> p g c", g=4)
                        kb.op("dve", lambda e: e.tensor_scalar(ddv, obv[64:128], 1e-30, None, ALU.max), reads=[bob], writes=[bdd])
                        yield
                        osb, bosb = osb_r.next()
                        kb.op("act", lambda e: e.copy(osb[:], ob[0:64, :]), reads=[bob], writes=[bosb])
                        yield
                        osv = osb[:].rearrange("p (g c) -> p g c", g=4)
                        rd, brd = rd_r.next()
                        kb.op("dve", lambda e: e.tensor_scalar(rd[:], pi[:, 64:260:65], 1e-30, None, ALU.max), reads=[bpi], writes=[brd])
                        yield
                        kb.op("dve", lambda e: e.reciprocal(rd[:], rd[:]), reads=[brd], writes=[brd])
                        yield
                        ia, bia = imp_r.next()
                        for g in range(4):
                            in1 = fb[:, qs, :] if g == 0 else ia[:]
                            kb.op("dve", lambda e: e.scalar_tensor_tensor(ia[:], pi[:, g * 65:g * 65 + 64], rd[:, g:g + 1], in1, ALU.mult, ALU.add),
                                  reads=[bpi, brd, bfb, bia], writes=[bia])
                            yield
                        m8, bm8 = m8_r.next()
                        ib, bib = imp2_r.next()
                        kb.op("dve", lambda e: e.max(out=m8[:, 0:8], in_=ia[:]), reads=[bia], writes=[bm8])
                        yield
                        kb.op("dve", lambda e: e.match_replace(out=ib[:], in_to_replace=m8[:, 0:8], in_values=ia[:], imm_value=-1e30),
                              reads=[bia, bm8], writes=[bib])
                        yield
                        kb.op("dve", lambda e: e.max(out=m8[:, 8:16], in_=ib[:]), reads=[bib], writes=[bm8])
                        yield
                        sq_, bsq_ = selq_r.next()
                        kb.op("dve", lambda e: e.tensor_scalar(sq_[:], ia[:], m8[:, 15:16], None, ALU.is_ge),
                              reads=[bia, bm8], writes=[bsq_])
                        yield
                        ptr, bptr = self.psM.next()
                        ptv = ptr[:].bitcast(BF16)
                        kb.op("pe", lambda e: e.transpose(ptv[0:64, 0:128], sq_[:], self.identb[:]),
                              reads=[bsq_, self.b_identb], writes=[bptr])
                        yield
                        kb.op("dve", lambda e: e.tensor_scalar(qaug[64:128, :, qs * 128:(qs + 1) * 128],
                                                               ptv[0:64, 0:128].unsqueeze(1).to_broadcast([64, 4, 128]), -1.0, None, ALU.add),
                              reads=[bptr], writes=[bqaug])
                        yield
                        gt, bgt = gate_r.next()
                        for g in range(4):
                            kb.load(gt[:, g, :], bgt, self.GT[s, 3 * (4 * h + g), qs * 128:(qs + 1) * 128].partition_broadcast(64), sbufs=gdeps)
                        kb.op("act", lambda e: e.activation(out=dd[:], in_=dd[:], func=AF.Ln), reads=[bdd], writes=[bdd])
                        yield
                        kb.op("act", lambda e: e.activation(out=dd[:], in_=dd[:], func=AF.Exp, scale=-1.0), reads=[bdd], writes=[bdd])
                        yield
                        kb.op("pool", lambda e: e.tensor_tensor(ddv, gt[:], ddv, ALU.mult), reads=[bdd, bgt], writes=[bdd])
                        yield
                        for g in range(4):
                            kb.op("dve", lambda e: e.tensor_tensor(acc[g][0][:, qs * 128:(qs + 1) * 128], osv[:, g, :], ddv[:, g, :], ALU.mult),
                                  reads=[bosb, bdd], writes=[acc[g][1]])
                            yield

                    backs.append(back())
                    if len(backs) == 2:
                        live = list(backs)
                        backs = []
                        while live:
                            for gen in list(live):
                                try:
                                    next(gen)
                                except StopIteration:
                                    live.remove(gen)
                kb.load(kaug[0:64, :], bkaug, zt[KS + 64 * h:KS + 64 * h + 64, :], sbufs=zall)
                kb.load(kw, bkw, zt[KW + 64 * h:KW + 64 * h + 64, :], sbufs=zall)
                kb.op("pool", lambda e: e.memset(vs[:, :, 64:128], 1.0), writes=[bvs])
                kb.op("pool", lambda e: e.memset(vw[:, :, 64:128], 1.0), writes=[bvw])
                kb.load(vs[:, :, 0:64], bvs, self.VT0[s][:, 64 * h:64 * h + 64].rearrange("(t p) d -> p t d", p=128), sbufs=vdeps, nonc=True)
                kb.load(vw[:, :, 0:64], bvw, self.VT0[s][:, 128 + 64 * h:128 + 64 * h + 64].rearrange("(t p) d -> p t d", p=128), sbufs=vdeps, nonc=True)
                for g in range(4):
                    hd = 4 * h + g
                    if AL and g not in AL["gs"]:
                        continue
                    for br, gi in (("sel", 1), ("win", 2)):
                        units = []
                        for u in range(8):
                            unit = []
                            for j in range(4):
                                qs = 4 * u + j
                                if br == "sel":
                                    qap = qaug[:, g, qs * 128:(qs + 1) * 128]
                                    blocks = [(kaug[:, kt * 128:(kt + 1) * 128], (0 if kt == qs else None), vs[:, kt, :]) for kt in range(qs + 1)]
                                else:
                                    qap = qaug[:, g, qs * 128:(qs + 1) * 128]
                                    blocks = []
                                    for kt in range(max(0, qs - 4), qs + 1):
                                        mk = 0 if kt == qs else (1 if kt == qs - 4 else None)
                                        blocks.append((kwf[:, kt * 128:(kt + 1) * 128], mk, vw[:, kt, :]))
                                unit.append((qap, blocks))
                            units.append(unit)

                        def post(u, ob, bob, g=g, gi=gi, hd=hd):
                            gt, bgt = g1_r.next()
                            kb.load(gt[:], bgt, self.GT[s, 3 * hd + gi, u * 512:(u + 1) * 512].partition_broadcast(64), sbufs=gdeps)
                            dd, bdd = d_r.next()
                            kb.op("dve", lambda e: e.tensor_scalar(dd[:], ob[64:128, :], 1e-30, None, ALU.max), reads=[bob], writes=[bdd])
                            kb.op("act", lambda e: e.activation(out=dd[:], in_=dd[:], func=AF.Ln), reads=[bdd], writes=[bdd])
                            kb.op("act", lambda e: e.activation(out=dd[:], in_=dd[:], func=AF.Exp, scale=-1.0), reads=[bdd], writes=[bdd])
                            kb.op("pool", lambda e: e.tensor_tensor(dd[:], gt[:], dd[:], ALU.mult), reads=[bdd, bgt], writes=[bdd])
                            tm, btm = tmp_r.next()
                            kb.op("dve", lambda e: e.tensor_tensor(tm[:], ob[0:64, :], dd[:], ALU.mult), reads=[bob, bdd], writes=[btm])
                            kb.op("pool", lambda e: e.tensor_tensor(acc[g][0][:, u * 512:(u + 1) * 512], acc[g][0][:, u * 512:(u + 1) * 512], tm[:], ALU.add),
                                  reads=[btm, acc[g][1]], writes=[acc[g][1]])

                        self.attend(kb, units, 128 if br == "sel" else 64, [bqaug, bkaug, bkw, bvs, bvw], post)
                    mx, bmx = mix_r.next()
                    kb.op("act", lambda e: e.copy(mx[:], acc[g][0][:]), reads=[acc[g][1]], writes=[bmx])
                    kb.store(self.MIXT[s, 64 * hd:64 * hd + 64, :], [kb.db("MIXT", s, hd)], mx[:], bmx)
            zb = [kb.db("ZT0", s, r, c) for c in range(NCH) for r in (BQ, BQ + 128, BQ + 256, BQ + 384, BK)]
            for hb in range(2):
                kb.load(kw, bkw, zt[BK + 64 * hb:BK + 64 * hb + 64, :], sbufs=zb)
                kb.load(vw[:, :, 0:64], bvw, self.VT0[s][:, 256 + 64 * hb:256 + 64 * hb + 64].rearrange("(t p) d -> p t d", p=128), sbufs=vdeps, nonc=True)
                for g in range(4):
                    hd = 4 * hb + g
                    if AL and hd not in AL["bh"]:
                        continue
                    Gq = G[0] if g % 2 == 0 else G[2]
                    bq, bbq = Gq[0], Gq[1]
                    kb.load(bq[0:64, :], bbq, zt[BQ + 64 * hd:BQ + 64 * hd + 64, :], sbufs=zb)
                    mx, bmx = mix_r.next()
                    units = []
                    for u in range(8):
                        unit = []
                        for j in range(4):
                            qs = 4 * u + j
                            blocks = []
                            if qs > 0:
                                blocks.append((kwf[:, (qs - 1) * 128:qs * 128], 1, vw[:, qs - 1, :]))
                            else:
                                blocks.append((kwf[:, 0:128], 3, vw[:, 0, :]))
                            blocks.append((kwf[:, qs * 128:(qs + 1) * 128], 0, vw[:, qs, :]))
                            unit.append((bq[:, qs * 128:(qs + 1) * 128], blocks))
                        units.append(unit)

                    def post(u, ob, bob, hd=hd, mx=mx, bmx=bmx):
                        dd, bdd = d_r.next()
                        kb.op("dve", lambda e: e.tensor_scalar(dd[:], ob[64:128, :], es_t[0:64, hd:hd + 1], None, ALU.add), reads=[bob, bes], writes=[bdd])
                        kb.op("act", lambda e: e.activation(out=dd[:], in_=dd[:], func=AF.Ln), reads=[bdd], writes=[bdd])
                        kb.op("act", lambda e: e.activation(out=dd[:], in_=dd[:], func=AF.Exp, scale=-1.0), reads=[bdd], writes=[bdd])
                        kb.op("dve", lambda e: e.tensor_tensor(mx[:, u * 512:(u + 1) * 512], ob[0:64, :], dd[:], ALU.mult), reads=[bob, bdd], writes=[bmx])

                    self.attend(kb, units, 64, [bbq, bkw, bvw], post, pair_kind=1)
                    kb.store(self.MIXT[s, 512 + 64 * hd:512 + 64 * hd + 64, :], [kb.db("MIXT", s, 8 + hd)], mx[:], bmx)

    def ph_outproj(self, layer):
        def run(kb):
            I = self.I
            wt, bw = kb.sb("wo", [128, 8, D], BF16)
            w2d = I["w_out_e"][0] if layer == 0 else I["w_out_o"][0]
            self.load_weight(kb, wt, bw, w2d, [(0, 0, D)])
            mx_r = Ring(kb, "mxc", [128, 8, TC], BF16, 2)
            xT_r = Ring(kb, "xTo", [128, 8, TC], F32, 2)
            for s in range(NSEQ):
                mdeps = [kb.db("MIXT", s, hd) for hd in range(16)]
                for c in range(NCH):
                    if self.lim is not None and s * NCH + c >= self.lim:
                        continue
                    t0 = c * TC
                    mx, bmx = mx_r.next()
                    kb.load(mx[:], bmx, self.MIXT[s, :, t0:t0 + TC].rearrange("(k p) t -> p k t", p=128), sbufs=mdeps)
                    xT, bxT = xT_r.next()
                    kb.load(xT[:], bxT, self.XT[s, :, t0:t0 + TC].rearrange("(k p) t -> p k t", p=128), sbufs=[kb.db("XT", s, c)])
                    for n in range(8):
                        pm, bpm = self.psA.next()
                        for k in range(8):
                            kb.op("pe", lambda e, k=k, n=n, pm=pm, mx=mx: e.matmul(pm[:], wt[:, k, n * 128:(n + 1) * 128], mx[:, k, :], start=(k == 0), stop=(k == 7)),
                                  reads=[bw, bmx], writes=[bpm], inc=(k == 7))
                        kb.op("dve", lambda e, n=n, pm=pm, xT=xT: e.tensor_tensor(xT[:, n, :], pm[:], xT[:, n, :], ALU.add), reads=[bpm, bxT], writes=[bxT])
                    kb.store(self.XT[s, :, t0:t0 + TC].rearrange("(k p) t -> p k t", p=128), [kb.db("XT", s, c)], xT[:], bxT)
        return run

    def ph_ffn(self, layer):
        def run(kb):
            I = self.I
            FT = 256
            wgu, bwgu = kb.sb("wgu", [128, 8, 2 * FH], BF16)
            stg_r = Ring(kb, "wstgf", [128, 1408], F32, 2)
            self.load_weight(kb, wgu, bwgu, I["w_gate_up"][layer], [(0, 0, 2 * FH)], gvec=I["ffn_norm"][layer], stg_r=stg_r, CH=1408)
            wd, bwd = kb.sb("wd", [128, 22, D], BF16)
            self.load_weight(kb, wd, bwd, I["w_down"][layer], [(0, 0, D)], stg_r=stg_r, CH=1408)
            xT_r = Ring(kb, "xTf", [128, 8, FT], F32, 2)
            hT_r = Ring(kb, "hTf", [128, 8, FT], BF16, 2)
            sq_r = Ring(kb, "sqf", [128, 8, FT], BF16, 1)
            rs_r = Ring(kb, "rsf", [128, FT], F32, 1)
            act_r = Ring(kb, "actf", [128, 22, FT], BF16, 1)
            sg_r = Ring(kb, "sgf", [128, FT], F32, 3)
            chunks = [(s, c) for s in range(NSEQ) for c in range(T // FT)
                      if not (self.lim is not None and s * (T // FT) + c >= 2 * self.lim)]

            def load_norm(i):
                s, c = chunks[i]
                t0 = c * FT
                xT, bxT = xT_r.next()
                kb.load(xT[:], bxT, self.XT[s, :, t0:t0 + FT].rearrange("(k p) t -> p k t", p=128), sbufs=[kb.db("XT", s, t0 // TC)])
                hT, bh = hT_r.next()
                sq, bsq = sq_r.next()
                rs, brs = rs_r.next()
                self.norm_chunk(kb, xT, bxT, hT, bh, sq, bsq, rs, brs, FT)
                return (s, t0, xT, bxT, hT, bh)

            nxt = load_norm(0) if chunks else None
            for i in range(len(chunks)):
                s, t0, xT, bxT, hT, bh = nxt
                act, bact = act_r.next()
                for n in range(22):
                    pg, bpg = self.psM.next()
                    pu, bpu = self.psS.next()
                    for k in range(8):
                        kb.op("pe", lambda e: e.matmul(pg[:, 0:FT], wgu[:, k, n * 128:(n + 1) * 128], hT[:, k, :], start=(k == 0), stop=(k == 7)),
                              reads=[bwgu, bh], writes=[bpg], inc=(k == 7))
                    for k in range(8):
                        kb.op("pe", lambda e: e.matmul(pu[:, 0:FT], wgu[:, k, FH + n * 128:FH + (n + 1) * 128], hT[:, k, :], start=(k == 0), stop=(k == 7)),
                              reads=[bwgu, bh], writes=[bpu], inc=(k == 7))
                    sg, bsg = sg_r.next()
                    kb.op("act", lambda e: e.activation(out=sg[:], in_=pg[:, 0:FT], func=AF.Silu), reads=[bpg], writes=[bsg])
                    kb.op("dve", lambda e: e.tensor_tensor(act[:, n, :], pu[:, 0:FT], sg[:], ALU.mult), reads=[bpu, bsg], writes=[bact])
                if i + 1 < len(chunks):
                    nxt = load_norm(i + 1)
                for n in range(8):
                    pm, bpm = self.psO.next()
                    for k in range(22):
                        kb.op("pe", lambda e: e.matmul(pm[:, 0:FT], wd[:, k, n * 128:(n + 1) * 128], act[:, k, :], start=(k == 0), stop=(k == 21)),
                              reads=[bwd, bact], writes=[bpm], inc=(k == 21))
                    kb.op("dve", lambda e: e.tensor_tensor(xT[:, n, :], pm[:, 0:FT], xT[:, n, :], ALU.add), reads=[bpm, bxT], writes=[bxT])
                kb.store(self.XT[s, :, t0:t0 + FT].rearrange("(k p) t -> p k t", p=128), [kb.db("XT", s, t0 // TC)], xT[:], bxT)
        return run

    def ph_inproj1(self, kb):
        I = self.I
        NC1 = 3072 + 2048
        wt, bw = kb.sb("w1", [128, 8, NC1], BF16)
        self.load_weight(kb, wt, bw, I["w_qkv_o"][0], [(0, 0, 3072)], gvec=I["attn_norm"][1])
        self.build_rot(kb, wt, bw, [(3072, 0, 32)])
        xT_r = Ring(kb, "xT1", [128, 8, TC], F32, 2)
        hT_r = Ring(kb, "hT1", [128, 8, TC], BF16, 2)
        sq_r = Ring(kb, "sq1", [128, 8, TC], BF16, 1)
        rs_r = Ring(kb, "rs1", [128, TC], F32, 2)
        cs_r = Ring(kb, "cs1", [128, 2, TC], F32, 2)
        t1_r = Ring(kb, "t11", [128, TC], F32, 2)
        t2_r = Ring(kb, "t21", [128, TC], F32, 2)
        st_r = Ring(kb, "stg1", [128, TC], BF16, 4)
        sv_r = Ring(kb, "stv1", [128, D], BF16, 2)
        chunks = [(s, c) for s in range(NSEQ) for c in range(NCH) if not (self.lim is not None and s * NCH + c >= self.lim)]

        def load_norm(i):
            s, c = chunks[i]
            t0 = c * TC
            xT, bxT = xT_r.next()
            kb.load(xT[:], bxT, self.XT[s, :, t0:t0 + TC].rearrange("(k p) t -> p k t", p=128), sbufs=[kb.db("XT", s, c)])
            cs, bcs = cs_r.next()
            kb.load(cs[:, 0, :], bcs, I["c_cos"][:, t0:t0 + TC])
            kb.load(cs[:, 1, :], bcs, I["c_sin"][:, t0:t0 + TC])
            hT, bh = hT_r.next()
            sq, bsq = sq_r.next()
            rs, brs = rs_r.next()
            self.norm_chunk(kb, xT, bxT, hT, bh, sq, bsq, rs, brs, TC)
            return (s, c, t0, hT, bh, cs, bcs)

        nxt = load_norm(0) if chunks else None
        for i in range(len(chunks)):
            s, c, t0, hT, bh, cs, bcs = nxt
            for p in range(16):
                if p == 10 and i + 1 < len(chunks):
                    nxt = load_norm(i + 1)
                col = 128 * p
                pa, bpa = self.psA.next()
                pb, bpb = self.psA.next()
                for (pp, bpp, cc) in ((pa, bpa, col), (pb, bpb, 3072 + col)):
                    for k in range(8):
                        kb.op("pe", lambda e: e.matmul(pp[:], wt[:, k, cc:cc + 128], hT[:, k, :], start=(k == 0), stop=(k == 7)),
                              reads=[bw, bh], writes=[bpp], inc=(k == 7))
                t1, bt1 = t1_r.next()
                t2, bt2 = t2_r.next()
                kb.op("dve", lambda e: e.tensor_tensor(t1[:], pa[:], cs[:, 0, :], ALU.mult), reads=[bpa, bcs], writes=[bt1])
                kb.op("dve", lambda e: e.tensor_tensor(t2[:], pb[:], cs[:, 1, :], ALU.mult), reads=[bpb, bcs], writes=[bt2])
                stg, bst = st_r.next()
                kb.op("pool", lambda e: e.tensor_tensor(stg[:], t1[:], t2[:], ALU.add), reads=[bt1, bt2], writes=[bst])
                kb.store(self.ZT1[s, col:col + 128, t0:t0 + TC], [kb.db("ZT1", s, p, c)], stg[:], bst)
            for j in range(4):
                sv, bsv = sv_r.next()
                for half in range(2):
                    pa, bpa = self.psA.next()
                    for k in range(8):
                        kb.op("pe", lambda e: e.matmul(pa[:], hT[:, k, j * 128:(j + 1) * 128], wt[:, k, 2048 + 512 * half:2048 + 512 * (half + 1)],
                                                       start=(k == 0), stop=(k == 7)),
                              reads=[bw, bh], writes=[bpa], inc=(k == 7))
                    kb.op("act", lambda e: e.copy(sv[:, 512 * half:512 * (half + 1)], pa[:]), reads=[bpa], writes=[bsv])
                kb.store(self.VT1[s, t0 + j * 128:t0 + (j + 1) * 128, :], [kb.db("VT1", s, c)], sv[:], bsv)

    def ph_attn1(self, kb):
        nc = self.nc
        self.attn_setup(kb)
        q_r = Ring(kb, "q1", [128, T], BF16, 2)
        k_r = Ring(kb, "k1", [128, T], BF16, 2)
        for rr in (q_r, k_r):
            for (t_, b_) in rr.items:
                kb.op("pool", lambda e: e.memset(t_[64:128, :], 0.0), writes=[b_])
        va = [Ring(kb, "va%d" % p, [128, 32, 128], BF16, 2) for p in range(3)]
        for p in range(3):
            for (t, b) in va[p].items:
                kb.op("pool", lambda e, t=t: e.memset(t[:, :, 64:128], 1.0), writes=[b])
        acc_r = Ring(kb, "acc1", [128, T], F32, 2)
        mx_r = Ring(kb, "mix1", [64, T], BF16, 2)
        rec_r = Ring(kb, "rec1", [64, 512], F32, 3)
        pending_final = []
        dils = (1, 4, 16)
        for s in range(NSEQ):
            zdeps = [kb.db("ZT1", s, p, c) for p in range(16) for c in range(NCH)]
            vdeps = [kb.db("VT1", s, c) for c in range(NCH)]
            for hd in range(16):
                if self.alim and (s >= self.alim["nseq"] or hd not in self.alim.get("ch", [0])):
                    continue
                qt, bq = q_r.next()
                kt_, bk = k_r.next()
                kb.load(qt[0:64, :], bq, self.ZT1[s, 64 * hd:64 * hd + 64, :], sbufs=zdeps)
                kb.load(kt_[0:64, :], bk, self.ZT1[s, 1024 + 64 * hd:1024 + 64 * hd + 64, :], sbufs=zdeps)
                vts = []
                for p, d in enumerate(dils):
                    vt, bv = va[p].next()
                    src = self.VT1[s][:, 64 * hd:64 * hd + 64]
                    if d == 1:
                        kb.load(vt[:, :, 0:64], bv, src.rearrange("(b i) f -> i b f", i=128), sbufs=vdeps, nonc=True)
                    else:
                        nbt = 32 // d
                        for r in range(d):
                            kb.load(vt[:, r * nbt:(r + 1) * nbt, 0:64], bv, src[r::d, :].rearrange("(b i) f -> i b f", i=128), sbufs=vdeps, nonc=True)
                    vts.append((vt, bv))
                acc, bacc = acc_r.next()
                for p, d in enumerate(dils):
                    vt, bv = vts[p]
                    nbt = 32 // d
                    qtl = []
                    if d == 1:
                        order = [(0, b) for b in range(32)]
                    elif d == 4:
                        order = [(r, b) for b in range(8) for r in range(4)]
                    else:
                        order = [(r, b) for b in range(2) for r in range(16)]
                    units = []
                    for u in range(8):
                        unit = []
                        for (r, b) in order[4 * u:4 * u + 4]:
                            st = d * 128 * b + r
                            qap = qt[:, st:st + d * 127 + 1:d]
                            blocks = []
                            if b > 0:
                                sp_ = d * 128 * (b - 1) + r
                                blocks.append((kt_[:, sp_:sp_ + d * 127 + 1:d], 2, vt[:, r * nbt + b - 1, :]))
                            else:
                                blocks.append((kt_[:, st:st + d * 127 + 1:d], 3, vt[:, r * nbt + b, :]))
                            blocks.append((kt_[:, st:st + d * 127 + 1:d], 0, vt[:, r * nbt + b, :]))
                            unit.append((qap, blocks))
                        units.append(unit)

                    def post(u, ob, bob, p=p, d=d, order=order, acc=acc, bacc=bacc):
                        r0, b0 = order[4 * u]
                        if d == 1:
                            dst = acc[:, 512 * u:512 * (u + 1)]
                            src = ob[:]
                        elif d == 4:
                            dst = acc[:, 512 * b0:512 * (b0 + 1)].rearrange("p (i r) -> p r i", r=4)
                            src = ob[:].rearrange("p (r i) -> p r i", r=4)
                        else:
                            dst = acc[:, 2048 * b0:2048 * (b0 + 1)].rearrange("p (i r) -> p r i", r=16)[:, r0:r0 + 4, :]
                            src = ob[:].rearrange("p (r i) -> p r i", r=4)
                        if p == 0:
                            kb.op("act", lambda e: e.copy(dst, src), reads=[bob], writes=[bacc])
                        else:
                            kb.op("dve", lambda e: e.tensor_tensor(dst, src, dst, ALU.add), reads=[bob, bacc], writes=[bacc])
                        if pending_final:
                            pending_final.pop(0)()

                    self.attend(kb, units, 64, [bq, bk, bv], post, pair_kind=0)
                while pending_final:
                    pending_final.pop(0)()

                def mk_final(s=s, hd=hd, acc=acc, bacc=bacc):
                    mx, bmx = mx_r.next()
                    fl = []
                    for c in range(8):
                        def chunk(c=c):
                            cs_ = slice(512 * c, 512 * (c + 1))
                            rc, brc = rec_r.next()
                            kb.op("dve", lambda e: e.tensor_copy(rc[:], acc[64:128, cs_]), reads=[bacc], writes=[brc])
                            kb.op("act", lambda e: e.activation(out=rc[:], in_=rc[:], func=AF.Ln), reads=[brc], writes=[brc])
                            kb.op("act", lambda e: e.activation(out=rc[:], in_=rc[:], func=AF.Exp, scale=-1.0), reads=[brc], writes=[brc])
                            kb.op("pool", lambda e: e.tensor_tensor(mx[:, cs_], acc[0:64, cs_], rc[:], ALU.mult), reads=[bacc, brc], writes=[bmx])
                        fl.append(chunk)
                    fl.append(lambda: kb.store(self.MIXT[s, 64 * hd:64 * hd + 64, :], [kb.db("MIXT", s, hd)], mx[:], bmx))
                    return fl
                pending_final.extend(mk_final())

        while pending_final:
            pending_final.pop(0)()

    def ph_final(self, kb):
        I = self.I
        nc = self.nc
        gt, gb = kb.sb("gfin", [128, 8], F32)
        kb.load(gt[:], gb, I["final_norm"].rearrange("(k p) -> p k", p=128), nonc=True)
        xT_r = Ring(kb, "xTz", [128, 8, TC], F32, 2)
        sq_r = Ring(kb, "sqz", [128, 8, TC], BF16, 1)
        rs_r = Ring(kb, "rsz", [128, TC], F32, 2)
        yo_r = Ring(kb, "yo", [128, 4, D], F32, 2)
        for s in range(NSEQ):
            for c in range(NCH):
                if self.lim is not None and s * NCH + c >= self.lim:
                    continue
                t0 = c * TC
                xT, bxT = xT_r.next()
                kb.load(xT[:], bxT, self.XT[s, :, t0:t0 + TC].rearrange("(k p) t -> p k t", p=128), sbufs=[kb.db("XT", s, c)])
                sq, bsq = sq_r.next()
                rs, brs = rs_r.next()
                self.norm_chunk(kb, xT, bxT, None, None, sq, bsq, rs, brs, TC)
                for k in range(8):
                    kb.op("dve", lambda e, k=k, xT=xT, rs=rs: e.scalar_tensor_tensor(xT[:, k, :], xT[:, k, :], gt[:, k:k + 1], rs[:], ALU.mult, ALU.mult),
                          reads=[bxT, gb, brs], writes=[bxT])
                yo, byo = yo_r.next()
                for j in range(4):
                    for kk in range(2):
                        pm, bpm = self.psM.next()
                        for k4 in range(4):
                            k = kk * 4 + k4
                            kb.op("pe", lambda e, k=k, k4=k4, j=j, pm=pm, xT=xT: e.transpose(pm[:, k4 * 128:(k4 + 1) * 128], xT[:, k, j * 128:(j + 1) * 128], self.identf[:]),
                                  reads=[bxT, self.b_identf], writes=[bpm], inc=(k4 == 3))
                        kb.op("act", lambda e, pm=pm, yo=yo, j=j, kk=kk: e.copy(yo[:, j, kk * 512:(kk + 1) * 512], pm[:]), reads=[bpm], writes=[byo])
                kb.store(self.out[s, t0:t0 + TC, :].rearrange("(j p) f -> p j f", p=128), [kb.db("out", s, c)], yo[:], byo, eng="sp")


_CONSTS = None


def kernel(**inputs):
    global _CONSTS
    if _CONSTS is None:
        _CONSTS = host_consts()
    prog = Prog()
    nc = prog.build()
    x = np.ascontiguousarray(inputs["x"], dtype=np.float32)
    in_maps = []
    for cid in range(8):
        m = {"x": x[cid * NSEQ:(cid + 1) * NSEQ]}
        for name, shape in IN_SPECS[1:]:
            m[name] = np.ascontiguousarray(inputs[name], dtype=np.float32)
        m.update(_CONSTS)
        in_maps.append(m)
    res = run_bass_kernel_spmd(nc, in_maps, core_ids=list(range(8)))
    return np.concatenate([r["out"] for r in res.results], axis=0)
```

```python
import numpy as np
import ml_dtypes
import concourse.bass as bass
import concourse.mybir as mybir
from concourse.bass_utils import run_bass_kernel_spmd
from contextlib import ExitStack

F32 = mybir.dt.float32
BF16 = mybir.dt.bfloat16
ALU = mybir.AluOpType
AF = mybir.ActivationFunctionType

T = 4096
D = 1024
NSEQ = 2
TC = 512
NCH = T // TC
FH = 2816
BIG = 30000.0
SCALE = 0.125
EPS = 1e-5

QP, QR, KC, VC, KS, KW, BQ, BK = 0, 512, 1024, 1152, 1280, 1408, 1536, 2048
ZT0_ROWS = 2176


class Buf:
    __slots__ = ("name", "w", "r", "box", "psum")

    def __init__(self, name="", psum=False):
        self.name = name
        self.w = None
        self.r = {}
        self.box = None
        self.psum = psum


class _Rec:
    def __init__(self):
        self.call = None

    def __getattr__(self, name):
        def f(*a, **kw):
            self.call = (name, a, kw)
            return None
        return f


class Sched:
    ENG = ("pe", "act", "dve", "pool", "sp")

    def __init__(self, nc, es, nbox=44):
        self.nc = nc
        self.es = es
        self.q = {e: [] for e in self.ENG}
        self.esem = {e: es.enter_context(nc.semaphore("es_" + e)) for e in self.ENG}
        self.ecount = {e: 0 for e in self.ENG}
        self.waited = {e: {} for e in self.ENG}
        self.boxes = [[es.enter_context(nc.semaphore("dq%d" % i)), 0] for i in range(2 * nbox)]
        self.boxpool = {"sp": self.boxes[:nbox], "pool": self.boxes[nbox:]}
        self.nextbox = {"sp": 0, "pool": 0}
        self.nops = 0

    def box(self, buf, eng):
        if buf.box is None:
            buf.box = {}
        if eng not in buf.box:
            pool = self.boxpool[eng]
            buf.box[eng] = pool[self.nextbox[eng] % len(pool)]
            self.nextbox[eng] += 1
        return buf.box[eng]

    def _add(self, deps, tok, eng, is_dma, raw):
        ek, sem, val, bx = tok
        if bx is not None:
            val = bx[1]
        elif ek == eng and not is_dma:
            if eng == "pe":
                return
        k = id(sem)
        if k not in deps or deps[k][1] < val:
            deps[k] = (sem, val)

    maxops = None

    def op(self, eng, fn, reads=(), writes=(), inc=True, dma=None, force=False):
        if self.maxops is not None and self.nops >= self.maxops and not force:
            return None
        rec = _Rec()
        fn(rec)
        call = rec.call
        is_dma = dma is not None
        deps = {}
        for b in reads:
            if b.w is not None:
                self._add(deps, b.w, eng, is_dma, True)
            if b.psum:
                for t in b.r.values():
                    if t[0] != eng:
                        self._add(deps, t, eng, is_dma, False)
        for b in writes:
            if b.w is not None:
                self._add(deps, b.w, eng, is_dma, False)
            for t in b.r.values():
                self._add(deps, t, eng, is_dma, False)
        waits = []
        wd = self.waited[eng]
        for k, (sem, val) in deps.items():
            if wd.get(k, 0) >= val:
                continue
            wd[k] = val
            waits.append((sem, val))
        if is_dma:
            dma[1] += 16
            tok = ("dma", dma[0], dma[1], dma)
            incspec = (dma[0], 16)
        elif inc:
            self.ecount[eng] += 1
            tok = (eng, self.esem[eng], self.ecount[eng], None)
            incspec = (self.esem[eng], 1)
        else:
            tok = (eng, self.esem[eng], self.ecount[eng] + 1, None)
            incspec = None
        kt = id(tok[1])
        for b in reads:
            o = b.r.get(kt)
            if o is None or o[2] < tok[2]:
                b.r[kt] = tok
        for b in writes:
            b.w = tok
            b.r = {}
        self.nops += 1
        self.q[eng].append((waits, call, incspec))
        return tok

    def barrier(self):
        targets = [(self.esem[f], self.ecount[f]) for f in self.ENG if self.ecount[f] > 0]
        targets += [(bx[0], bx[1]) for bx in self.boxes if bx[1] > 0]
        for e in self.ENG:
            waits = []
            wd = self.waited[e]
            for sem, val in targets:
                k = id(sem)
                if sem is self.esem[e] and e == "pe":
                    continue
                if wd.get(k, 0) >= val:
                    continue
                wd[k] = val
                waits.append((sem, val))
            self.q[e].append((waits, ("nop", (), {}), None))

    def emit(self, block):
        def run(e, key):
            for waits, (name, a, kw), incspec in self.q[key]:
                for sem, val in waits:
                    e.wait_ge(sem, val)
                ins = getattr(e, name)(*a, **kw)
                if incspec is not None:
                    ins.then_inc(incspec[0], incspec[1])

        @block.tensor
        def _(e):
            run(e, "pe")

        @block.scalar
        def _(e):
            run(e, "act")

        @block.vector
        def _(e):
            run(e, "dve")

        @block.gpsimd
        def _(e):
            run(e, "pool")

        @block.sync
        def _(e):
            run(e, "sp")


class Ring:
    def __init__(self, kb, name, shape, dt, n, psum=False):
        self.items = []
        for i in range(n):
            self.items.append(kb.ps(name + str(i), shape, dt) if psum else kb.sb(name + str(i), shape, dt))
        self.i = 0

    def next(self):
        it = self.items[self.i % len(self.items)]
        self.i += 1
        return it


class KB:
    def __init__(self, nc, es):
        self.nc = nc
        self.es = es
        self.S = Sched(nc, es)
        self.dbufs = {}

    _uid = 0

    def sb(self, name, shape, dt):
        KB._uid += 1
        name = "%s_%d" % (name, KB._uid)
        return self.es.enter_context(self.nc.sbuf_tensor(name, shape, dt)), Buf(name)

    def ps(self, name, shape, dt):
        KB._uid += 1
        name = "%s_%d" % (name, KB._uid)
        return self.es.enter_context(self.nc.psum_tensor(name, shape, dt)), Buf(name, psum=True)

    def db(self, *key):
        b = self.dbufs.get(key)
        if b is None:
            b = self.dbufs[key] = Buf(str(key))
        return b

    def load(self, dst, dbuf, src, sbufs=(), eng="sp", nonc=False):
        kw = dict(allow_slow_non_contiguous=True) if nonc else {}
        self.S.op(eng, lambda e: e.dma_start(out=dst, in_=src, **kw), reads=list(sbufs), writes=[dbuf],
                  dma=self.S.box(dbuf, eng))

    def store(self, dst, dbufs, src, sbuf, eng="pool"):
        self.S.op(eng, lambda e: e.dma_start(out=dst, in_=src), reads=[sbuf], writes=list(dbufs),
                  dma=self.S.box(sbuf, eng))

    def op(self, eng, fn, reads=(), writes=(), inc=True):
        return self.S.op(eng, fn, reads=reads, writes=writes, inc=inc)


def host_consts():
    bf = ml_dtypes.bfloat16
    c = {}
    inv = 1.0 / (10000.0 ** (np.arange(0, 64, 2, dtype=np.float32) / 64.0))
    ang = np.arange(T, dtype=np.float32)[:, None] * inv[None, :]
    ang = np.concatenate([ang, ang], axis=-1)
    cos = np.cos(ang).astype(np.float32).T
    sin = np.sin(ang).astype(np.float32).T
    c["c_cos"] = np.ascontiguousarray(np.concatenate([cos, cos], 0))
    c["c_sin"] = np.ascontiguousarray(np.concatenate([sin, sin], 0))
    c["c_identf"] = np.eye(128, dtype=np.float32)
    c["c_identb"] = np.eye(128, dtype=np.float32).astype(bf)
    c["c_onesb"] = np.ones((128, 128), dtype=np.float32).astype(bf)
    k = np.arange(128)[:, None]
    q = np.arange(128)[None, :]
    tri = np.where(k > q, -BIG, 0.0)
    anti_s = np.where(k <= q, -BIG, 0.0)
    anti_i = np.where(k < q, -BIG, 0.0)
    full = np.full((128, 128), -BIG)
    c["c_masks"] = np.ascontiguousarray(np.stack([tri, anti_s, anti_i, full], 1).astype(np.float32)).astype(bf)
    comps = []
    for prev in (anti_i, anti_s):
        for v in range(4):
            p0 = full if (v & 1) else prev
            p1 = full if (v & 2) else prev
            comps.append(np.concatenate([p0, tri, p1, tri], axis=1))
    c["c_cmask"] = np.ascontiguousarray(np.stack(comps, 1).astype(np.float32)).astype(bf)
    u = np.arange(T)[None, :]
    c["c_st"] = np.where(16 * k + 31 > u, -BIG, 0.0).astype(np.float32).astype(bf)
    n = np.arange(64)[:, None]
    j = np.arange(T)[None, :]
    c["c_e"] = np.where(j // 64 == n, BIG, 0.0).astype(np.float32).astype(bf)
    cs = np.arange(256) * 16
    ss = np.arange(64) * 64
    ov = ((cs[:, None] <= ss[None, :] + 63) & (cs[:, None] + 31 >= ss[None, :])).astype(np.float32)
    ov[255, :] = 0.0
    ova = np.zeros((256, 65), np.float32)
    ova[:, :64] = ov
    ova[:255, 64] = 1.0
    c["c_ova"] = np.ascontiguousarray(ova.reshape(2, 128, 65).transpose(1, 0, 2)).astype(bf)
    t = np.arange(T)
    cur = t // 64
    blk = np.arange(64)[None, :]
    fb = np.zeros((T, 64), np.float32)
    fb[blk > cur[:, None]] = -1e4
    forced = (blk == 0) | (blk == cur[:, None]) | (blk == cur[:, None] - 1)
    fb[forced] = 1e4
    c["c_fb"] = np.ascontiguousarray(fb.reshape(32, 128, 64).transpose(1, 0, 2))
    return c


CONST_SPECS = [("c_cos", [128, T], F32), ("c_sin", [128, T], F32), ("c_identf", [128, 128], F32),
               ("c_identb", [128, 128], BF16), ("c_onesb", [128, 128], BF16), ("c_masks", [128, 4, 128], BF16), ("c_cmask", [128, 8, 512], BF16),
               ("c_st", [128, T], BF16), ("c_e", [64, T], BF16), ("c_ova", [128, 2, 65], BF16),
               ("c_fb", [128, 32, 64], F32)]

IN_SPECS = [("x", [NSEQ, T, D]), ("attn_norm", [2, D]), ("ffn_norm", [2, D]), ("final_norm", [D]),
            ("w_in_e", [1, D, 2072]), ("w_out_e", [1, D, D]), ("cmp_pe_k", [1, 32, 64]), ("cmp_pe_v", [1, 32, 64]),
            ("cmp_k_w1", [1, 2048, 256]), ("cmp_k_w2", [1, 256, 64]), ("cmp_v_w1", [1, 2048, 256]),
            ("cmp_v_w2", [1, 256, 64]), ("sinks", [1, 8]), ("w_qkv_o", [1, D, 3072]), ("w_out_o", [1, D, D]),
            ("w_gate_up", [2, D, 2 * FH]), ("w_down", [2, FH, D])]


class Prog:
    def __init__(self, stop_after=None, dump=(), lim=None, maxops=None, alim=None, only=None):
        self.only = only
        Sched.maxops = maxops
        self.alim = alim
        self.lim = lim
        self.stop_after = stop_after
        self.dump = set(dump)
        nc = self.nc = bass.Bass("TRN2", target_bir_lowering=False)
        self.I = {}
        for name, shape in IN_SPECS:
            self.I[name] = nc.dram_tensor(name, shape, F32, kind="ExternalInput").ap()
        for name, shape, dt in CONST_SPECS:
            self.I[name] = nc.dram_tensor(name, shape, dt, kind="ExternalInput").ap()
        self.out = nc.dram_tensor("out", [NSEQ, T, D], F32, kind="ExternalOutput").ap()

        def scratch(name, shape, dt):
            kind = "ExternalOutput" if name in self.dump else "Internal"
            return nc.dram_tensor(name, shape, dt, kind=kind).ap()

        self.XT = scratch("XT", [NSEQ, D, T], F32)
        self.ZT0 = scratch("ZT0", [NSEQ, ZT0_ROWS, T], BF16)
        self.VT0 = scratch("VT0", [NSEQ, T, 384], BF16)
        self.GT = scratch("GT", [NSEQ, 24, T], F32)
        self.MIXT = scratch("MIXT", [NSEQ, D, T], BF16)
        self.ZT1 = scratch("ZT1", [NSEQ, 2048, T], BF16)
        self.VT1 = scratch("VT1", [NSEQ, T, D], BF16)
        self.DBG = scratch("DBG", [128, 4096], F32)

    def build(self):
        nc = self.nc
        phases = [self.ph_inproj0, self.ph_attn0, self.ph_outproj(0), self.ph_ffn(0),
                  self.ph_inproj1, self.ph_attn1, self.ph_outproj(1), self.ph_ffn(1), self.ph_final]
        names = ["inproj0", "attn0", "outproj0", "ffn0", "inproj1", "attn1", "outproj1", "ffn1", "final"]
        with ExitStack() as es0:
            kb = self.kb = KB(nc, es0)
            self.consts_setup(kb)
            self.final_bufs = []
            for ph, nm in zip(phases, names):
                if self.only is not None and nm not in self.only:
                    continue
                with ExitStack() as es:
                    kb.es = es
                    ph(kb)
                    kb.S.barrier()
                kb.es = es0
                if self.stop_after == nm:
                    break
            kb.S.op("sp", lambda e: e.nop(), reads=list(kb.dbufs.values()), force=True)
            with nc.Block() as block:
                kb.S.emit(block)
        return nc

    def consts_setup(self, kb):
        I = self.I
        self.identf, self.b_identf = kb.sb("identf", [128, 128], F32)
        self.identb, self.b_identb = kb.sb("identb", [128, 128], BF16)
        self.onesb, self.b_onesb = kb.sb("onesb", [128, 128], BF16)
        self.masks, self.b_masks = kb.sb("masks", [128, 4, 128], BF16)
        self.cmask, self.b_cmask = kb.sb("cmask", [128, 8, 512], BF16)
        kb.load(self.cmask[:], self.b_cmask, I["c_cmask"])
        self.epsb, self.b_eps = kb.sb("epsb", [128, 1], F32)
        kb.load(self.identf[:], self.b_identf, I["c_identf"])
        kb.load(self.identb[:], self.b_identb, I["c_identb"])
        kb.load(self.onesb[:], self.b_onesb, I["c_onesb"])
        kb.load(self.masks[:], self.b_masks, I["c_masks"])
        kb.op("pool", lambda e: e.memset(self.epsb[:], EPS), writes=[self.b_eps])
        self.psS = Ring(kb, "psS", [128, 512], F32, 4, psum=True)
        self.psO = Ring(kb, "psO", [128, 512], F32, 2, psum=True)
        self.psM = Ring(kb, "psM", [128, 512], F32, 2, psum=True)

        class _Cat:
            def __init__(self, rings):
                self.items = [it for r in rings for it in r.items]
                self.i = 0

            def next(self):
                it = self.items[self.i % len(self.items)]
                self.i += 1
                return it
        self.psA = _Cat([self.psM, self.psS])

    def load_weight(self, kb, wt, wbuf, src2d, col_pairs, gvec=None, stg_r=None, CH=2048):
        nk = wt.shape[1]
        if stg_r is None:
            stg_r = Ring(kb, "wstg_" + wbuf.name, [128, CH], F32, 3)
        gt = gb = None
        if gvec is not None:
            gt, gb = kb.sb("g_" + wbuf.name, [128, nk], F32)
            kb.load(gt[:], gb, gvec.rearrange("(k p) -> p k", p=128), nonc=True)
        i = 0
        for k in range(nk):
            for (dc, sc, n) in col_pairs:
                for c0 in range(0, n, CH):
                    cn = min(CH, n - c0)
                    st, bst = stg_r.next()
                    kb.load(st[:, 0:cn], bst, src2d[k * 128:(k + 1) * 128, sc + c0:sc + c0 + cn])
                    dst = wt[:, k, dc + c0:dc + c0 + cn]
                    if i % 2 == 0:
                        if gt is not None:
                            kb.op("dve", lambda e: e.tensor_scalar(dst, st[:, 0:cn], gt[:, k:k + 1], None, ALU.mult), reads=[bst, gb], writes=[wbuf])
                        else:
                            kb.op("dve", lambda e: e.tensor_copy(dst, st[:, 0:cn]), reads=[bst], writes=[wbuf])
                    else:
                        if gt is not None:
                            kb.op("act", lambda e: e.activation(out=dst, in_=st[:, 0:cn], func=AF.Copy, scale=gt[:, k:k + 1]), reads=[bst, gb], writes=[wbuf])
                        else:
                            kb.op("act", lambda e: e.copy(dst, st[:, 0:cn]), reads=[bst], writes=[wbuf])
                    i += 1

    def build_rot(self, kb, wt, wbuf, ranges):
        nk = wt.shape[1]
        for (dc, sc, nb) in ranges:
            for k in range(nk):
                dv = wt[:, k, dc:dc + 64 * nb].rearrange("p (b h j) -> p b h j", h=2, j=32)
                sv = wt[:, k, sc:sc + 64 * nb].rearrange("p (b h j) -> p b h j", h=2, j=32)
                kb.op("pool", lambda e, dv=dv, sv=sv: e.tensor_scalar(dv[:, :, 0, :], sv[:, :, 1, :], -1.0, None, ALU.mult),
                      reads=[wbuf], writes=[wbuf])
                kb.op("pool", lambda e, dv=dv, sv=sv: e.tensor_copy(dv[:, :, 1, :], sv[:, :, 0, :]),
                      reads=[wbuf], writes=[wbuf])

    def norm_chunk(self, kb, xT, bx, hT, bh, sq, bsq, rstd, brstd, ntok):
        kb.op("act", lambda e: e.activation(out=sq[:, :, 0:ntok], in_=xT[:, :, 0:ntok], func=AF.Square),
              reads=[bx], writes=[bsq])
        pm, bpm = self.psM.next()
        for k in range(8):
            kb.op("pe", lambda e, k=k: e.matmul(pm[:, 0:ntok], self.onesb[:], sq[:, k, 0:ntok], start=(k == 0), stop=(k == 7)),
                  reads=[bsq, self.b_onesb], writes=[bpm], inc=(k == 7))
        kb.op("act", lambda e: e.activation(out=rstd[:, 0:ntok], in_=pm[:, 0:ntok], func=AF.Ln, bias=self.epsb[:, 0:1], scale=1.0 / D),
              reads=[bpm, self.b_eps], writes=[brstd])
        kb.op("act", lambda e: e.activation(out=rstd[:, 0:ntok], in_=rstd[:, 0:ntok], func=AF.Exp, scale=-0.5),
              reads=[brstd], writes=[brstd])
        if hT is not None:
            kb.op("dve", lambda e: e.tensor_tensor(hT[:, :, 0:ntok], xT[:, :, 0:ntok],
                                                   rstd[:, 0:ntok].unsqueeze(1).to_broadcast([128, 8, ntok]), ALU.mult),
                  reads=[bx, brstd], writes=[bh])

    def ph_inproj0(self, kb):
        I = self.I
        NC0 = 3488
        wt, bw = kb.sb("w0", [128, 8, NC0], BF16)
        w2d = I["w_in_e"][0]
        self.load_weight(kb, wt, bw, w2d,
                         [(0, 0, 896), (896, 1024, 128), (1024, 1304, 640), (1664, 1280, 24),
                          (1696, 896, 128), (1824, 1152, 128), (1952, 1944, 128)], gvec=I["attn_norm"][0])
        self.build_rot(kb, wt, bw, [(2080, 0, 8), (2592, 768, 4), (2848, 1024, 8), (3360, 1536, 2)])
        xin_r = Ring(kb, "xin", [128, 4, D], F32, 2)
        xT_r = Ring(kb, "xT", [128, 8, TC], F32, 2)
        hT_r = Ring(kb, "hT", [128, 8, TC], BF16, 2)
        sq_r = Ring(kb, "sq", [128, 8, TC], BF16, 1)
        rs_r = Ring(kb, "rs", [128, TC], F32, 2)
        cs_r = Ring(kb, "cs", [128, 2, TC], F32, 2)
        t1_r = Ring(kb, "t1", [128, TC], F32, 2)
        t2_r = Ring(kb, "t2", [128, TC], F32, 2)
        st_r = Ring(kb, "stg", [128, TC], BF16, 4)
        sv_r = Ring(kb, "stv", [128, 384], BF16, 2)
        sg_r = Ring(kb, "stgt", [24, TC], F32, 2)
        tiles = []
        for p in range(4):
            tiles.append((128 * p, 2080 + 128 * p, 128, QR + 128 * p, QP + 128 * p))
        tiles.append((512, None, 128, KC, None))
        tiles.append((640, None, 128, VC, None))
        tiles.append((768, 2592, 128, KS, None))
        tiles.append((896, 2720, 128, KW, None))
        for p in range(4):
            tiles.append((1024 + 128 * p, 2848 + 128 * p, 128, BQ + 128 * p, None))
        tiles.append((1536, 3360, 128, BK, None))

        def mm_group(ps, bps, hT, bh, col, rows):
            for k in range(8):
                kb.op("pe", lambda e, k=k: e.matmul(ps[0:rows, :], wt[:, k, col:col + rows], hT[:, k, :],
                                                    start=(k == 0), stop=(k == 7)),
                      reads=[bw, bh], writes=[bps], inc=(k == 7))

        chunks = [(s, c) for s in range(NSEQ) for c in range(NCH) if not (self.lim is not None and s * NCH + c >= self.lim)]

        def load_norm(i):
            s, c = chunks[i]
            t0 = c * TC
            xin, bxin = xin_r.next()
            kb.load(xin[:], bxin, I["x"][s, t0:t0 + TC, :].rearrange("(j p) f -> p j f", p=128))
            cs, bcs = cs_r.next()
            kb.load(cs[:, 0, :], bcs, I["c_cos"][:, t0:t0 + TC])
            kb.load(cs[:, 1, :], bcs, I["c_sin"][:, t0:t0 + TC])
            xT, bxT = xT_r.next()
            for k in range(8):
                pm, bpm = self.psA.next()
                for j in range(4):
                    kb.op("pe", lambda e, k=k, j=j, pm=pm: e.transpose(pm[:, j * 128:(j + 1) * 128], xin[:, j, k * 128:(k + 1) * 128], self.identf[:]),
                          reads=[bxin, self.b_identf], writes=[bpm], inc=(j == 3))
                kb.op("act", lambda e, k=k, pm=pm: e.copy(xT[:, k, :], pm[:]), reads=[bpm], writes=[bxT])
            for k in range(8):
                kb.store(self.XT[s, k * 128:(k + 1) * 128, t0:t0 + TC], [kb.db("XT", s, c)], xT[:, k, :], bxT)
            hT, bh = hT_r.next()
            sq, bsq = sq_r.next()
            rs, brs = rs_r.next()
            self.norm_chunk(kb, xT, bxT, hT, bh, sq, bsq, rs, brs, TC)
            return (s, c, t0, hT, bh, cs, bcs)

        nxt = load_norm(0) if chunks else None
        for i in range(len(chunks)):
            s, c, t0, hT, bh, cs, bcs = nxt
            for ti, (col, rcol, rows, zrow, prow) in enumerate(tiles):
                if ti == 8 and i + 1 < len(chunks):
                    nxt = load_norm(i + 1)
                pa, bpa = self.psA.next()
                mm_group(pa, bpa, hT, bh, col, rows)
                if rcol is None:
                    stg, bst = st_r.next()
                    kb.op("act", lambda e, stg=stg, pa=pa: e.copy(stg[:], pa[:]), reads=[bpa], writes=[bst])
                    kb.store(self.ZT0[s, zrow:zrow + 128, t0:t0 + TC], [kb.db("ZT0", s, zrow, c)], stg[:], bst)
                    continue
                pb, bpb = self.psA.next()
                mm_group(pb, bpb, hT, bh, rcol, rows)
                if prow is not None:
                    stg, bst = st_r.next()
                    kb.op("act", lambda e, stg=stg, pa=pa: e.copy(stg[:], pa[:]), reads=[bpa], writes=[bst])
                    kb.store(self.ZT0[s, prow:prow + 128, t0:t0 + TC], [kb.db("ZT0", s, prow, c)], stg[:], bst)
                t1, bt1 = t1_r.next()
                t2, bt2 = t2_r.next()
                kb.op("dve", lambda e, t1=t1, pa=pa, cs=cs: e.tensor_tensor(t1[:], pa[:], cs[:, 0, :], ALU.mult),
                      reads=[bpa, bcs], writes=[bt1])
                kb.op("dve", lambda e, t2=t2, pb=pb, cs=cs: e.tensor_tensor(t2[:], pb[:], cs[:, 1, :], ALU.mult),
                      reads=[bpb, bcs], writes=[bt2])
                stg, bst = st_r.next()
                kb.op("pool", lambda e, stg=stg, t1=t1, t2=t2: e.tensor_tensor(stg[:], t1[:], t2[:], ALU.add),
                      reads=[bt1, bt2], writes=[bst])
                kb.store(self.ZT0[s, zrow:zrow + 128, t0:t0 + TC], [kb.db("ZT0", s, zrow, c)], stg[:], bst)
            pa, bpa = self.psA.next()
            mm_group(pa, bpa, hT, bh, 1664, 24)
            sg, bsg = sg_r.next()
            kb.op("act", lambda e, sg=sg, pa=pa: e.activation(out=sg[:], in_=pa[0:24, :], func=AF.Sigmoid),
                  reads=[bpa], writes=[bsg])
            kb.store(self.GT[s, :, t0:t0 + TC], [kb.db("GT", s, c)], sg[:], bsg)
            for j in range(4):
                pa, bpa = self.psA.next()
                for k in range(8):
                    kb.op("pe", lambda e, k=k, j=j, pa=pa: e.matmul(pa[:, 0:384], hT[:, k, j * 128:(j + 1) * 128], wt[:, k, 1696:2080],
                                                                    start=(k == 0), stop=(k == 7)),
                          reads=[bw, bh], writes=[bpa], inc=(k == 7))
                sv, bsv = sv_r.next()
                kb.op("act", lambda e, sv=sv, pa=pa: e.copy(sv[:], pa[:, 0:384]), reads=[bpa], writes=[bsv])
                kb.store(self.VT0[s, t0 + j * 128:t0 + (j + 1) * 128, :], [kb.db("VT0", s, c)], sv[:], bsv)

    def attn_setup(self, kb):
        self.pt_r = Ring(kb, "pt", [128, 512], BF16, 4)

    def attend(self, kb, qtiles, kdim, q_bufs, post, pair_kind=None, look=3):
        banks = []
        cur = []
        for u, unit in enumerate(qtiles):
            for j, (qap, blocks) in enumerate(unit):
                nb = len(blocks)
                for bi, (lh, mk, va) in enumerate(blocks):
                    cur.append((u, j, bi, nb, qap, lh, mk, va, j == len(unit) - 1 and bi == nb - 1))
                    if len(cur) == 4:
                        banks.append(cur)
                        cur = []
        if cur:
            banks.append(cur)
        state = {"ob": None, "u": -1, "first": True}
        pending_posts = []

        def qk(bank):
            ps, bps = self.psS.next()
            n = len(bank)
            first = True
            if pair_kind is not None and n == 4:
                v = (1 if bank[0][6] == 3 else 0) + (2 if bank[2][6] == 3 else 0)
                kb.op("pe", lambda e: e.matmul(ps[:], self.identb[:], self.cmask[:, 4 * pair_kind + v, :], start=True, stop=False, skip_group_check=True),
                      reads=[self.b_identb, self.b_cmask], writes=[bps], inc=False)
                first = False
            else:
                for si, (u, j, bi, nb, qap, lh, mk, va, last) in enumerate(bank):
                    if mk is not None:
                        sl = ps[:, si * 128:(si + 1) * 128]
                        kb.op("pe", lambda e: e.matmul(sl, self.identb[:], self.masks[:, mk, :], start=first, stop=False, skip_group_check=True),
                              reads=[self.b_identb, self.b_masks], writes=[bps], inc=False)
                        first = False
            for si, (u, j, bi, nb, qap, lh, mk, va, last) in enumerate(bank):
                sl = ps[:, si * 128:(si + 1) * 128]
                kb.op("pe", lambda e: e.matmul(sl, lh, qap, start=first, stop=(si == n - 1), skip_group_check=True),
                      reads=q_bufs, writes=[bps], inc=(si == n - 1))
                first = False
            return ps, bps

        def ex_pv(bank, ps, bps):
            n = len(bank)
            pt, bpt = self.pt_r.next()
            kb.op("act", lambda e: e.activation(out=pt[:, 0:n * 128], in_=ps[:, 0:n * 128], func=AF.Exp, scale=SCALE),
                  reads=[bps], writes=[bpt])
            while pending_posts:
                pending_posts.pop(0)()
            for si, (u, j, bi, nb, qap, lh, mk, va, last) in enumerate(bank):
                if u != state["u"]:
                    state["ob"] = self.psO.next()
                    state["u"] = u
                    state["first"] = True
                ob, bob = state["ob"]
                psl = pt[:, si * 128:(si + 1) * 128]
                kb.op("pe", lambda e: e.matmul(ob[:, j * 128:(j + 1) * 128], va, psl, start=state["first"], stop=last, skip_group_check=True),
                      reads=[bpt] + q_bufs, writes=[bob], inc=(si == n - 1 or last))
                state["first"] = False
                if last:
                    pending_posts.append(lambda u=u, ob=ob, bob=bob: post(u, ob, bob))

        pend = []
        for bank in banks:
            r = qk(bank)
            pend.append((bank, r[0], r[1]))
            if len(pend) > look:
                ex_pv(*pend.pop(0))
        while pend:
            ex_pv(*pend.pop(0))
        while pending_posts:
            pending_posts.pop(0)()

    def ph_attn0(self, kb):
        I = self.I
        nc = self.nc
        self.attn_setup(kb)
        stt, bstt = kb.sb("stt", [128, T], BF16)
        kb.load(stt[:], bstt, I["c_st"])
        ova, bova = kb.sb("ova", [128, 2, 65], BF16)
        kb.load(ova[:], bova, I["c_ova"])
        fb, bfb = kb.sb("fb", [128, 32, 64], F32)
        kb.load(fb[:], bfb, I["c_fb"])
        es_t, bes = kb.sb("esink", [128, 8], F32)
        kb.load(es_t[:], bes, I["sinks"][0].partition_broadcast(128))
        kb.op("act", lambda e: e.activation(out=es_t[:], in_=es_t[:], func=AF.Exp), reads=[bes], writes=[bes])
        acc = [kb.sb("acc%d" % g, [64, T], F32) for g in range(4)]
        G = [kb.sb("G%d" % i, [128, T], BF16) for i in range(4)]
        w1 = {"k": (acc[0][0][:].bitcast(BF16).rearrange("p (j m) -> p j m", m=256), acc[0][1]),
              "v": (acc[1][0][:].bitcast(BF16).rearrange("p (j m) -> p j m", m=256), acc[1][1])}
        w2 = {}
        pet = {}
        for nm in ("k", "v"):
            w2[nm] = kb.sb("cw2" + nm, [128, 2, 64], BF16)
            kb.load(w2[nm][0][:], w2[nm][1], I["cmp_%s_w2" % nm][0].rearrange("(c p) d -> p c d", p=128), eng="pool")
            pet[nm] = kb.sb("pet" + nm, [64, 32], BF16)
            kb.load(pet[nm][0][:], pet[nm][1], I["cmp_pe_" + nm][0].rearrange("j d -> d j"), eng="pool", nonc=True)
        cbias = {nm: kb.sb("cb" + nm, [128, 2], F32) for nm in ("k", "v")}

        def load_w1_and_bias(first):
            for nm in ("k", "v"):
                kb.load(w1[nm][0], w1[nm][1], I["cmp_%s_w1" % nm][0].rearrange("(j d) m -> d j m", d=64), eng="pool")
            if not first:
                return
            for nm in ("k", "v"):
                for mc in range(2):
                    pm, bpm = self.psM.next()
                    for j in range(32):
                        kb.op("pe", lambda e: e.matmul(pm[:, 0:1], w1[nm][0][:, j, mc * 128:(mc + 1) * 128], pet[nm][0][:, j:j + 1],
                                                       start=(j == 0), stop=(j == 31)),
                              reads=[w1[nm][1], pet[nm][1]], writes=[bpm], inc=(j == 31))
                    kb.op("act", lambda e: e.copy(cbias[nm][0][:, mc:mc + 1], pm[:, 0:1]), reads=[bpm], writes=[cbias[nm][1]])
        kin, bkin = kb.sb("kin", [64, T], BF16)
        hid, bhid = kb.sb("hid", [128, 2, 256], BF16)
        kcT = [kb.sb("kcT%d" % h, [128, 256], BF16) for h in range(2)]
        vca = [kb.sb("vca%d" % h, [128, 2, 128], BF16) for h in range(2)]
        qaug, bqaug = kb.sb("qaug", [128, 4, T], BF16)
        kaug, bkaug = G[0][0], G[0][1]
        kw, bkw = G[1][0][0:64, :], G[1][1]
        kwf = G[1][0]
        for gi_ in (1, 2, 3):
            kb.op("pool", lambda e: e.memset(G[gi_][0][64:128, :], 0.0), writes=[G[gi_][1]])
        vs, bvs = G[2][0][:].rearrange("p (t d) -> p t d", d=128), G[2][1]
        vw, bvw = G[3][0][:].rearrange("p (t d) -> p t d", d=128), G[3][1]
        gate_r = Ring(kb, "gate", [64, 4, 128], F32, 2)
        g1_r = Ring(kb, "gate1", [64, 512], F32, 2)
        d_r = Ring(kb, "dd", [64, 512], F32, 2)
        tmp_r = Ring(kb, "tmp", [64, 512], F32, 2)
        osb_r = Ring(kb, "osb", [64, 512], F32, 2)
        imp_r = Ring(kb, "impa", [128, 64], F32, 2)
        imp2_r = Ring(kb, "impb", [128, 64], F32, 2)
        m8_r = Ring(kb, "m8", [128, 16], F32, 2)
        rd_r = Ring(kb, "rd", [128, 4], F32, 2)
        selq_r = Ring(kb, "selq", [128, 64], BF16, 2)
        mix_r = Ring(kb, "mixs", [64, T], BF16, 1)
        for h in range(2):
            kb.op("pool", lambda e: e.memset(vca[h][0][:], 0.0), writes=[vca[h][1]])
            kb.op("pool", lambda e: e.memset(kcT[h][0][:], 0.0), writes=[kcT[h][1]])
        kb.op("pool", lambda e: e.memset(hid[:], 0.0), writes=[bhid])
        kb.load(kaug[64:128, :], bkaug, I["c_e"])

        AL = self.alim
        for s in range(NSEQ):
            if AL and s >= AL["nseq"]:
                continue
            zt = self.ZT0[s]
            zdeps = [kb.db("ZT0", s, r, c) for c in range(NCH) for r in (KC, VC)]
            load_w1_and_bias(s == 0)
            for nm, zrow in (("k", KC), ("v", VC)):
                for h in range(2):
                    kb.load(kin[:], bkin, zt[zrow + 64 * h:zrow + 64 * h + 64, :], sbufs=zdeps)
                    for mc in range(2):
                        pm, bpm = self.psM.next()
                        for j in range(32):
                            kb.op("pe", lambda e: e.matmul(pm[:, 0:255], w1[nm][0][:, j, mc * 128:(mc + 1) * 128],
                                                           kin[:, j:j + 16 * 254 + 1:16], start=(j == 0), stop=(j == 31)),
                                  reads=[w1[nm][1], bkin], writes=[bpm], inc=(j == 31))
                        kb.op("act", lambda e: e.activation(out=hid[:, mc, 0:255], in_=pm[:, 0:255], func=AF.Silu,
                                                            bias=cbias[nm][0][:, mc:mc + 1]),
                              reads=[bpm, cbias[nm][1]], writes=[bhid])
                    if nm == "k":
                        pm, bpm = self.psM.next()
                        for mc in range(2):
                            kb.op("pe", lambda e: e.matmul(pm[0:64, 0:255], w2["k"][0][:, mc, :], hid[:, mc, 0:255],
                                                           start=(mc == 0), stop=(mc == 1)),
                                  reads=[w2["k"][1], bhid], writes=[bpm], inc=(mc == 1))
                        kb.op("act", lambda e: e.copy(kcT[h][0][0:64, 0:255], pm[0:64, 0:255]), reads=[bpm], writes=[kcT[h][1]])
                    else:
                        for ct in range(2):
                            rows = 128 if ct == 0 else 127
                            pm, bpm = self.psM.next()
                            for mc in range(2):
                                kb.op("pe", lambda e: e.matmul(pm[0:rows, 0:64], hid[:, mc, ct * 128:ct * 128 + rows], w2["v"][0][:, mc, :],
                                                               start=(mc == 0), stop=(mc == 1)),
                                      reads=[w2["v"][1], bhid], writes=[bpm], inc=(mc == 1))
                            kb.op("act", lambda e: e.copy(vca[h][0][0:rows, ct, 0:64], pm[0:rows, 0:64]),
                                  reads=[bpm], writes=[vca[h][1]])
                            kb.op("pool", lambda e: e.memset(vca[h][0][0:rows, ct, 64:128], 1.0), writes=[vca[h][1]])
            zall = [kb.db("ZT0", s, r, c) for c in range(NCH) for r in (QP, QP + 128, QP + 256, QP + 384, QR, QR + 128, QR + 256, QR + 384, KS, KW)]
            vdeps = [kb.db("VT0", s, c) for c in range(NCH)]
            gdeps = [kb.db("GT", s, c) for c in range(NCH)]
            for h in range(2):
                if AL and h not in AL["hs"]:
                    continue
                for g in range(4):
                    hd = 4 * h + g
                    kb.load(qaug[0:64, g, :], bqaug, zt[QR + 64 * hd:QR + 64 * hd + 64, :], sbufs=zall)
                qps = []
                for g in range(4):
                    qp, bqp = G[g][0], G[g][1]
                    kb.load(qp[0:64, :], bqp, zt[QP + 64 * (4 * h + g):QP + 64 * (4 * h + g) + 64, :], sbufs=zall)
                    qps.append((qp, bqp))
                backs = []
                for qs in range(32):
                    ncts = 2 if qs >= 16 else 1
                    ob, bob = self.psO.next()
                    pi, bpi = self.psM.next()
                    pts = []
                    for ct in range(ncts):
                        ps, bps = self.psS.next()
                        off = 128 * qs - 2048 * ct
                        kb.op("pe", lambda e: e.matmul(ps[:], self.identb[:], stt[:, off:off + 128].unsqueeze(1).to_broadcast([128, 4, 128]),
                                                       start=True, stop=False, skip_group_check=True),
                              reads=[self.b_identb, bstt], writes=[bps], inc=False)
                        for g in range(4):
                            kb.op("pe", lambda e: e.matmul(ps[:, g * 128:(g + 1) * 128], kcT[h][0][:, ct * 128:(ct + 1) * 128],
                                                           qps[g][0][:, qs * 128:(qs + 1) * 128], start=False, stop=(g == 3), skip_group_check=True),
                                  reads=[kcT[h][1], qps[g][1]], writes=[bps], inc=(g == 3))
                        pt, bpt = self.pt_r.next()
                        kb.op("act", lambda e: e.activation(out=pt[:], in_=ps[:], func=AF.Exp, scale=SCALE),
                              reads=[bps], writes=[bpt])
                        pts.append((pt, bpt))
                    for g in range(4):
                        for ct in range(ncts):
                            pt, bpt = pts[ct]
                            kb.op("pe", lambda e: e.matmul(ob[:, g * 128:(g + 1) * 128], vca[h][0][:, ct, :], pt[:, g * 128:(g + 1) * 128],
                                                           start=(ct == 0), stop=(ct == ncts - 1)),
                                  reads=[bpt, vca[h][1]], writes=[bob], inc=False)
                    for g in range(4):
                        for ct in range(ncts):
                            pt, bpt = pts[ct]
                            kb.op("pe", lambda e: e.matmul(pi[:, g * 65:(g + 1) * 65], pt[:, g * 128:(g + 1) * 128], ova[:, ct, :],
                                                           start=(ct == 0), stop=(ct == ncts - 1)),
                                  reads=[bpt, bova], writes=[bpi, bob], inc=(g == 3 and ct == ncts - 1))
                    def back(qs=qs, ob=ob, bob=bob, pi=pi, bpi=bpi):
                        dd, bdd = d_r.next()
                        obv = ob[:].rearrange("p (g c) -> p g c", g=4)
                        ddv = dd[:].rearrange("p (g c) -> p g c", g=4)
                        kb.op("dve", lambda e: e.tensor_scalar(ddv, obv[64:128], 1e-30, None, ALU.max), reads=[bob], writes=[bdd])
                        yield
                        osb, bosb = osb_r.next()
                        kb.op("act", lambda e: e.copy(osb[:], ob[0:64, :]), reads=[bob], writes=[bosb])
                        yield
                        osv = osb[:].rearrange("p (g c) -> p g c", g=4)
                        rd, brd = rd_r.next()
                        kb.op("dve", lambda e: e.tensor_scalar(rd[:], pi[:, 64:260:65], 1e-30, None, ALU.max), reads=[bpi], writes=[brd])
                        yield
                        kb.op("dve", lambda e: e.reciprocal(rd[:], rd[:]), reads=[brd], writes=[brd])
                        yield
                        ia, bia = imp_r.next()
                        for g in range(4):
                            in1 = fb[:, qs, :] if g == 0 else ia[:]
                            kb.op("dve", lambda e: e.scalar_tensor_tensor(ia[:], pi[:, g * 65:g * 65 + 64], rd[:, g:g + 1], in1, ALU.mult, ALU.add),
                                  reads=[bpi, brd, bfb, bia], writes=[bia])
                            yield
                        m8, bm8 = m8_r.next()
                        ib, bib = imp2_r.next()
                        kb.op("dve", lambda e: e.max(out=m8[:, 0:8], in_=ia[:]), reads=[bia], writes=[bm8])
                        yield
                        kb.op("dve", lambda e: e.match_replace(out=ib[:], in_to_replace=m8[:, 0:8], in_values=ia[:], imm_value=-1e30),
                              reads=[bia, bm8], writes=[bib])
                        yield
                        kb.op("dve", lambda e: e.max(out=m8[:, 8:16], in_=ib[:]), reads=[bib], writes=[bm8])
                        yield
                        sq_, bsq_ = selq_r.next()
                        kb.op("dve", lambda e: e.tensor_scalar(sq_[:], ia[:], m8[:, 15:16], None, ALU.is_ge),
                              reads=[bia, bm8], writes=[bsq_])
                        yield
                        ptr, bptr = self.psM.next()
                        ptv = ptr[:].bitcast(BF16)
                        kb.op("pe", lambda e: e.transpose(ptv[0:64, 0:128], sq_[:], self.identb[:]),
                              reads=[bsq_, self.b_identb], writes=[bptr])
                        yield
                        kb.op("dve", lambda e: e.tensor_scalar(qaug[64:128, :, qs * 128:(qs + 1) * 128],
                                                               ptv[0:64, 0:128].unsqueeze(1).to_broadcast([64, 4, 128]), -1.0, None, ALU.add),
                              reads=[bptr], writes=[bqaug])
                        yield
                        gt, bgt = gate_r.next()
                        for g in range(4):
                            kb.load(gt[:, g, :], bgt, self.GT[s, 3 * (4 * h + g), qs * 128:(qs + 1) * 128].partition_broadcast(64), sbufs=gdeps)
                        kb.op("act", lambda e: e.activation(out=dd[:], in_=dd[:], func=AF.Ln), reads=[bdd], writes=[bdd])
                        yield
                        kb.op("act", lambda e: e.activation(out=dd[:], in_=dd[:], func=AF.Exp, scale=-1.0), reads=[bdd], writes=[bdd])
                        yield
                        kb.op("pool", lambda e: e.tensor_tensor(ddv, gt[:], ddv, ALU.mult), reads=[bdd, bgt], writes=[bdd])
                        yield
                        for g in range(4):
                            kb.op("dve", lambda e: e.tensor_tensor(acc[g][0][:, qs * 128:(qs + 1) * 128], osv[:, g, :], ddv[:, g, :], ALU.mult),
                                  reads=[bosb, bdd], writes=[acc[g][1]])
                            yield

                    backs.append(back())
                    if len(backs) == 2:
                        live = list(backs)
                        backs = []
                        while live:
                            for gen in list(live):
                                try:
                                    next(gen)
                                except StopIteration:
                                    live.remove(gen)
                kb.load(kaug[0:64, :], bkaug, zt[KS + 64 * h:KS + 64 * h + 64, :], sbufs=zall)
                kb.load(kw, bkw, zt[KW + 64 * h:KW + 64 * h + 64, :], sbufs=zall)
                kb.op("pool", lambda e: e.memset(vs[:, :, 64:128], 1.0), writes=[bvs])
                kb.op("pool", lambda e: e.memset(vw[:, :, 64:128], 1.0), writes=[bvw])
                kb.load(vs[:, :, 0:64], bvs, self.VT0[s][:, 64 * h:64 * h + 64].rearrange("(t p) d -> p t d", p=128), sbufs=vdeps, nonc=True)
                kb.load(vw[:, :, 0:64], bvw, self.VT0[s][:, 128 + 64 * h:128 + 64 * h + 64].rearrange("(t p) d -> p t d", p=128), sbufs=vdeps, nonc=True)
                for g in range(4):
                    hd = 4 * h + g
                    if AL and g not in AL["gs"]:
                        continue
                    for br, gi in (("sel", 1), ("win", 2)):
                        units = []
                        for u in range(8):
                            unit = []
                            for j in range(4):
                                qs = 4 * u + j
                                if br == "sel":
                                    qap = qaug[:, g, qs * 128:(qs + 1) * 128]
                                    blocks = [(kaug[:, kt * 128:(kt + 1) * 128], (0 if kt == qs else None), vs[:, kt, :]) for kt in range(qs + 1)]
                                else:
                                    qap = qaug[:, g, qs * 128:(qs + 1) * 128]
                                    blocks = []
                                    for kt in range(max(0, qs - 4), qs + 1):
                                        mk = 0 if kt == qs else (1 if kt == qs - 4 else None)
                                        blocks.append((kwf[:, kt * 128:(kt + 1) * 128], mk, vw[:, kt, :]))
                                unit.append((qap, blocks))
                            units.append(unit)

                        def post(u, ob, bob, g=g, gi=gi, hd=hd):
                            gt, bgt = g1_r.next()
                            kb.load(gt[:], bgt, self.GT[s, 3 * hd + gi, u * 512:(u + 1) * 512].partition_broadcast(64), sbufs=gdeps)
                            dd, bdd = d_r.next()
                            kb.op("dve", lambda e: e.tensor_scalar(dd[:], ob[64:128, :], 1e-30, None, ALU.max), reads=[bob], writes=[bdd])
                            kb.op("act", lambda e: e.activation(out=dd[:], in_=dd[:], func=AF.Ln), reads=[bdd], writes=[bdd])
                            kb.op("act", lambda e: e.activation(out=dd[:], in_=dd[:], func=AF.Exp, scale=-1.0), reads=[bdd], writes=[bdd])
                            kb.op("pool", lambda e: e.tensor_tensor(dd[:], gt[:], dd[:], ALU.mult), reads=[bdd, bgt], writes=[bdd])
                            tm, btm = tmp_r.next()
                            kb.op("dve", lambda e: e.tensor_tensor(tm[:], ob[0:64, :], dd[:], ALU.mult), reads=[bob, bdd], writes=[btm])
                            kb.op("pool", lambda e: e.tensor_tensor(acc[g][0][:, u * 512:(u + 1) * 512], acc[g][0][:, u * 512:(u + 1) * 512], tm[:], ALU.add),
                                  reads=[btm, acc[g][1]], writes=[acc[g][1]])

                        self.attend(kb, units, 128 if br == "sel" else 64, [bqaug, bkaug, bkw, bvs, bvw], post)
                    mx, bmx = mix_r.next()
                    kb.op("act", lambda e: e.copy(mx[:], acc[g][0][:]), reads=[acc[g][1]], writes=[bmx])
                    kb.store(self.MIXT[s, 64 * hd:64 * hd + 64, :], [kb.db("MIXT", s, hd)], mx[:], bmx)
            zb = [kb.db("ZT0", s, r, c) for c in range(NCH) for r in (BQ, BQ + 128, BQ + 256, BQ + 384, BK)]
            for hb in range(2):
                kb.load(kw, bkw, zt[BK + 64 * hb:BK + 64 * hb + 64, :], sbufs=zb)
                kb.load(vw[:, :, 0:64], bvw, self.VT0[s][:, 256 + 64 * hb:256 + 64 * hb + 64].rearrange("(t p) d -> p t d", p=128), sbufs=vdeps, nonc=True)
                for g in range(4):
                    hd = 4 * hb + g
                    if AL and hd not in AL["bh"]:
                        continue
                    Gq = G[0] if g % 2 == 0 else G[2]
                    bq, bbq = Gq[0], Gq[1]
                    kb.load(bq[0:64, :], bbq, zt[BQ + 64 * hd:BQ + 64 * hd + 64, :], sbufs=zb)
                    mx, bmx = mix_r.next()
                    units = []
                    for u in range(8):
                        unit = []
                        for j in range(4):
                            qs = 4 * u + j
                            blocks = []
                            if qs > 0:
                                blocks.append((kwf[:, (qs - 1) * 128:qs * 128], 1, vw[:, qs - 1, :]))
                            else:
                                blocks.append((kwf[:, 0:128], 3, vw[:, 0, :]))
                            blocks.append((kwf[:, qs * 128:(qs + 1) * 128], 0, vw[:, qs, :]))
                            unit.append((bq[:, qs * 128:(qs + 1) * 128], blocks))
                        units.append(unit)

                    def post(u, ob, bob, hd=hd, mx=mx, bmx=bmx):
                        dd, bdd = d_r.next()
                        kb.op("dve", lambda e: e.tensor_scalar(dd[:], ob[64:128, :], es_t[0:64, hd:hd + 1], None, ALU.add), reads=[bob, bes], writes=[bdd])
                        kb.op("act", lambda e: e.activation(out=dd[:], in_=dd[:], func=AF.Ln), reads=[bdd], writes=[bdd])
                        kb.op("act", lambda e: e.activation(out=dd[:], in_=dd[:], func=AF.Exp, scale=-1.0), reads=[bdd], writes=[bdd])
                        kb.op("dve", lambda e: e.tensor_tensor(mx[:, u * 512:(u + 1) * 512], ob[0:64, :], dd[:], ALU.mult), reads=[bob, bdd], writes=[bmx])

                    self.attend(kb, units, 64, [bbq, bkw, bvw], post, pair_kind=1)
                    kb.store(self.MIXT[s, 512 + 64 * hd:512 + 64 * hd + 64, :], [kb.db("MIXT", s, 8 + hd)], mx[:], bmx)

    def ph_outproj(self, layer):
        def run(kb):
            I = self.I
            wt, bw = kb.sb("wo", [128, 8, D], BF16)
            w2d = I["w_out_e"][0] if layer == 0 else I["w_out_o"][0]
            self.load_weight(kb, wt, bw, w2d, [(0, 0, D)])
            mx_r = Ring(kb, "mxc", [128, 8, TC], BF16, 2)
            xT_r = Ring(kb, "xTo", [128, 8, TC], F32, 2)
            for s in range(NSEQ):
                mdeps = [kb.db("MIXT", s, hd) for hd in range(16)]
                for c in range(NCH):
                    if self.lim is not None and s * NCH + c >= self.lim:
                        continue
                    t0 = c * TC
                    mx, bmx = mx_r.next()
                    kb.load(mx[:], bmx, self.MIXT[s, :, t0:t0 + TC].rearrange("(k p) t -> p k t", p=128), sbufs=mdeps)
                    xT, bxT = xT_r.next()
                    kb.load(xT[:], bxT, self.XT[s, :, t0:t0 + TC].rearrange("(k p) t -> p k t", p=128), sbufs=[kb.db("XT", s, c)])
                    for n in range(8):
                        pm, bpm = self.psA.next()
                        for k in range(8):
                            kb.op("pe", lambda e, k=k, n=n, pm=pm, mx=mx: e.matmul(pm[:], wt[:, k, n * 128:(n + 1) * 128], mx[:, k, :], start=(k == 0), stop=(k == 7)),
                                  reads=[bw, bmx], writes=[bpm], inc=(k == 7))
                        kb.op("dve", lambda e, n=n, pm=pm, xT=xT: e.tensor_tensor(xT[:, n, :], pm[:], xT[:, n, :], ALU.add), reads=[bpm, bxT], writes=[bxT])
                    kb.store(self.XT[s, :, t0:t0 + TC].rearrange("(k p) t -> p k t", p=128), [kb.db("XT", s, c)], xT[:], bxT)
        return run

    def ph_ffn(self, layer):
        def run(kb):
            I = self.I
            FT = 256
            wgu, bwgu = kb.sb("wgu", [128, 8, 2 * FH], BF16)
            stg_r = Ring(kb, "wstgf", [128, 1408], F32, 2)
            wd, bwd = kb.sb("wd", [128, 22, D], BF16)
            xT_r = Ring(kb, "xTf", [128, 8, FT], F32, 2)
            hT_r = Ring(kb, "hTf", [128, 8, FT], BF16, 2)
            sq_r = Ring(kb, "sqf", [128, 8, FT], BF16, 1)
            rs_r = Ring(kb, "rsf", [128, FT], F32, 1)
            act_r = Ring(kb, "actf", [128, 22, FT], BF16, 1)
            sg_r = Ring(kb, "sgf", [128, FT], F32, 3)
            chunks = [(s, c) for s in range(NSEQ) for c in range(T // FT)
                      if not (self.lim is not None and s * (T // FT) + c >= 2 * self.lim)]

            def load_norm(i):
                s, c = chunks[i]
                t0 = c * FT
                xT, bxT = xT_r.next()
                kb.load(xT[:], bxT, self.XT[s, :, t0:t0 + FT].rearrange("(k p) t -> p k t", p=128), sbufs=[kb.db("XT", s, t0 // TC)])
                hT, bh = hT_r.next()
                sq, bsq = sq_r.next()
                rs, brs = rs_r.next()
                self.norm_chunk(kb, xT, bxT, hT, bh, sq, bsq, rs, brs, FT)
                return (s, t0, xT, bxT, hT, bh)

            nxt = load_norm(0) if chunks else None
            self.load_weight(kb, wgu, bwgu, I["w_gate_up"][layer], [(0, 0, 2 * FH)], gvec=I["ffn_norm"][layer], stg_r=stg_r, CH=1408)
            for i in range(len(chunks)):
                s, t0, xT, bxT, hT, bh = nxt
                act, bact = act_r.next()
                for n in range(22):
                    pg, bpg = self.psM.next()
                    pu, bpu = self.psS.next()
                    for k in range(8):
                        kb.op("pe", lambda e: e.matmul(pg[:, 0:FT], wgu[:, k, n * 128:(n + 1) * 128], hT[:, k, :], start=(k == 0), stop=(k == 7)),
                              reads=[bwgu, bh], writes=[bpg], inc=(k == 7))
                    for k in range(8):
                        kb.op("pe", lambda e: e.matmul(pu[:, 0:FT], wgu[:, k, FH + n * 128:FH + (n + 1) * 128], hT[:, k, :], start=(k == 0), stop=(k == 7)),
                              reads=[bwgu, bh], writes=[bpu], inc=(k == 7))
                    sg, bsg = sg_r.next()
                    kb.op("act", lambda e: e.activation(out=sg[:], in_=pg[:, 0:FT], func=AF.Silu), reads=[bpg], writes=[bsg])
                    kb.op("dve", lambda e: e.tensor_tensor(act[:, n, :], pu[:, 0:FT], sg[:], ALU.mult), reads=[bpu, bsg], writes=[bact])
                if i == 0:
                    self.load_weight(kb, wd, bwd, I["w_down"][layer], [(0, 0, D)], stg_r=stg_r, CH=1408)
                if i + 1 < len(chunks):
                    nxt = load_norm(i + 1)
                for n in range(8):
                    pm, bpm = self.psO.next()
                    for k in range(22):
                        kb.op("pe", lambda e: e.matmul(pm[:, 0:FT], wd[:, k, n * 128:(n + 1) * 128], act[:, k, :], start=(k == 0), stop=(k == 21)),
                              reads=[bwd, bact], writes=[bpm], inc=(k == 21))
                    kb.op("dve", lambda e: e.tensor_tensor(xT[:, n, :], pm[:, 0:FT], xT[:, n, :], ALU.add), reads=[bpm, bxT], writes=[bxT])
                kb.store(self.XT[s, :, t0:t0 + FT].rearrange("(k p) t -> p k t", p=128), [kb.db("XT", s, t0 // TC)], xT[:], bxT)
        return run

    def ph_inproj1(self, kb):
        I = self.I
        NC1 = 3072 + 2048
        wt, bw = kb.sb("w1", [128, 8, NC1], BF16)
        self.load_weight(kb, wt, bw, I["w_qkv_o"][0], [(0, 0, 3072)], gvec=I["attn_norm"][1])
        self.build_rot(kb, wt, bw, [(3072, 0, 32)])
        xT_r = Ring(kb, "xT1", [128, 8, TC], F32, 2)
        hT_r = Ring(kb, "hT1", [128, 8, TC], BF16, 2)
        sq_r = Ring(kb, "sq1", [128, 8, TC], BF16, 1)
        rs_r = Ring(kb, "rs1", [128, TC], F32, 2)
        cs_r = Ring(kb, "cs1", [128, 2, TC], F32, 2)
        t1_r = Ring(kb, "t11", [128, TC], F32, 2)
        t2_r = Ring(kb, "t21", [128, TC], F32, 2)
        st_r = Ring(kb, "stg1", [128, TC], BF16, 4)
        sv_r = Ring(kb, "stv1", [128, D], BF16, 2)
        chunks = [(s, c) for s in range(NSEQ) for c in range(NCH) if not (self.lim is not None and s * NCH + c >= self.lim)]

        def load_norm(i):
            s, c = chunks[i]
            t0 = c * TC
            xT, bxT = xT_r.next()
            kb.load(xT[:], bxT, self.XT[s, :, t0:t0 + TC].rearrange("(k p) t -> p k t", p=128), sbufs=[kb.db("XT", s, c)])
            cs, bcs = cs_r.next()
            kb.load(cs[:, 0, :], bcs, I["c_cos"][:, t0:t0 + TC])
            kb.load(cs[:, 1, :], bcs, I["c_sin"][:, t0:t0 + TC])
            hT, bh = hT_r.next()
            sq, bsq = sq_r.next()
            rs, brs = rs_r.next()
            self.norm_chunk(kb, xT, bxT, hT, bh, sq, bsq, rs, brs, TC)
            return (s, c, t0, hT, bh, cs, bcs)

        nxt = load_norm(0) if chunks else None
        for i in range(len(chunks)):
            s, c, t0, hT, bh, cs, bcs = nxt
            for p in range(16):
                if p == 10 and i + 1 < len(chunks):
                    nxt = load_norm(i + 1)
                col = 128 * p
                pa, bpa = self.psA.next()
                pb, bpb = self.psA.next()
                for (pp, bpp, cc) in ((pa, bpa, col), (pb, bpb, 3072 + col)):
                    for k in range(8):
                        kb.op("pe", lambda e: e.matmul(pp[:], wt[:, k, cc:cc + 128], hT[:, k, :], start=(k == 0), stop=(k == 7)),
                              reads=[bw, bh], writes=[bpp], inc=(k == 7))
                t1, bt1 = t1_r.next()
                t2, bt2 = t2_r.next()
                kb.op("dve", lambda e: e.tensor_tensor(t1[:], pa[:], cs[:, 0, :], ALU.mult), reads=[bpa, bcs], writes=[bt1])
                kb.op("dve", lambda e: e.tensor_tensor(t2[:], pb[:], cs[:, 1, :], ALU.mult), reads=[bpb, bcs], writes=[bt2])
                stg, bst = st_r.next()
                kb.op("pool", lambda e: e.tensor_tensor(stg[:], t1[:], t2[:], ALU.add), reads=[bt1, bt2], writes=[bst])
                kb.store(self.ZT1[s, col:col + 128, t0:t0 + TC], [kb.db("ZT1", s, p, c)], stg[:], bst)
            for j in range(4):
                sv, bsv = sv_r.next()
                for half in range(2):
                    pa, bpa = self.psA.next()
                    for k in range(8):
                        kb.op("pe", lambda e: e.matmul(pa[:], hT[:, k, j * 128:(j + 1) * 128], wt[:, k, 2048 + 512 * half:2048 + 512 * (half + 1)],
                                                       start=(k == 0), stop=(k == 7)),
                              reads=[bw, bh], writes=[bpa], inc=(k == 7))
                    kb.op("act", lambda e: e.copy(sv[:, 512 * half:512 * (half + 1)], pa[:]), reads=[bpa], writes=[bsv])
                kb.store(self.VT1[s, t0 + j * 128:t0 + (j + 1) * 128, :], [kb.db("VT1", s, c)], sv[:], bsv)

    def ph_attn1(self, kb):
        nc = self.nc
        self.attn_setup(kb)
        q_r = Ring(kb, "q1", [128, T], BF16, 2)
        k_r = Ring(kb, "k1", [128, T], BF16, 2)
        for rr in (q_r, k_r):
            for (t_, b_) in rr.items:
                kb.op("pool", lambda e: e.memset(t_[64:128, :], 0.0), writes=[b_])
        va = [Ring(kb, "va%d" % p, [128, 32, 128], BF16, 2) for p in range(3)]
        for p in range(3):
            for (t, b) in va[p].items:
                kb.op("pool", lambda e, t=t: e.memset(t[:, :, 64:128], 1.0), writes=[b])
        acc_r = Ring(kb, "acc1", [128, T], F32, 2)
        mx_r = Ring(kb, "mix1", [64, T], BF16, 2)
        rec_r = Ring(kb, "rec1", [64, 512], F32, 3)
        pending_final = []
        dils = (1, 4, 16)
        for s in range(NSEQ):
            zdeps = [kb.db("ZT1", s, p, c) for p in range(16) for c in range(NCH)]
            vdeps = [kb.db("VT1", s, c) for c in range(NCH)]
            for hd in range(16):
                if self.alim and (s >= self.alim["nseq"] or hd not in self.alim.get("ch", [0])):
                    continue
                qt, bq = q_r.next()
                kt_, bk = k_r.next()
                kb.load(qt[0:64, :], bq, self.ZT1[s, 64 * hd:64 * hd + 64, :], sbufs=zdeps)
                kb.load(kt_[0:64, :], bk, self.ZT1[s, 1024 + 64 * hd:1024 + 64 * hd + 64, :], sbufs=zdeps)
                vts = []
                for p, d in enumerate(dils):
                    vt, bv = va[p].next()
                    src = self.VT1[s][:, 64 * hd:64 * hd + 64]
                    if d == 1:
                        kb.load(vt[:, :, 0:64], bv, src.rearrange("(b i) f -> i b f", i=128), sbufs=vdeps, nonc=True)
                    else:
                        nbt = 32 // d
                        for r in range(d):
                            kb.load(vt[:, r * nbt:(r + 1) * nbt, 0:64], bv, src[r::d, :].rearrange("(b i) f -> i b f", i=128), sbufs=vdeps, nonc=True)
                    vts.append((vt, bv))
                acc, bacc = acc_r.next()
                for p, d in enumerate(dils):
                    vt, bv = vts[p]
                    nbt = 32 // d
                    qtl = []
                    if d == 1:
                        order = [(0, b) for b in range(32)]
                    elif d == 4:
                        order = [(r, b) for b in range(8) for r in range(4)]
                    else:
                        order = [(r, b) for b in range(2) for r in range(16)]
                    units = []
                    for u in range(8):
                        unit = []
                        for (r, b) in order[4 * u:4 * u + 4]:
                            st = d * 128 * b + r
                            qap = qt[:, st:st + d * 127 + 1:d]
                            blocks = []
                            if b > 0:
                                sp_ = d * 128 * (b - 1) + r
                                blocks.append((kt_[:, sp_:sp_ + d * 127 + 1:d], 2, vt[:, r * nbt + b - 1, :]))
                            else:
                                blocks.append((kt_[:, st:st + d * 127 + 1:d], 3, vt[:, r * nbt + b, :]))
                            blocks.append((kt_[:, st:st + d * 127 + 1:d], 0, vt[:, r * nbt + b, :]))
                            unit.append((qap, blocks))
                        units.append(unit)

                    def post(u, ob, bob, p=p, d=d, order=order, acc=acc, bacc=bacc):
                        r0, b0 = order[4 * u]
                        if d == 1:
                            dst = acc[:, 512 * u:512 * (u + 1)]
                            src = ob[:]
                        elif d == 4:
                            dst = acc[:, 512 * b0:512 * (b0 + 1)].rearrange("p (i r) -> p r i", r=4)
                            src = ob[:].rearrange("p (r i) -> p r i", r=4)
                        else:
                            dst = acc[:, 2048 * b0:2048 * (b0 + 1)].rearrange("p (i r) -> p r i", r=16)[:, r0:r0 + 4, :]
                            src = ob[:].rearrange("p (r i) -> p r i", r=4)
                        if p == 0:
                            kb.op("act", lambda e: e.copy(dst, src), reads=[bob], writes=[bacc])
                        else:
                            kb.op("dve", lambda e: e.tensor_tensor(dst, src, dst, ALU.add), reads=[bob, bacc], writes=[bacc])
                        if pending_final:
                            pending_final.pop(0)()

                    self.attend(kb, units, 64, [bq, bk, bv], post, pair_kind=0)
                while pending_final:
                    pending_final.pop(0)()

                def mk_final(s=s, hd=hd, acc=acc, bacc=bacc):
                    mx, bmx = mx_r.next()
                    fl = []
                    for c in range(8):
                        def chunk(c=c):
                            cs_ = slice(512 * c, 512 * (c + 1))
                            rc, brc = rec_r.next()
                            kb.op("dve", lambda e: e.tensor_copy(rc[:], acc[64:128, cs_]), reads=[bacc], writes=[brc])
                            kb.op("act", lambda e: e.activation(out=rc[:], in_=rc[:], func=AF.Ln), reads=[brc], writes=[brc])
                            kb.op("act", lambda e: e.activation(out=rc[:], in_=rc[:], func=AF.Exp, scale=-1.0), reads=[brc], writes=[brc])
                            kb.op("pool", lambda e: e.tensor_tensor(mx[:, cs_], acc[0:64, cs_], rc[:], ALU.mult), reads=[bacc, brc], writes=[bmx])
                        fl.append(chunk)
                    fl.append(lambda: kb.store(self.MIXT[s, 64 * hd:64 * hd + 64, :], [kb.db("MIXT", s, hd)], mx[:], bmx))
                    return fl
                pending_final.extend(mk_final())

        while pending_final:
            pending_final.pop(0)()

    def ph_final(self, kb):
        I = self.I
        nc = self.nc
        gt, gb = kb.sb("gfin", [128, 8], F32)
        kb.load(gt[:], gb, I["final_norm"].rearrange("(k p) -> p k", p=128), nonc=True)
        xT_r = Ring(kb, "xTz", [128, 8, TC], F32, 2)
        sq_r = Ring(kb, "sqz", [128, 8, TC], BF16, 1)
        rs_r = Ring(kb, "rsz", [128, TC], F32, 2)
        yo_r = Ring(kb, "yo", [128, 4, D], F32, 2)
        for s in range(NSEQ):
            for c in range(NCH):
                if self.lim is not None and s * NCH + c >= self.lim:
                    continue
                t0 = c * TC
                xT, bxT = xT_r.next()
                kb.load(xT[:], bxT, self.XT[s, :, t0:t0 + TC].rearrange("(k p) t -> p k t", p=128), sbufs=[kb.db("XT", s, c)])
                sq, bsq = sq_r.next()
                rs, brs = rs_r.next()
                self.norm_chunk(kb, xT, bxT, None, None, sq, bsq, rs, brs, TC)
                for k in range(8):
                    kb.op("dve", lambda e, k=k, xT=xT, rs=rs: e.scalar_tensor_tensor(xT[:, k, :], xT[:, k, :], gt[:, k:k + 1], rs[:], ALU.mult, ALU.mult),
                          reads=[bxT, gb, brs], writes=[bxT])
                yo, byo = yo_r.next()
                for j in range(4):
                    for kk in range(2):
                        pm, bpm = self.psM.next()
                        for k4 in range(4):
                            k = kk * 4 + k4
                            kb.op("pe", lambda e, k=k, k4=k4, j=j, pm=pm, xT=xT: e.transpose(pm[:, k4 * 128:(k4 + 1) * 128], xT[:, k, j * 128:(j + 1) * 128], self.identf[:]),
                                  reads=[bxT, self.b_identf], writes=[bpm], inc=(k4 == 3))
                        kb.op("act", lambda e, pm=pm, yo=yo, j=j, kk=kk: e.copy(yo[:, j, kk * 512:(kk + 1) * 512], pm[:]), reads=[bpm], writes=[byo])
                kb.store(self.out[s, t0:t0 + TC, :].rearrange("(j p) f -> p j f", p=128), [kb.db("out", s, c)], yo[:], byo, eng="sp")


_CONSTS = None


def kernel(**inputs):
    global _CONSTS
    if _CONSTS is None:
        _CONSTS = host_consts()
    prog = Prog()
    nc = prog.build()
    x = np.ascontiguousarray(inputs["x"], dtype=np.float32)
    in_maps = []
    for cid in range(8):
        m = {"x": x[cid * NSEQ:(cid + 1) * NSEQ]}
        for name, shape in IN_SPECS[1:]:
            m[name] = np.ascontiguousarray(inputs[name], dtype=np.float32)
        m.update(_CONSTS)
        in_maps.append(m)
    res = run_bass_kernel_spmd(nc, in_maps, core_ids=list(range(8)))
    return np.concatenate([r["out"] for r in res.results], axis=0)
```

```python
import numpy as np
import ml_dtypes
import concourse.bass as bass
import concourse.mybir as mybir
from concourse.bass_utils import run_bass_kernel_spmd
from contextlib import ExitStack

F32 = mybir.dt.float32
BF16 = mybir.dt.bfloat16
ALU = mybir.AluOpType
AF = mybir.ActivationFunctionType

T = 4096
D = 1024
NSEQ = 2
TC = 512
NCH = T // TC
FH = 2816
BIG = 30000.0
SCALE = 0.125
EPS = 1e-5

QP, QR, KC, VC, KS, KW, BQ, BK = 0, 512, 1024, 1152, 1280, 1408, 1536, 2048
ZT0_ROWS = 2176


class Buf:
    __slots__ = ("name", "w", "r", "box", "psum")

    def __init__(self, name="", psum=False):
        self.name = name
        self.w = None
        self.r = {}
        self.box = None
        self.psum = psum


class _Rec:
    def __init__(self):
        self.call = None

    def __getattr__(self, name):
        def f(*a, **kw):
            self.call = (name, a, kw)
            return None
        return f


class Sched:
    ENG = ("pe", "act", "dve", "pool", "sp")

    def __init__(self, nc, es, nbox=44):
        self.nc = nc
        self.es = es
        self.q = {e: [] for e in self.ENG}
        self.esem = {e: es.enter_context(nc.semaphore("es_" + e)) for e in self.ENG}
        self.ecount = {e: 0 for e in self.ENG}
        self.waited = {e: {} for e in self.ENG}
        self.boxes = [[es.enter_context(nc.semaphore("dq%d" % i)), 0] for i in range(2 * nbox)]
        self.boxpool = {"sp": self.boxes[:nbox], "pool": self.boxes[nbox:]}
        self.nextbox = {"sp": 0, "pool": 0}
        self.nops = 0

    def box(self, buf, eng):
        if buf.box is None:
            buf.box = {}
        if eng not in buf.box:
            pool = self.boxpool[eng]
            buf.box[eng] = pool[self.nextbox[eng] % len(pool)]
            self.nextbox[eng] += 1
        return buf.box[eng]

    def _add(self, deps, tok, eng, is_dma, raw):
        ek, sem, val, bx = tok
        if bx is not None:
            val = bx[1]
        elif ek == eng and not is_dma:
            if eng == "pe":
                return
        k = id(sem)
        if k not in deps or deps[k][1] < val:
            deps[k] = (sem, val)

    maxops = None

    def op(self, eng, fn, reads=(), writes=(), inc=True, dma=None, force=False):
        if self.maxops is not None and self.nops >= self.maxops and not force:
            return None
        rec = _Rec()
        fn(rec)
        call = rec.call
        is_dma = dma is not None
        deps = {}
        for b in reads:
            if b.w is not None:
                self._add(deps, b.w, eng, is_dma, True)
            if b.psum:
                for t in b.r.values():
                    if t[0] != eng:
                        self._add(deps, t, eng, is_dma, False)
        for b in writes:
            if b.w is not None:
                self._add(deps, b.w, eng, is_dma, False)
            for t in b.r.values():
                self._add(deps, t, eng, is_dma, False)
        waits = []
        wd = self.waited[eng]
        for k, (sem, val) in deps.items():
            if wd.get(k, 0) >= val:
                continue
            wd[k] = val
            waits.append((sem, val))
        if is_dma:
            dma[1] += 16
            tok = ("dma", dma[0], dma[1], dma)
            incspec = (dma[0], 16)
        elif inc:
            self.ecount[eng] += 1
            tok = (eng, self.esem[eng], self.ecount[eng], None)
            incspec = (self.esem[eng], 1)
        else:
            tok = (eng, self.esem[eng], self.ecount[eng] + 1, None)
            incspec = None
        kt = id(tok[1])
        for b in reads:
            o = b.r.get(kt)
            if o is None or o[2] < tok[2]:
                b.r[kt] = tok
        for b in writes:
            b.w = tok
            b.r = {}
        self.nops += 1
        self.q[eng].append((waits, call, incspec))
        return tok

    def barrier(self):
        targets = [(self.esem[f], self.ecount[f]) for f in self.ENG if self.ecount[f] > 0]
        targets += [(bx[0], bx[1]) for bx in self.boxes if bx[1] > 0]
        for e in self.ENG:
            waits = []
            wd = self.waited[e]
            for sem, val in targets:
                k = id(sem)
                if sem is self.esem[e] and e == "pe":
                    continue
                if wd.get(k, 0) >= val:
                    continue
                wd[k] = val
                waits.append((sem, val))
            self.q[e].append((waits, ("nop", (), {}), None))

    def emit(self, block):
        def run(e, key):
            for waits, (name, a, kw), incspec in self.q[key]:
                for sem, val in waits:
                    e.wait_ge(sem, val)
                ins = getattr(e, name)(*a, **kw)
                if incspec is not None:
                    ins.then_inc(incspec[0], incspec[1])

        @block.tensor
        def _(e):
            run(e, "pe")

        @block.scalar
        def _(e):
            run(e, "act")

        @block.vector
        def _(e):
            run(e, "dve")

        @block.gpsimd
        def _(e):
            run(e, "pool")

        @block.sync
        def _(e):
            run(e, "sp")


class Ring:
    def __init__(self, kb, name, shape, dt, n, psum=False):
        self.items = []
        for i in range(n):
            self.items.append(kb.ps(name + str(i), shape, dt) if psum else kb.sb(name + str(i), shape, dt))
        self.i = 0

    def next(self):
        it = self.items[self.i % len(self.items)]
        self.i += 1
        return it


class KB:
    def __init__(self, nc, es):
        self.nc = nc
        self.es = es
        self.S = Sched(nc, es)
        self.dbufs = {}

    _uid = 0

    def sb(self, name, shape, dt):
        KB._uid += 1
        name = "%s_%d" % (name, KB._uid)
        return self.es.enter_context(self.nc.sbuf_tensor(name, shape, dt)), Buf(name)

    def ps(self, name, shape, dt):
        KB._uid += 1
        name = "%s_%d" % (name, KB._uid)
        return self.es.enter_context(self.nc.psum_tensor(name, shape, dt)), Buf(name, psum=True)

    def db(self, *key):
        b = self.dbufs.get(key)
        if b is None:
            b = self.dbufs[key] = Buf(str(key))
        return b

    def load(self, dst, dbuf, src, sbufs=(), eng="sp", nonc=False):
        kw = dict(allow_slow_non_contiguous=True) if nonc else {}
        self.S.op(eng, lambda e: e.dma_start(out=dst, in_=src, **kw), reads=list(sbufs), writes=[dbuf],
                  dma=self.S.box(dbuf, eng))

    def store(self, dst, dbufs, src, sbuf, eng="pool"):
        self.S.op(eng, lambda e: e.dma_start(out=dst, in_=src), reads=[sbuf], writes=list(dbufs),
                  dma=self.S.box(sbuf, eng))

    def op(self, eng, fn, reads=(), writes=(), inc=True):
        return self.S.op(eng, fn, reads=reads, writes=writes, inc=inc)


def host_consts():
    bf = ml_dtypes.bfloat16
    c = {}
    inv = 1.0 / (10000.0 ** (np.arange(0, 64, 2, dtype=np.float32) / 64.0))
    ang = np.arange(T, dtype=np.float32)[:, None] * inv[None, :]
    ang = np.concatenate([ang, ang], axis=-1)
    cos = np.cos(ang).astype(np.float32).T
    sin = np.sin(ang).astype(np.float32).T
    c["c_cos"] = np.ascontiguousarray(np.concatenate([cos, cos], 0))
    c["c_sin"] = np.ascontiguousarray(np.concatenate([sin, sin], 0))
    c["c_identf"] = np.eye(128, dtype=np.float32)
    c["c_identb"] = np.eye(128, dtype=np.float32).astype(bf)
    c["c_onesb"] = np.ones((128, 128), dtype=np.float32).astype(bf)
    k = np.arange(128)[:, None]
    q = np.arange(128)[None, :]
    tri = np.where(k > q, -BIG, 0.0)
    anti_s = np.where(k <= q, -BIG, 0.0)
    anti_i = np.where(k < q, -BIG, 0.0)
    full = np.full((128, 128), -BIG)
    c["c_masks"] = np.ascontiguousarray(np.stack([tri, anti_s, anti_i, full], 1).astype(np.float32)).astype(bf)
    comps = []
    for prev in (anti_i, anti_s):
        for v in range(4):
            p0 = full if (v & 1) else prev
            p1 = full if (v & 2) else prev
            comps.append(np.concatenate([p0, tri, p1, tri], axis=1))
    c["c_cmask"] = np.ascontiguousarray(np.stack(comps, 1).astype(np.float32)).astype(bf)
    u = np.arange(T)[None, :]
    c["c_st"] = np.where(16 * k + 31 > u, -BIG, 0.0).astype(np.float32).astype(bf)
    n = np.arange(64)[:, None]
    j = np.arange(T)[None, :]
    c["c_e"] = np.where(j // 64 == n, BIG, 0.0).astype(np.float32).astype(bf)
    cs = np.arange(256) * 16
    ss = np.arange(64) * 64
    ov = ((cs[:, None] <= ss[None, :] + 63) & (cs[:, None] + 31 >= ss[None, :])).astype(np.float32)
    ov[255, :] = 0.0
    ova = np.zeros((256, 65), np.float32)
    ova[:, :64] = ov
    ova[:255, 64] = 1.0
    c["c_ova"] = np.ascontiguousarray(ova.reshape(2, 128, 65).transpose(1, 0, 2)).astype(bf)
    t = np.arange(T)
    cur = t // 64
    blk = np.arange(64)[None, :]
    fb = np.zeros((T, 64), np.float32)
    fb[blk > cur[:, None]] = -1e4
    forced = (blk == 0) | (blk == cur[:, None]) | (blk == cur[:, None] - 1)
    fb[forced] = 1e4
    c["c_fb"] = np.ascontiguousarray(fb.reshape(32, 128, 64).transpose(1, 0, 2))
    return c


CONST_SPECS = [("c_cos", [128, T], F32), ("c_sin", [128, T], F32), ("c_identf", [128, 128], F32),
               ("c_identb", [128, 128], BF16), ("c_onesb", [128, 128], BF16), ("c_masks", [128, 4, 128], BF16), ("c_cmask", [128, 8, 512], BF16),
               ("c_st", [128, T], BF16), ("c_e", [64, T], BF16), ("c_ova", [128, 2, 65], BF16),
               ("c_fb", [128, 32, 64], F32)]

IN_SPECS = [("x", [NSEQ, T, D]), ("attn_norm", [2, D]), ("ffn_norm", [2, D]), ("final_norm", [D]),
            ("w_in_e", [1, D, 2072]), ("w_out_e", [1, D, D]), ("cmp_pe_k", [1, 32, 64]), ("cmp_pe_v", [1, 32, 64]),
            ("cmp_k_w1", [1, 2048, 256]), ("cmp_k_w2", [1, 256, 64]), ("cmp_v_w1", [1, 2048, 256]),
            ("cmp_v_w2", [1, 256, 64]), ("sinks", [1, 8]), ("w_qkv_o", [1, D, 3072]), ("w_out_o", [1, D, D]),
            ("w_gate_up", [2, D, 2 * FH]), ("w_down", [2, FH, D])]


class Prog:
    def __init__(self, stop_after=None, dump=(), lim=None, maxops=None, alim=None, only=None):
        self.only = only
        Sched.maxops = maxops
        self.alim = alim
        self.lim = lim
        self.stop_after = stop_after
        self.dump = set(dump)
        nc = self.nc = bass.Bass("TRN2", target_bir_lowering=False)
        self.I = {}
        for name, shape in IN_SPECS:
            self.I[name] = nc.dram_tensor(name, shape, F32, kind="ExternalInput").ap()
        for name, shape, dt in CONST_SPECS:
            self.I[name] = nc.dram_tensor(name, shape, dt, kind="ExternalInput").ap()
        self.out = nc.dram_tensor("out", [NSEQ, T, D], F32, kind="ExternalOutput").ap()

        def scratch(name, shape, dt):
            kind = "ExternalOutput" if name in self.dump else "Internal"
            return nc.dram_tensor(name, shape, dt, kind=kind).ap()

        self.XT = scratch("XT", [NSEQ, D, T], F32)
        self.ZT0 = scratch("ZT0", [NSEQ, ZT0_ROWS, T], BF16)
        self.VT0 = scratch("VT0", [NSEQ, T, 384], BF16)
        self.GT = scratch("GT", [NSEQ, 24, T], F32)
        self.MIXT = scratch("MIXT", [NSEQ, D, T], BF16)
        self.ZT1 = scratch("ZT1", [NSEQ, 2048, T], BF16)
        self.VT1 = scratch("VT1", [NSEQ, T, D], BF16)
        self.DBG = scratch("DBG", [128, 4096], F32)

    def build(self):
        nc = self.nc
        phases = [self.ph_inproj0, self.ph_attn0, self.ph_outproj(0), self.ph_ffn(0),
                  self.ph_inproj1, self.ph_attn1, self.ph_outproj(1), self.ph_ffn(1), self.ph_final]
        names = ["inproj0", "attn0", "outproj0", "ffn0", "inproj1", "attn1", "outproj1", "ffn1", "final"]
        with ExitStack() as es0:
            kb = self.kb = KB(nc, es0)
            self.consts_setup(kb)
            self.final_bufs = []
            for ph, nm in zip(phases, names):
                if self.only is not None and nm not in self.only:
                    continue
                with ExitStack() as es:
                    kb.es = es
                    ph(kb)
                    kb.S.barrier()
                kb.es = es0
                if self.stop_after == nm:
                    break
            kb.S.op("sp", lambda e: e.nop(), reads=list(kb.dbufs.values()), force=True)
            with nc.Block() as block:
                kb.S.emit(block)
        return nc

    def consts_setup(self, kb):
        I = self.I
        self.identf, self.b_identf = kb.sb("identf", [128, 128], F32)
        self.identb, self.b_identb = kb.sb("identb", [128, 128], BF16)
        self.onesb, self.b_onesb = kb.sb("onesb", [128, 128], BF16)
        self.masks, self.b_masks = kb.sb("masks", [128, 4, 128], BF16)
        self.cmask, self.b_cmask = kb.sb("cmask", [128, 8, 512], BF16)
        kb.load(self.cmask[:], self.b_cmask, I["c_cmask"])
        self.epsb, self.b_eps = kb.sb("epsb", [128, 1], F32)
        kb.load(self.identf[:], self.b_identf, I["c_identf"])
        kb.load(self.identb[:], self.b_identb, I["c_identb"])
        kb.load(self.onesb[:], self.b_onesb, I["c_onesb"])
        kb.load(self.masks[:], self.b_masks, I["c_masks"])
        kb.op("pool", lambda e: e.memset(self.epsb[:], EPS), writes=[self.b_eps])
        self.psS = Ring(kb, "psS", [128, 512], F32, 4, psum=True)
        self.psO = Ring(kb, "psO", [128, 512], F32, 2, psum=True)
        self.psM = Ring(kb, "psM", [128, 512], F32, 2, psum=True)

        class _Cat:
            def __init__(self, rings):
                self.items = [it for r in rings for it in r.items]
                self.i = 0

            def next(self):
                it = self.items[self.i % len(self.items)]
                self.i += 1
                return it
        self.psA = _Cat([self.psM, self.psS])

    def load_weight(self, kb, wt, wbuf, src2d, col_pairs, gvec=None, stg_r=None, CH=2048):
        nk = wt.shape[1]
        if stg_r is None:
            stg_r = Ring(kb, "wstg_" + wbuf.name, [128, CH], F32, 3)
        gt = gb = None
        if gvec is not None:
            gt, gb = kb.sb("g_" + wbuf.name, [128, nk], F32)
            kb.load(gt[:], gb, gvec.rearrange("(k p) -> p k", p=128), nonc=True)
        i = 0
        for k in range(nk):
            for (dc, sc, n) in col_pairs:
                for c0 in range(0, n, CH):
                    cn = min(CH, n - c0)
                    st, bst = stg_r.next()
                    kb.load(st[:, 0:cn], bst, src2d[k * 128:(k + 1) * 128, sc + c0:sc + c0 + cn])
                    dst = wt[:, k, dc + c0:dc + c0 + cn]
                    if i % 2 == 0:
                        if gt is not None:
                            kb.op("dve", lambda e: e.tensor_scalar(dst, st[:, 0:cn], gt[:, k:k + 1], None, ALU.mult), reads=[bst, gb], writes=[wbuf])
                        else:
                            kb.op("dve", lambda e: e.tensor_copy(dst, st[:, 0:cn]), reads=[bst], writes=[wbuf])
                    else:
                        if gt is not None:
                            kb.op("act", lambda e: e.activation(out=dst, in_=st[:, 0:cn], func=AF.Copy, scale=gt[:, k:k + 1]), reads=[bst, gb], writes=[wbuf])
                        else:
                            kb.op("act", lambda e: e.copy(dst, st[:, 0:cn]), reads=[bst], writes=[wbuf])
                    i += 1

    def build_rot(self, kb, wt, wbuf, ranges):
        nk = wt.shape[1]
        for (dc, sc, nb) in ranges:
            for k in range(nk):
                dv = wt[:, k, dc:dc + 64 * nb].rearrange("p (b h j) -> p b h j", h=2, j=32)
                sv = wt[:, k, sc:sc + 64 * nb].rearrange("p (b h j) -> p b h j", h=2, j=32)
                kb.op("pool", lambda e, dv=dv, sv=sv: e.tensor_scalar(dv[:, :, 0, :], sv[:, :, 1, :], -1.0, None, ALU.mult),
                      reads=[wbuf], writes=[wbuf])
                kb.op("pool", lambda e, dv=dv, sv=sv: e.tensor_copy(dv[:, :, 1, :], sv[:, :, 0, :]),
                      reads=[wbuf], writes=[wbuf])

    def norm_chunk(self, kb, xT, bx, hT, bh, sq, bsq, rstd, brstd, ntok):
        kb.op("act", lambda e: e.activation(out=sq[:, :, 0:ntok], in_=xT[:, :, 0:ntok], func=AF.Square),
              reads=[bx], writes=[bsq])
        pm, bpm = self.psM.next()
        for k in range(8):
            kb.op("pe", lambda e, k=k: e.matmul(pm[:, 0:ntok], self.onesb[:], sq[:, k, 0:ntok], start=(k == 0), stop=(k == 7)),
                  reads=[bsq, self.b_onesb], writes=[bpm], inc=(k == 7))
        kb.op("act", lambda e: e.activation(out=rstd[:, 0:ntok], in_=pm[:, 0:ntok], func=AF.Ln, bias=self.epsb[:, 0:1], scale=1.0 / D),
              reads=[bpm, self.b_eps], writes=[brstd])
        kb.op("act", lambda e: e.activation(out=rstd[:, 0:ntok], in_=rstd[:, 0:ntok], func=AF.Exp, scale=-0.5),
              reads=[brstd], writes=[brstd])
        if hT is not None:
            kb.op("dve", lambda e: e.tensor_tensor(hT[:, :, 0:ntok], xT[:, :, 0:ntok],
                                                   rstd[:, 0:ntok].unsqueeze(1).to_broadcast([128, 8, ntok]), ALU.mult),
                  reads=[bx, brstd], writes=[bh])

    def ph_inproj0(self, kb):
        I = self.I
        NC0 = 3488
        wt, bw = kb.sb("w0", [128, 8, NC0], BF16)
        w2d = I["w_in_e"][0]
        self.load_weight(kb, wt, bw, w2d,
                         [(0, 0, 896), (896, 1024, 128), (1024, 1304, 640), (1664, 1280, 24),
                          (1696, 896, 128), (1824, 1152, 128), (1952, 1944, 128)], gvec=I["attn_norm"][0])
        self.build_rot(kb, wt, bw, [(2080, 0, 8), (2592, 768, 4), (2848, 1024, 8), (3360, 1536, 2)])
        xin_r = Ring(kb, "xin", [128, 4, D], F32, 2)
        xT_r = Ring(kb, "xT", [128, 8, TC], F32, 2)
        hT_r = Ring(kb, "hT", [128, 8, TC], BF16, 2)
        sq_r = Ring(kb, "sq", [128, 8, TC], BF16, 1)
        rs_r = Ring(kb, "rs", [128, TC], F32, 2)
        cs_r = Ring(kb, "cs", [128, 2, TC], F32, 2)
        t1_r = Ring(kb, "t1", [128, TC], F32, 2)
        t2_r = Ring(kb, "t2", [128, TC], F32, 2)
        st_r = Ring(kb, "stg", [128, TC], BF16, 4)
        sv_r = Ring(kb, "stv", [128, 384], BF16, 2)
        sg_r = Ring(kb, "stgt", [24, TC], F32, 2)
        tiles = []
        for p in range(4):
            tiles.append((128 * p, 2080 + 128 * p, 128, QR + 128 * p, QP + 128 * p))
        tiles.append((512, None, 128, KC, None))
        tiles.append((640, None, 128, VC, None))
        tiles.append((768, 2592, 128, KS, None))
        tiles.append((896, 2720, 128, KW, None))
        for p in range(4):
            tiles.append((1024 + 128 * p, 2848 + 128 * p, 128, BQ + 128 * p, None))
        tiles.append((1536, 3360, 128, BK, None))

        def mm_group(ps, bps, hT, bh, col, rows):
            for k in range(8):
                kb.op("pe", lambda e, k=k: e.matmul(ps[0:rows, :], wt[:, k, col:col + rows], hT[:, k, :],
                                                    start=(k == 0), stop=(k == 7)),
                      reads=[bw, bh], writes=[bps], inc=(k == 7))

        chunks = [(s, c) for s in range(NSEQ) for c in range(NCH) if not (self.lim is not None and s * NCH + c >= self.lim)]

        def load_norm(i):
            s, c = chunks[i]
            t0 = c * TC
            xin, bxin = xin_r.next()
            kb.load(xin[:], bxin, I["x"][s, t0:t0 + TC, :].rearrange("(j p) f -> p j f", p=128))
            cs, bcs = cs_r.next()
            kb.load(cs[:, 0, :], bcs, I["c_cos"][:, t0:t0 + TC])
            kb.load(cs[:, 1, :], bcs, I["c_sin"][:, t0:t0 + TC])
            xT, bxT = xT_r.next()
            for k in range(8):
                pm, bpm = self.psA.next()
                for j in range(4):
                    kb.op("pe", lambda e, k=k, j=j, pm=pm: e.transpose(pm[:, j * 128:(j + 1) * 128], xin[:, j, k * 128:(k + 1) * 128], self.identf[:]),
                          reads=[bxin, self.b_identf], writes=[bpm], inc=(j == 3))
                kb.op("act", lambda e, k=k, pm=pm: e.copy(xT[:, k, :], pm[:]), reads=[bpm], writes=[bxT])
            for k in range(8):
                kb.store(self.XT[s, k * 128:(k + 1) * 128, t0:t0 + TC], [kb.db("XT", s, c)], xT[:, k, :], bxT)
            hT, bh = hT_r.next()
            sq, bsq = sq_r.next()
            rs, brs = rs_r.next()
            self.norm_chunk(kb, xT, bxT, hT, bh, sq, bsq, rs, brs, TC)
            return (s, c, t0, hT, bh, cs, bcs)

        nxt = load_norm(0) if chunks else None
        for i in range(len(chunks)):
            s, c, t0, hT, bh, cs, bcs = nxt
            for ti, (col, rcol, rows, zrow, prow) in enumerate(tiles):
                if ti == 8 and i + 1 < len(chunks):
                    nxt = load_norm(i + 1)
                pa, bpa = self.psA.next()
                mm_group(pa, bpa, hT, bh, col, rows)
                if rcol is None:
                    stg, bst = st_r.next()
                    kb.op("act", lambda e, stg=stg, pa=pa: e.copy(stg[:], pa[:]), reads=[bpa], writes=[bst])
                    kb.store(self.ZT0[s, zrow:zrow + 128, t0:t0 + TC], [kb.db("ZT0", s, zrow, c)], stg[:], bst)
                    continue
                pb, bpb = self.psA.next()
                mm_group(pb, bpb, hT, bh, rcol, rows)
                if prow is not None:
                    stg, bst = st_r.next()
                    kb.op("act", lambda e, stg=stg, pa=pa: e.copy(stg[:], pa[:]), reads=[bpa], writes=[bst])
                    kb.store(self.ZT0[s, prow:prow + 128, t0:t0 + TC], [kb.db("ZT0", s, prow, c)], stg[:], bst)
                t1, bt1 = t1_r.next()
                t2, bt2 = t2_r.next()
                kb.op("dve", lambda e, t1=t1, pa=pa, cs=cs: e.tensor_tensor(t1[:], pa[:], cs[:, 0, :], ALU.mult),
                      reads=[bpa, bcs], writes=[bt1])
                kb.op("dve", lambda e, t2=t2, pb=pb, cs=cs: e.tensor_tensor(t2[:], pb[:], cs[:, 1, :], ALU.mult),
                      reads=[bpb, bcs], writes=[bt2])
                stg, bst = st_r.next()
                kb.op("pool", lambda e, stg=stg, t1=t1, t2=t2: e.tensor_tensor(stg[:], t1[:], t2[:], ALU.add),
                      reads=[bt1, bt2], writes=[bst])
                kb.store(self.ZT0[s, zrow:zrow + 128, t0:t0 + TC], [kb.db("ZT0", s, zrow, c)], stg[:], bst)
            pa, bpa = self.psA.next()
            mm_group(pa, bpa, hT, bh, 1664, 24)
            sg, bsg = sg_r.next()
            kb.op("act", lambda e, sg=sg, pa=pa: e.activation(out=sg[:], in_=pa[0:24, :], func=AF.Sigmoid),
                  reads=[bpa], writes=[bsg])
            kb.store(self.GT[s, :, t0:t0 + TC], [kb.db("GT", s, c)], sg[:], bsg)
            for j in range(4):
                pa, bpa = self.psA.next()
                for k in range(8):
                    kb.op("pe", lambda e, k=k, j=j, pa=pa: e.matmul(pa[:, 0:384], hT[:, k, j * 128:(j + 1) * 128], wt[:, k, 1696:2080],
                                                                    start=(k == 0), stop=(k == 7)),
                          reads=[bw, bh], writes=[bpa], inc=(k == 7))
                sv, bsv = sv_r.next()
                kb.op("act", lambda e, sv=sv, pa=pa: e.copy(sv[:], pa[:, 0:384]), reads=[bpa], writes=[bsv])
                kb.store(self.VT0[s, t0 + j * 128:t0 + (j + 1) * 128, :], [kb.db("VT0", s, c)], sv[:], bsv)

    def attn_setup(self, kb):
        self.pt_r = Ring(kb, "pt", [128, 512], BF16, 4)

    def attend(self, kb, qtiles, kdim, q_bufs, post, pair_kind=None, look=3):
        banks = []
        cur = []
        for u, unit in enumerate(qtiles):
            for j, (qap, blocks) in enumerate(unit):
                nb = len(blocks)
                for bi, (lh, mk, va) in enumerate(blocks):
                    cur.append((u, j, bi, nb, qap, lh, mk, va, j == len(unit) - 1 and bi == nb - 1))
                    if len(cur) == 4:
                        banks.append(cur)
                        cur = []
        if cur:
            banks.append(cur)
        state = {"ob": None, "u": -1, "first": True}
        pending_posts = []

        def qk(bank):
            ps, bps = self.psS.next()
            n = len(bank)
            first = True
            if pair_kind is not None and n == 4:
                v = (1 if bank[0][6] == 3 else 0) + (2 if bank[2][6] == 3 else 0)
                kb.op("pe", lambda e: e.matmul(ps[:], self.identb[:], self.cmask[:, 4 * pair_kind + v, :], start=True, stop=False, skip_group_check=True),
                      reads=[self.b_identb, self.b_cmask], writes=[bps], inc=False)
                first = False
            else:
                for si, (u, j, bi, nb, qap, lh, mk, va, last) in enumerate(bank):
                    if mk is not None:
                        sl = ps[:, si * 128:(si + 1) * 128]
                        kb.op("pe", lambda e: e.matmul(sl, self.identb[:], self.masks[:, mk, :], start=first, stop=False, skip_group_check=True),
                              reads=[self.b_identb, self.b_masks], writes=[bps], inc=False)
                        first = False
            for si, (u, j, bi, nb, qap, lh, mk, va, last) in enumerate(bank):
                if mk == 3 and si != n - 1 and not first:
                    continue
                sl = ps[:, si * 128:(si + 1) * 128]
                kb.op("pe", lambda e: e.matmul(sl, lh, qap, start=first, stop=(si == n - 1), skip_group_check=True),
                      reads=q_bufs, writes=[bps], inc=(si == n - 1))
                first = False
            return ps, bps

        def ex_pv(bank, ps, bps):
            n = len(bank)
            pt, bpt = self.pt_r.next()
            kb.op("act", lambda e: e.activation(out=pt[:, 0:n * 128], in_=ps[:, 0:n * 128], func=AF.Exp, scale=SCALE),
                  reads=[bps], writes=[bpt])
            while pending_posts:
                pending_posts.pop(0)()
            for si, (u, j, bi, nb, qap, lh, mk, va, last) in enumerate(bank):
                if u != state["u"]:
                    state["ob"] = self.psO.next()
                    state["u"] = u
                    state["first"] = True
                ob, bob = state["ob"]
                psl = pt[:, si * 128:(si + 1) * 128]
                if mk == 3 and not last and si != n - 1:
                    continue
                kb.op("pe", lambda e: e.matmul(ob[:, j * 128:(j + 1) * 128], va, psl, start=state["first"], stop=last, skip_group_check=True),
                      reads=[bpt] + q_bufs, writes=[bob], inc=(si == n - 1 or last))
                state["first"] = False
                if last:
                    pending_posts.append(lambda u=u, ob=ob, bob=bob: post(u, ob, bob))

        pend = []
        for bank in banks:
            r = qk(bank)
            pend.append((bank, r[0], r[1]))
            if len(pend) > look:
                ex_pv(*pend.pop(0))
        while pend:
            ex_pv(*pend.pop(0))
        while pending_posts:
            pending_posts.pop(0)()

    def ph_attn0(self, kb):
        I = self.I
        nc = self.nc
        self.attn_setup(kb)
        stt, bstt = kb.sb("stt", [128, T], BF16)
        kb.load(stt[:], bstt, I["c_st"])
        ova, bova = kb.sb("ova", [128, 2, 65], BF16)
        kb.load(ova[:], bova, I["c_ova"])
        fb, bfb = kb.sb("fb", [128, 32, 64], F32)
        kb.load(fb[:], bfb, I["c_fb"])
        es_t, bes = kb.sb("esink", [128, 8], F32)
        kb.load(es_t[:], bes, I["sinks"][0].partition_broadcast(128))
        kb.op("act", lambda e: e.activation(out=es_t[:], in_=es_t[:], func=AF.Exp), reads=[bes], writes=[bes])
        acc = [kb.sb("acc%d" % g, [64, T], F32) for g in range(4)]
        G = [kb.sb("G%d" % i, [128, T], BF16) for i in range(4)]
        w1 = {"k": (acc[0][0][:].bitcast(BF16).rearrange("p (j m) -> p j m", m=256), acc[0][1]),
              "v": (acc[1][0][:].bitcast(BF16).rearrange("p (j m) -> p j m", m=256), acc[1][1])}
        w2 = {}
        pet = {}
        for nm in ("k", "v"):
            w2[nm] = kb.sb("cw2" + nm, [128, 2, 64], BF16)
            kb.load(w2[nm][0][:], w2[nm][1], I["cmp_%s_w2" % nm][0].rearrange("(c p) d -> p c d", p=128), eng="pool")
            pet[nm] = kb.sb("pet" + nm, [64, 32], BF16)
            kb.load(pet[nm][0][:], pet[nm][1], I["cmp_pe_" + nm][0].rearrange("j d -> d j"), eng="pool", nonc=True)
        cbias = {nm: kb.sb("cb" + nm, [128, 2], F32) for nm in ("k", "v")}

        def load_w1_and_bias(first):
            for nm in ("k", "v"):
                kb.load(w1[nm][0], w1[nm][1], I["cmp_%s_w1" % nm][0].rearrange("(j d) m -> d j m", d=64), eng="pool")
            if not first:
                return
            for nm in ("k", "v"):
                for mc in range(2):
                    pm, bpm = self.psM.next()
                    for j in range(32):
                        kb.op("pe", lambda e: e.matmul(pm[:, 0:1], w1[nm][0][:, j, mc * 128:(mc + 1) * 128], pet[nm][0][:, j:j + 1],
                                                       start=(j == 0), stop=(j == 31)),
                              reads=[w1[nm][1], pet[nm][1]], writes=[bpm], inc=(j == 31))
                    kb.op("act", lambda e: e.copy(cbias[nm][0][:, mc:mc + 1], pm[:, 0:1]), reads=[bpm], writes=[cbias[nm][1]])
        kin, bkin = kb.sb("kin", [64, T], BF16)
        hid, bhid = kb.sb("hid", [128, 2, 256], BF16)
        kcT = [kb.sb("kcT%d" % h, [128, 256], BF16) for h in range(2)]
        vca = [kb.sb("vca%d" % h, [128, 2, 128], BF16) for h in range(2)]
        qaug, bqaug = kb.sb("qaug", [128, 4, T], BF16)
        kaug, bkaug = G[0][0], G[0][1]
        kw, bkw = G[1][0][0:64, :], G[1][1]
        kwf = G[1][0]
        for gi_ in (1, 2, 3):
            kb.op("pool", lambda e: e.memset(G[gi_][0][64:128, :], 0.0), writes=[G[gi_][1]])
        vs, bvs = G[2][0][:].rearrange("p (t d) -> p t d", d=128), G[2][1]
        vw, bvw = G[3][0][:].rearrange("p (t d) -> p t d", d=128), G[3][1]
        gate_r = Ring(kb, "gate", [64, 4, 128], F32, 2)
        g1_r = Ring(kb, "gate1", [64, 512], F32, 2)
        d_r = Ring(kb, "dd", [64, 512], F32, 2)
        tmp_r = Ring(kb, "tmp", [64, 512], F32, 2)
        osb_r = Ring(kb, "osb", [64, 512], F32, 2)
        imp_r = Ring(kb, "impa", [128, 64], F32, 2)
        imp2_r = Ring(kb, "impb", [128, 64], F32, 2)
        m8_r = Ring(kb, "m8", [128, 16], F32, 2)
        rd_r = Ring(kb, "rd", [128, 4], F32, 2)
        selq_r = Ring(kb, "selq", [128, 64], BF16, 2)
        mix_r = Ring(kb, "mixs", [64, T], BF16, 1)
        for h in range(2):
            kb.op("pool", lambda e: e.memset(vca[h][0][:], 0.0), writes=[vca[h][1]])
            kb.op("pool", lambda e: e.memset(kcT[h][0][:], 0.0), writes=[kcT[h][1]])
        kb.op("pool", lambda e: e.memset(hid[:], 0.0), writes=[bhid])
        kb.load(kaug[64:128, :], bkaug, I["c_e"])

        AL = self.alim
        for s in range(NSEQ):
            if AL and s >= AL["nseq"]:
                continue
            zt = self.ZT0[s]
            zdeps = [kb.db("ZT0", s, r, c) for c in range(NCH) for r in (KC, VC)]
            load_w1_and_bias(s == 0)
            for nm, zrow in (("k", KC), ("v", VC)):
                for h in range(2):
                    kb.load(kin[:], bkin, zt[zrow + 64 * h:zrow + 64 * h + 64, :], sbufs=zdeps)
                    for mc in range(2):
                        pm, bpm = self.psM.next()
                        for j in range(32):
                            kb.op("pe", lambda e: e.matmul(pm[:, 0:255], w1[nm][0][:, j, mc * 128:(mc + 1) * 128],
                                                           kin[:, j:j + 16 * 254 + 1:16], start=(j == 0), stop=(j == 31)),
                                  reads=[w1[nm][1], bkin], writes=[bpm], inc=(j == 31))
                        kb.op("act", lambda e: e.activation(out=hid[:, mc, 0:255], in_=pm[:, 0:255], func=AF.Silu,
                                                            bias=cbias[nm][0][:, mc:mc + 1]),
                              reads=[bpm, cbias[nm][1]], writes=[bhid])
                    if nm == "k":
                        pm, bpm = self.psM.next()
                        for mc in range(2):
                            kb.op("pe", lambda e: e.matmul(pm[0:64, 0:255], w2["k"][0][:, mc, :], hid[:, mc, 0:255],
                                                           start=(mc == 0), stop=(mc == 1)),
                                  reads=[w2["k"][1], bhid], writes=[bpm], inc=(mc == 1))
                        kb.op("act", lambda e: e.copy(kcT[h][0][0:64, 0:255], pm[0:64, 0:255]), reads=[bpm], writes=[kcT[h][1]])
                    else:
                        for ct in range(2):
                            rows = 128 if ct == 0 else 127
                            pm, bpm = self.psM.next()
                            for mc in range(2):
                                kb.op("pe", lambda e: e.matmul(pm[0:rows, 0:64], hid[:, mc, ct * 128:ct * 128 + rows], w2["v"][0][:, mc, :],
                                                               start=(mc == 0), stop=(mc == 1)),
                                      reads=[w2["v"][1], bhid], writes=[bpm], inc=(mc == 1))
                            kb.op("act", lambda e: e.copy(vca[h][0][0:rows, ct, 0:64], pm[0:rows, 0:64]),
                                  reads=[bpm], writes=[vca[h][1]])
                            kb.op("pool", lambda e: e.memset(vca[h][0][0:rows, ct, 64:128], 1.0), writes=[vca[h][1]])
            zall = [kb.db("ZT0", s, r, c) for c in range(NCH) for r in (QP, QP + 128, QP + 256, QP + 384, QR, QR + 128, QR + 256, QR + 384, KS, KW)]
            vdeps = [kb.db("VT0", s, c) for c in range(NCH)]
            gdeps = [kb.db("GT", s, c) for c in range(NCH)]
            for h in range(2):
                if AL and h not in AL["hs"]:
                    continue
                for g in range(4):
                    hd = 4 * h + g
                    kb.load(qaug[0:64, g, :], bqaug, zt[QR + 64 * hd:QR + 64 * hd + 64, :], sbufs=zall)
                qps = []
                for g in range(4):
                    qp, bqp = G[g][0], G[g][1]
                    kb.load(qp[0:64, :], bqp, zt[QP + 64 * (4 * h + g):QP + 64 * (4 * h + g) + 64, :], sbufs=zall)
                    qps.append((qp, bqp))
                backs = []
                for qs in range(32):
                    ncts = 2 if qs >= 16 else 1
                    ob, bob = self.psO.next()
                    pi, bpi = self.psM.next()
                    pts = []
                    for ct in range(ncts):
                        ps, bps = self.psS.next()
                        off = 128 * qs - 2048 * ct
                        kb.op("pe", lambda e: e.matmul(ps[:], self.identb[:], stt[:, off:off + 128].unsqueeze(1).to_broadcast([128, 4, 128]),
                                                       start=True, stop=False, skip_group_check=True),
                              reads=[self.b_identb, bstt], writes=[bps], inc=False)
                        for g in range(4):
                            kb.op("pe", lambda e: e.matmul(ps[:, g * 128:(g + 1) * 128], kcT[h][0][:, ct * 128:(ct + 1) * 128],
                                                           qps[g][0][:, qs * 128:(qs + 1) * 128], start=False, stop=(g == 3), skip_group_check=True),
                                  reads=[kcT[h][1], qps[g][1]], writes=[bps], inc=(g == 3))
                        pt, bpt = self.pt_r.next()
                        kb.op("act", lambda e: e.activation(out=pt[:], in_=ps[:], func=AF.Exp, scale=SCALE),
                              reads=[bps], writes=[bpt])
                        pts.append((pt, bpt))
                    for g in range(4):
                        for ct in range(ncts):
                            pt, bpt = pts[ct]
                            kb.op("pe", lambda e: e.matmul(ob[:, g * 128:(g + 1) * 128], vca[h][0][:, ct, :], pt[:, g * 128:(g + 1) * 128],
                                                           start=(ct == 0), stop=(ct == ncts - 1)),
                                  reads=[bpt, vca[h][1]], writes=[bob], inc=False)
                    for g in range(4):
                        for ct in range(ncts):
                            pt, bpt = pts[ct]
                            kb.op("pe", lambda e: e.matmul(pi[:, g * 65:(g + 1) * 65], pt[:, g * 128:(g + 1) * 128], ova[:, ct, :],
                                                           start=(ct == 0), stop=(ct == ncts - 1)),
                                  reads=[bpt, bova], writes=[bpi, bob], inc=(g == 3 and ct == ncts - 1))
                    def back(qs=qs, ob=ob, bob=bob, pi=pi, bpi=bpi):
                        dd, bdd = d_r.next()
                        obv = ob[:].rearrange("p (g c) -> p g c", g=4)
                        ddv = dd[:].rearrange("p (g c) -> p g c", g=4)
                        kb.op("dve", lambda e: e.tensor_scalar(ddv, obv[64:128], 1e-30, None, ALU.max), reads=[bob], writes=[bdd])
                        yield
                        osb, bosb = osb_r.next()
                        kb.op("act", lambda e: e.copy(osb[:], ob[0:64, :]), reads=[bob], writes=[bosb])
                        yield
                        osv = osb[:].rearrange("p (g c) -> p g c", g=4)
                        rd, brd = rd_r.next()
                        kb.op("dve", lambda e: e.tensor_scalar(rd[:], pi[:, 64:260:65], 1e-30, None, ALU.max), reads=[bpi], writes=[brd])
                        yield
                        kb.op("dve", lambda e: e.reciprocal(rd[:], rd[:]), reads=[brd], writes=[brd])
                        yield
                        ia, bia = imp_r.next()
                        for g in range(4):
                            in1 = fb[:, qs, :] if g == 0 else ia[:]
                            kb.op("dve", lambda e: e.scalar_tensor_tensor(ia[:], pi[:, g * 65:g * 65 + 64], rd[:, g:g + 1], in1, ALU.mult, ALU.add),
                                  reads=[bpi, brd, bfb, bia], writes=[bia])
                            yield
                        m8, bm8 = m8_r.next()
                        ib, bib = imp2_r.next()
                        kb.op("dve", lambda e: e.max(out=m8[:, 0:8], in_=ia[:]), reads=[bia], writes=[bm8])
                        yield
                        kb.op("dve", lambda e: e.match_replace(out=ib[:], in_to_replace=m8[:, 0:8], in_values=ia[:], imm_value=-1e30),
                              reads=[bia, bm8], writes=[bib])
                        yield
                        kb.op("dve", lambda e: e.max(out=m8[:, 8:16], in_=ib[:]), reads=[bib], writes=[bm8])
                        yield
                        sq_, bsq_ = selq_r.next()
                        kb.op("dve", lambda e: e.tensor_scalar(sq_[:], ia[:], m8[:, 15:16], None, ALU.is_ge),
                              reads=[bia, bm8], writes=[bsq_])
                        yield
                        ptr, bptr = self.psM.next()
                        ptv = ptr[:].bitcast(BF16)
                        kb.op("pe", lambda e: e.transpose(ptv[0:64, 0:128], sq_[:], self.identb[:]),
                              reads=[bsq_, self.b_identb], writes=[bptr])
                        yield
                        kb.op("dve", lambda e: e.tensor_scalar(qaug[64:128, :, qs * 128:(qs + 1) * 128],
                                                               ptv[0:64, 0:128].unsqueeze(1).to_broadcast([64, 4, 128]), -1.0, None, ALU.add),
                              reads=[bptr], writes=[bqaug])
                        yield
                        gt, bgt = gate_r.next()
                        for g in range(4):
                            kb.load(gt[:, g, :], bgt, self.GT[s, 3 * (4 * h + g), qs * 128:(qs + 1) * 128].partition_broadcast(64), sbufs=gdeps)
                        kb.op("act", lambda e: e.activation(out=dd[:], in_=dd[:], func=AF.Ln), reads=[bdd], writes=[bdd])
                        yield
                        kb.op("act", lambda e: e.activation(out=dd[:], in_=dd[:], func=AF.Exp, scale=-1.0), reads=[bdd], writes=[bdd])
                        yield
                        kb.op("pool", lambda e: e.tensor_tensor(ddv, gt[:], ddv, ALU.mult), reads=[bdd, bgt], writes=[bdd])
                        yield
                        for g in range(4):
                            kb.op("dve", lambda e: e.tensor_tensor(acc[g][0][:, qs * 128:(qs + 1) * 128], osv[:, g, :], ddv[:, g, :], ALU.mult),
                                  reads=[bosb, bdd], writes=[acc[g][1]])
                            yield

                    backs.append(back())
                    if len(backs) == 2:
                        live = list(backs)
                        backs = []
                        while live:
                            for gen in list(live):
                                try:
                                    next(gen)
                                except StopIteration:
                                    live.remove(gen)
                kb.load(kaug[0:64, :], bkaug, zt[KS + 64 * h:KS + 64 * h + 64, :], sbufs=zall)
                kb.load(kw, bkw, zt[KW + 64 * h:KW + 64 * h + 64, :], sbufs=zall)
                kb.op("pool", lambda e: e.memset(vs[:, :, 64:128], 1.0), writes=[bvs])
                kb.op("pool", lambda e: e.memset(vw[:, :, 64:128], 1.0), writes=[bvw])
                kb.load(vs[:, :, 0:64], bvs, self.VT0[s][:, 64 * h:64 * h + 64].rearrange("(t p) d -> p t d", p=128), sbufs=vdeps, nonc=True)
                kb.load(vw[:, :, 0:64], bvw, self.VT0[s][:, 128 + 64 * h:128 + 64 * h + 64].rearrange("(t p) d -> p t d", p=128), sbufs=vdeps, nonc=True)
                for g in range(4):
                    hd = 4 * h + g
                    if AL and g not in AL["gs"]:
                        continue
                    for br, gi in (("sel", 1), ("win", 2)):
                        units = []
                        for u in range(8):
                            unit = []
                            for j in range(4):
                                qs = 4 * u + j
                                if br == "sel":
                                    qap = qaug[:, g, qs * 128:(qs + 1) * 128]
                                    blocks = [(kaug[:, kt * 128:(kt + 1) * 128], (0 if kt == qs else None), vs[:, kt, :]) for kt in range(qs + 1)]
                                else:
                                    qap = qaug[:, g, qs * 128:(qs + 1) * 128]
                                    blocks = []
                                    for kt in range(max(0, qs - 4), qs + 1):
                                        mk = 0 if kt == qs else (1 if kt == qs - 4 else None)
                                        blocks.append((kwf[:, kt * 128:(kt + 1) * 128], mk, vw[:, kt, :]))
                                unit.append((qap, blocks))
                            units.append(unit)

                        def post(u, ob, bob, g=g, gi=gi, hd=hd):
                            gt, bgt = g1_r.next()
                            kb.load(gt[:], bgt, self.GT[s, 3 * hd + gi, u * 512:(u + 1) * 512].partition_broadcast(64), sbufs=gdeps)
                            dd, bdd = d_r.next()
                            kb.op("dve", lambda e: e.tensor_scalar(dd[:], ob[64:128, :], 1e-30, None, ALU.max), reads=[bob], writes=[bdd])
                            kb.op("act", lambda e: e.activation(out=dd[:], in_=dd[:], func=AF.Ln), reads=[bdd], writes=[bdd])
                            kb.op("act", lambda e: e.activation(out=dd[:], in_=dd[:], func=AF.Exp, scale=-1.0), reads=[bdd], writes=[bdd])
                            kb.op("pool", lambda e: e.tensor_tensor(dd[:], gt[:], dd[:], ALU.mult), reads=[bdd, bgt], writes=[bdd])
                            tm, btm = tmp_r.next()
                            kb.op("dve", lambda e: e.tensor_tensor(tm[:], ob[0:64, :], dd[:], ALU.mult), reads=[bob, bdd], writes=[btm])
                            kb.op("pool", lambda e: e.tensor_tensor(acc[g][0][:, u * 512:(u + 1) * 512], acc[g][0][:, u * 512:(u + 1) * 512], tm[:], ALU.add),
                                  reads=[btm, acc[g][1]], writes=[acc[g][1]])

                        self.attend(kb, units, 128 if br == "sel" else 64, [bqaug, bkaug, bkw, bvs, bvw], post)
                    mx, bmx = mix_r.next()
                    kb.op("act", lambda e: e.copy(mx[:], acc[g][0][:]), reads=[acc[g][1]], writes=[bmx])
                    kb.store(self.MIXT[s, 64 * hd:64 * hd + 64, :], [kb.db("MIXT", s, hd)], mx[:], bmx)
            zb = [kb.db("ZT0", s, r, c) for c in range(NCH) for r in (BQ, BQ + 128, BQ + 256, BQ + 384, BK)]
            for hb in range(2):
                kb.load(kw, bkw, zt[BK + 64 * hb:BK + 64 * hb + 64, :], sbufs=zb)
                kb.load(vw[:, :, 0:64], bvw, self.VT0[s][:, 256 + 64 * hb:256 + 64 * hb + 64].rearrange("(t p) d -> p t d", p=128), sbufs=vdeps, nonc=True)
                for g in range(4):
                    hd = 4 * hb + g
                    if AL and hd not in AL["bh"]:
                        continue
                    Gq = G[0] if g % 2 == 0 else G[2]
                    bq, bbq = Gq[0], Gq[1]
                    kb.load(bq[0:64, :], bbq, zt[BQ + 64 * hd:BQ + 64 * hd + 64, :], sbufs=zb)
                    mx, bmx = mix_r.next()
                    units = []
                    for u in range(8):
                        unit = []
                        for j in range(4):
                            qs = 4 * u + j
                            blocks = []
                            if qs > 0:
                                blocks.append((kwf[:, (qs - 1) * 128:qs * 128], 1, vw[:, qs - 1, :]))
                            else:
                                blocks.append((kwf[:, 0:128], 3, vw[:, 0, :]))
                            blocks.append((kwf[:, qs * 128:(qs + 1) * 128], 0, vw[:, qs, :]))
                            unit.append((bq[:, qs * 128:(qs + 1) * 128], blocks))
                        units.append(unit)

                    def post(u, ob, bob, hd=hd, mx=mx, bmx=bmx):
                        dd, bdd = d_r.next()
                        kb.op("dve", lambda e: e.tensor_scalar(dd[:], ob[64:128, :], es_t[0:64, hd:hd + 1], None, ALU.add), reads=[bob, bes], writes=[bdd])
                        kb.op("act", lambda e: e.activation(out=dd[:], in_=dd[:], func=AF.Ln), reads=[bdd], writes=[bdd])
                        kb.op("act", lambda e: e.activation(out=dd[:], in_=dd[:], func=AF.Exp, scale=-1.0), reads=[bdd], writes=[bdd])
                        kb.op("dve", lambda e: e.tensor_tensor(mx[:, u * 512:(u + 1) * 512], ob[0:64, :], dd[:], ALU.mult), reads=[bob, bdd], writes=[bmx])

                    self.attend(kb, units, 64, [bbq, bkw, bvw], post, pair_kind=1)
                    kb.store(self.MIXT[s, 512 + 64 * hd:512 + 64 * hd + 64, :], [kb.db("MIXT", s, 8 + hd)], mx[:], bmx)

    def ph_outproj(self, layer):
        def run(kb):
            I = self.I
            wt, bw = kb.sb("wo", [128, 8, D], BF16)
            w2d = I["w_out_e"][0] if layer == 0 else I["w_out_o"][0]
            self.load_weight(kb, wt, bw, w2d, [(0, 0, D)])
            mx_r = Ring(kb, "mxc", [128, 8, TC], BF16, 2)
            xT_r = Ring(kb, "xTo", [128, 8, TC], F32, 2)
            for s in range(NSEQ):
                mdeps = [kb.db("MIXT", s, hd) for hd in range(16)]
                for c in range(NCH):
                    if self.lim is not None and s * NCH + c >= self.lim:
                        continue
                    t0 = c * TC
                    mx, bmx = mx_r.next()
                    kb.load(mx[:], bmx, self.MIXT[s, :, t0:t0 + TC].rearrange("(k p) t -> p k t", p=128), sbufs=mdeps)
                    xT, bxT = xT_r.next()
                    kb.load(xT[:], bxT, self.XT[s, :, t0:t0 + TC].rearrange("(k p) t -> p k t", p=128), sbufs=[kb.db("XT", s, c)])
                    for n in range(8):
                        pm, bpm = self.psA.next()
                        for k in range(8):
                            kb.op("pe", lambda e, k=k, n=n, pm=pm, mx=mx: e.matmul(pm[:], wt[:, k, n * 128:(n + 1) * 128], mx[:, k, :], start=(k == 0), stop=(k == 7)),
                                  reads=[bw, bmx], writes=[bpm], inc=(k == 7))
                        kb.op("dve", lambda e, n=n, pm=pm, xT=xT: e.tensor_tensor(xT[:, n, :], pm[:], xT[:, n, :], ALU.add), reads=[bpm, bxT], writes=[bxT])
                    kb.store(self.XT[s, :, t0:t0 + TC].rearrange("(k p) t -> p k t", p=128), [kb.db("XT", s, c)], xT[:], bxT)
        return run

    def ph_ffn(self, layer):
        def run(kb):
            I = self.I
            FT = 256
            wgu, bwgu = kb.sb("wgu", [128, 8, 2 * FH], BF16)
            stg_r = Ring(kb, "wstgf", [128, 1408], F32, 2)
            self.load_weight(kb, wgu, bwgu, I["w_gate_up"][layer], [(0, 0, 2 * FH)], gvec=I["ffn_norm"][layer], stg_r=stg_r, CH=1408)
            wd, bwd = kb.sb("wd", [128, 22, D], BF16)
            self.load_weight(kb, wd, bwd, I["w_down"][layer], [(0, 0, D)], stg_r=stg_r, CH=1408)
            xT_r = Ring(kb, "xTf", [128, 8, FT], F32, 2)
            hT_r = Ring(kb, "hTf", [128, 8, FT], BF16, 2)
            sq_r = Ring(kb, "sqf", [128, 8, FT], BF16, 1)
            rs_r = Ring(kb, "rsf", [128, FT], F32, 1)
            act_r = Ring(kb, "actf", [128, 22, FT], BF16, 1)
            sg_r = Ring(kb, "sgf", [128, FT], F32, 3)
            chunks = [(s, c) for s in range(NSEQ) for c in range(T // FT)
                      if not (self.lim is not None and s * (T // FT) + c >= 2 * self.lim)]

            def load_norm(i):
                s, c = chunks[i]
                t0 = c * FT
                xT, bxT = xT_r.next()
                kb.load(xT[:], bxT, self.XT[s, :, t0:t0 + FT].rearrange("(k p) t -> p k t", p=128), sbufs=[kb.db("XT", s, t0 // TC)])
                hT, bh = hT_r.next()
                sq, bsq = sq_r.next()
                rs, brs = rs_r.next()
                self.norm_chunk(kb, xT, bxT, hT, bh, sq, bsq, rs, brs, FT)
                return (s, t0, xT, bxT, hT, bh)

            nxt = load_norm(0) if chunks else None
            for i in range(len(chunks)):
                s, t0, xT, bxT, hT, bh = nxt
                act, bact = act_r.next()
                for n in range(22):
                    pg, bpg = self.psM.next()
                    pu, bpu = self.psS.next()
                    for k in range(8):
                        kb.op("pe", lambda e: e.matmul(pg[:, 0:FT], wgu[:, k, n * 128:(n + 1) * 128], hT[:, k, :], start=(k == 0), stop=(k == 7)),
                              reads=[bwgu, bh], writes=[bpg], inc=(k == 7))
                    for k in range(8):
                        kb.op("pe", lambda e: e.matmul(pu[:, 0:FT], wgu[:, k, FH + n * 128:FH + (n + 1) * 128], hT[:, k, :], start=(k == 0), stop=(k == 7)),
                              reads=[bwgu, bh], writes=[bpu], inc=(k == 7))
                    sg, bsg = sg_r.next()
                    kb.op("act", lambda e: e.activation(out=sg[:], in_=pg[:, 0:FT], func=AF.Silu), reads=[bpg], writes=[bsg])
                    kb.op("dve", lambda e: e.tensor_tensor(act[:, n, :], pu[:, 0:FT], sg[:], ALU.mult), reads=[bpu, bsg], writes=[bact])
                if i + 1 < len(chunks):
                    nxt = load_norm(i + 1)
                for n in range(8):
                    pm, bpm = self.psO.next()
                    for k in range(22):
                        kb.op("pe", lambda e: e.matmul(pm[:, 0:FT], wd[:, k, n * 128:(n + 1) * 128], act[:, k, :], start=(k == 0), stop=(k == 21)),
                              reads=[bwd, bact], writes=[bpm], inc=(k == 21))
                    kb.op("dve", lambda e: e.tensor_tensor(xT[:, n, :], pm[:, 0:FT], xT[:, n, :], ALU.add), reads=[bpm, bxT], writes=[bxT])
                kb.store(self.XT[s, :, t0:t0 + FT].rearrange("(k p) t -> p k t", p=128), [kb.db("XT", s, t0 // TC)], xT[:], bxT)
        return run

    def ph_inproj1(self, kb):
        I = self.I
        NC1 = 3072 + 2048
        wt, bw = kb.sb("w1", [128, 8, NC1], BF16)
        self.load_weight(kb, wt, bw, I["w_qkv_o"][0], [(0, 0, 3072)], gvec=I["attn_norm"][1])
        self.build_rot(kb, wt, bw, [(3072, 0, 32)])
        xT_r = Ring(kb, "xT1", [128, 8, TC], F32, 2)
        hT_r = Ring(kb, "hT1", [128, 8, TC], BF16, 2)
        sq_r = Ring(kb, "sq1", [128, 8, TC], BF16, 1)
        rs_r = Ring(kb, "rs1", [128, TC], F32, 2)
        cs_r = Ring(kb, "cs1", [128, 2, TC], F32, 2)
        t1_r = Ring(kb, "t11", [128, TC], F32, 2)
        t2_r = Ring(kb, "t21", [128, TC], F32, 2)
        st_r = Ring(kb, "stg1", [128, TC], BF16, 4)
        sv_r = Ring(kb, "stv1", [128, D], BF16, 2)
        chunks = [(s, c) for s in range(NSEQ) for c in range(NCH) if not (self.lim is not None and s * NCH + c >= self.lim)]

        def load_norm(i):
            s, c = chunks[i]
            t0 = c * TC
            xT, bxT = xT_r.next()
            kb.load(xT[:], bxT, self.XT[s, :, t0:t0 + TC].rearrange("(k p) t -> p k t", p=128), sbufs=[kb.db("XT", s, c)])
            cs, bcs = cs_r.next()
            kb.load(cs[:, 0, :], bcs, I["c_cos"][:, t0:t0 + TC])
            kb.load(cs[:, 1, :], bcs, I["c_sin"][:, t0:t0 + TC])
            hT, bh = hT_r.next()
            sq, bsq = sq_r.next()
            rs, brs = rs_r.next()
            self.norm_chunk(kb, xT, bxT, hT, bh, sq, bsq, rs, brs, TC)
            return (s, c, t0, hT, bh, cs, bcs)

        nxt = load_norm(0) if chunks else None
        for i in range(len(chunks)):
            s, c, t0, hT, bh, cs, bcs = nxt
            for p in range(16):
                if p == 10 and i + 1 < len(chunks):
                    nxt = load_norm(i + 1)
                col = 128 * p
                pa, bpa = self.psA.next()
                pb, bpb = self.psA.next()
                for (pp, bpp, cc) in ((pa, bpa, col), (pb, bpb, 3072 + col)):
                    for k in range(8):
                        kb.op("pe", lambda e: e.matmul(pp[:], wt[:, k, cc:cc + 128], hT[:, k, :], start=(k == 0), stop=(k == 7)),
                              reads=[bw, bh], writes=[bpp], inc=(k == 7))
                t1, bt1 = t1_r.next()
                t2, bt2 = t2_r.next()
                kb.op("dve", lambda e: e.tensor_tensor(t1[:], pa[:], cs[:, 0, :], ALU.mult), reads=[bpa, bcs], writes=[bt1])
                kb.op("dve", lambda e: e.tensor_tensor(t2[:], pb[:], cs[:, 1, :], ALU.mult), reads=[bpb, bcs], writes=[bt2])
                stg, bst = st_r.next()
                kb.op("pool", lambda e: e.tensor_tensor(stg[:], t1[:], t2[:], ALU.add), reads=[bt1, bt2], writes=[bst])
                kb.store(self.ZT1[s, col:col + 128, t0:t0 + TC], [kb.db("ZT1", s, p, c)], stg[:], bst)
            for j in range(4):
                sv, bsv = sv_r.next()
                for half in range(2):
                    pa, bpa = self.psA.next()
                    for k in range(8):
                        kb.op("pe", lambda e: e.matmul(pa[:], hT[:, k, j * 128:(j + 1) * 128], wt[:, k, 2048 + 512 * half:2048 + 512 * (half + 1)],
                                                       start=(k == 0), stop=(k == 7)),
                              reads=[bw, bh], writes=[bpa], inc=(k == 7))
                    kb.op("act", lambda e: e.copy(sv[:, 512 * half:512 * (half + 1)], pa[:]), reads=[bpa], writes=[bsv])
                kb.store(self.VT1[s, t0 + j * 128:t0 + (j + 1) * 128, :], [kb.db("VT1", s, c)], sv[:], bsv)

    def ph_attn1(self, kb):
        nc = self.nc
        self.attn_setup(kb)
        q_r = Ring(kb, "q1", [128, T], BF16, 2)
        k_r = Ring(kb, "k1", [128, T], BF16, 2)
        for rr in (q_r, k_r):
            for (t_, b_) in rr.items:
                kb.op("pool", lambda e: e.memset(t_[64:128, :], 0.0), writes=[b_])
        va = [Ring(kb, "va%d" % p, [128, 32, 128], BF16, 2) for p in range(3)]
        for p in range(3):
            for (t, b) in va[p].items:
                kb.op("pool", lambda e, t=t: e.memset(t[:, :, 64:128], 1.0), writes=[b])
        acc_r = Ring(kb, "acc1", [128, T], F32, 2)
        mx_r = Ring(kb, "mix1", [64, T], BF16, 2)
        rec_r = Ring(kb, "rec1", [64, 512], F32, 3)
        pending_final = []
        dils = (1, 4, 16)
        for s in range(NSEQ):
            zdeps = [kb.db("ZT1", s, p, c) for p in range(16) for c in range(NCH)]
            vdeps = [kb.db("VT1", s, c) for c in range(NCH)]
            for hd in range(16):
                if self.alim and (s >= self.alim["nseq"] or hd not in self.alim.get("ch", [0])):
                    continue
                qt, bq = q_r.next()
                kt_, bk = k_r.next()
                kb.load(qt[0:64, :], bq, self.ZT1[s, 64 * hd:64 * hd + 64, :], sbufs=zdeps)
                kb.load(kt_[0:64, :], bk, self.ZT1[s, 1024 + 64 * hd:1024 + 64 * hd + 64, :], sbufs=zdeps)
                vts = []
                for p, d in enumerate(dils):
                    vt, bv = va[p].next()
                    src = self.VT1[s][:, 64 * hd:64 * hd + 64]
                    if d == 1:
                        kb.load(vt[:, :, 0:64], bv, src.rearrange("(b i) f -> i b f", i=128), sbufs=vdeps, nonc=True)
                    else:
                        nbt = 32 // d
                        for r in range(d):
                            kb.load(vt[:, r * nbt:(r + 1) * nbt, 0:64], bv, src[r::d, :].rearrange("(b i) f -> i b f", i=128), sbufs=vdeps, nonc=True)
                    vts.append((vt, bv))
                acc, bacc = acc_r.next()
                for p, d in enumerate(dils):
                    vt, bv = vts[p]
                    nbt = 32 // d
                    qtl = []
                    if d == 1:
                        order = [(0, b) for b in range(32)]
                    elif d == 4:
                        order = [(r, b) for b in range(8) for r in range(4)]
                    else:
                        order = [(r, b) for b in range(2) for r in range(16)]
                    units = []
                    for u in range(8):
                        unit = []
                        for (r, b) in order[4 * u:4 * u + 4]:
                            st = d * 128 * b + r
                            qap = qt[:, st:st + d * 127 + 1:d]
                            blocks = []
                            if b > 0:
                                sp_ = d * 128 * (b - 1) + r
                                blocks.append((kt_[:, sp_:sp_ + d * 127 + 1:d], 2, vt[:, r * nbt + b - 1, :]))
                            else:
                                blocks.append((kt_[:, st:st + d * 127 + 1:d], 3, vt[:, r * nbt + b, :]))
                            blocks.append((kt_[:, st:st + d * 127 + 1:d], 0, vt[:, r * nbt + b, :]))
                            unit.append((qap, blocks))
                        units.append(unit)

                    def post(u, ob, bob, p=p, d=d, order=order, acc=acc, bacc=bacc):
                        r0, b0 = order[4 * u]
                        if d == 1:
                            dst = acc[:, 512 * u:512 * (u + 1)]
                            src = ob[:]
                        elif d == 4:
                            dst = acc[:, 512 * b0:512 * (b0 + 1)].rearrange("p (i r) -> p r i", r=4)
                            src = ob[:].rearrange("p (r i) -> p r i", r=4)
                        else:
                            dst = acc[:, 2048 * b0:2048 * (b0 + 1)].rearrange("p (i r) -> p r i", r=16)[:, r0:r0 + 4, :]
                            src = ob[:].rearrange("p (r i) -> p r i", r=4)
                        if p == 0:
                            kb.op("act", lambda e: e.copy(dst, src), reads=[bob], writes=[bacc])
                        else:
                            kb.op("dve", lambda e: e.tensor_tensor(dst, src, dst, ALU.add), reads=[bob, bacc], writes=[bacc])
                        if pending_final:
                            pending_final.pop(0)()

                    self.attend(kb, units, 64, [bq, bk, bv], post, pair_kind=0)
                while pending_final:
                    pending_final.pop(0)()

                def mk_final(s=s, hd=hd, acc=acc, bacc=bacc):
                    mx, bmx = mx_r.next()
                    fl = []
                    for c in range(8):
                        def chunk(c=c):
                            cs_ = slice(512 * c, 512 * (c + 1))
                            rc, brc = rec_r.next()
                            kb.op("dve", lambda e: e.tensor_copy(rc[:], acc[64:128, cs_]), reads=[bacc], writes=[brc])
                            kb.op("act", lambda e: e.activation(out=rc[:], in_=rc[:], func=AF.Ln), reads=[brc], writes=[brc])
                            kb.op("act", lambda e: e.activation(out=rc[:], in_=rc[:], func=AF.Exp, scale=-1.0), reads=[brc], writes=[brc])
                            kb.op("pool", lambda e: e.tensor_tensor(mx[:, cs_], acc[0:64, cs_], rc[:], ALU.mult), reads=[bacc, brc], writes=[bmx])
                        fl.append(chunk)
                    fl.append(lambda: kb.store(self.MIXT[s, 64 * hd:64 * hd + 64, :], [kb.db("MIXT", s, hd)], mx[:], bmx))
                    return fl
                pending_final.extend(mk_final())

        while pending_final:
            pending_final.pop(0)()

    def ph_final(self, kb):
        I = self.I
        nc = self.nc
        gt, gb = kb.sb("gfin", [128, 8], F32)
        kb.load(gt[:], gb, I["final_norm"].rearrange("(k p) -> p k", p=128), nonc=True)
        xT_r = Ring(kb, "xTz", [128, 8, TC], F32, 2)
        sq_r = Ring(kb, "sqz", [128, 8, TC], BF16, 1)
        rs_r = Ring(kb, "rsz", [128, TC], F32, 2)
        yo_r = Ring(kb, "yo", [128, 4, D], F32, 2)
        for s in range(NSEQ):
            for c in range(NCH):
                if self.lim is not None and s * NCH + c >= self.lim:
                    continue
                t0 = c * TC
                xT, bxT = xT_r.next()
                kb.load(xT[:], bxT, self.XT[s, :, t0:t0 + TC].rearrange("(k p) t -> p k t", p=128), sbufs=[kb.db("XT", s, c)])
                sq, bsq = sq_r.next()
                rs, brs = rs_r.next()
                self.norm_chunk(kb, xT, bxT, None, None, sq, bsq, rs, brs, TC)
                for k in range(8):
                    kb.op("dve", lambda e, k=k, xT=xT, rs=rs: e.scalar_tensor_tensor(xT[:, k, :], xT[:, k, :], gt[:, k:k + 1], rs[:], ALU.mult, ALU.mult),
                          reads=[bxT, gb, brs], writes=[bxT])
                yo, byo = yo_r.next()
                for j in range(4):
                    for kk in range(2):
                        pm, bpm = self.psM.next()
                        for k4 in range(4):
                            k = kk * 4 + k4
                            kb.op("pe", lambda e, k=k, k4=k4, j=j, pm=pm, xT=xT: e.transpose(pm[:, k4 * 128:(k4 + 1) * 128], xT[:, k, j * 128:(j + 1) * 128], self.identf[:]),
                                  reads=[bxT, self.b_identf], writes=[bpm], inc=(k4 == 3))
                        kb.op("act", lambda e, pm=pm, yo=yo, j=j, kk=kk: e.copy(yo[:, j, kk * 512:(kk + 1) * 512], pm[:]), reads=[bpm], writes=[byo])
                kb.store(self.out[s, t0:t0 + TC, :].rearrange("(j p) f -> p j f", p=128), [kb.db("out", s, c)], yo[:], byo, eng="sp")


_CONSTS = None


def kernel(**inputs):
    global _CONSTS
    if _CONSTS is None:
        _CONSTS = host_consts()
    prog = Prog()
    nc = prog.build()
    x = np.ascontiguousarray(inputs["x"], dtype=np.float32)
    in_maps = []
    for cid in range(8):
        m = {"x": x[cid * NSEQ:(cid + 1) * NSEQ]}
        for name, shape in IN_SPECS[1:]:
            m[name] = np.ascontiguousarray(inputs[name], dtype=np.float32)
        m.update(_CONSTS)
        in_maps.append(m)
    res = run_bass_kernel_spmd(nc, in_maps, core_ids=list(range(8)))
    return np.concatenate([r["out"] for r in res.results], axis=0)
```
